# Optimizing a Trainium2 kernel written in Bass

```python
import math
import jax, jax.numpy as jnp
from jax import lax
import numpy as np

D_MODEL = 1024
BATCH = 2
SEQ = 8192
DEPTH = 1
DEC_BATCH = 128
DEC_SEQ = 1
PAST_LEN = 16384
PAGE_SIZE = 128

ATTN_HEADS = 8
KV_HEADS = 2
HEAD_DIM = 64
GQA_GROUP = ATTN_HEADS // KV_HEADS
ATTN_WIDTH = ATTN_HEADS * HEAD_DIM
KV_WIDTH = KV_HEADS * HEAD_DIM
WINDOW = 128
BLOCK = WINDOW
GDN_HEADS = 4
GDN_DK = 128
GDN_DV = 128
GDN_KEY_WIDTH = GDN_HEADS * GDN_DK
GDN_VAL_WIDTH = GDN_HEADS * GDN_DV
GDN_CONV_DIM = 2 * GDN_KEY_WIDTH + GDN_VAL_WIDTH
CONV_W = 4
CHUNK = 64
MIX_WIDTH = ATTN_WIDTH + GDN_VAL_WIDTH
IN_WIDTH = ATTN_WIDTH + 2 * KV_WIDTH + GDN_CONV_DIM + GDN_VAL_WIDTH + 2 * GDN_HEADS
D_FF = 4 * D_MODEL
EPS = 1e-6

kernel_name = 'hymba_swa_sink_alibi_gated_deltanet_step'


def _split_points():
    sizes = (ATTN_WIDTH, KV_WIDTH, KV_WIDTH, GDN_CONV_DIM, GDN_VAL_WIDTH, GDN_HEADS, GDN_HEADS)
    pts, acc = [], 0
    for s in sizes[:-1]:
        acc += s
        pts.append(acc)
    return pts


def rms_norm(x, w):
    xf = x.astype(jnp.float32)
    y = xf * lax.rsqrt(jnp.mean(xf * xf, axis=-1, keepdims=True) + EPS)
    return (y * w.astype(jnp.float32)).astype(x.dtype)


def l2_norm(x):
    return x * lax.rsqrt(jnp.sum(x * x, axis=-1, keepdims=True) + EPS)


def alibi_slopes():
    return jnp.exp2(-8.0 * jnp.arange(1, ATTN_HEADS + 1, dtype=jnp.float32) / ATTN_HEADS)


def window_attend(q, k, v, dist, valid, sinks):
    scores = jnp.einsum('...qkgd,...skd->...kgqs', q, k).astype(jnp.float32) * (HEAD_DIM ** -0.5)
    slopes = alibi_slopes().reshape(KV_HEADS, GQA_GROUP, 1, 1)
    scores = scores - slopes * dist.astype(jnp.float32)
    scores = jnp.where(valid, scores, -jnp.inf)
    sink = jnp.broadcast_to(sinks.astype(jnp.float32).reshape(KV_HEADS, GQA_GROUP, 1, 1),
                            scores.shape[:-1] + (1,))
    probs = jax.nn.softmax(jnp.concatenate([scores, sink], axis=-1), axis=-1)[..., :-1]
    return jnp.einsum('...kgqs,...skd->...qkgd', probs.astype(v.dtype), v)


def attn_prompt(q, k, v, sinks):
    Bn, L = q.shape[:2]
    nb = L // BLOCK
    qb = q.reshape(Bn, nb, BLOCK, KV_HEADS, GQA_GROUP, HEAD_DIM)

    def with_prev(x):
        xb = x.reshape(Bn, nb, BLOCK, KV_HEADS, HEAD_DIM)
        prev = jnp.concatenate([jnp.zeros_like(xb[:, :1]), xb[:, :-1]], axis=1)
        return jnp.concatenate([prev, xb], axis=2)

    qi = jnp.arange(BLOCK)[:, None]
    si = jnp.arange(2 * BLOCK)[None, :]
    dist = BLOCK + qi - si
    band = (dist >= 0) & (dist <= WINDOW)
    in_seq = (jnp.arange(nb)[:, None, None] > 0) | (si >= BLOCK)[None]
    valid = (band[None] & in_seq).reshape(nb, 1, 1, BLOCK, 2 * BLOCK)
    out = window_attend(qb, with_prev(k), with_prev(v), dist, valid, sinks)
    return out.reshape(Bn, L, ATTN_WIDTH)


def attn_sample(q, k_new, v_new, k_buf, v_buf, sinks):
    DB, T = q.shape[:2]
    keys = jnp.concatenate([k_buf.astype(k_new.dtype), k_new], axis=1)
    vals = jnp.concatenate([v_buf.astype(v_new.dtype), v_new], axis=1)
    qi = jnp.arange(T)[:, None]
    sj = jnp.arange(WINDOW + T)[None, :]
    dist = WINDOW + qi - sj
    valid = (dist >= 0) & (dist <= WINDOW)
    out = window_attend(q.reshape(DB, T, KV_HEADS, GQA_GROUP, HEAD_DIM), keys, vals, dist, valid, sinks)
    return out.reshape(DB, T, ATTN_WIDTH), keys[:, -WINDOW:], vals[:, -WINDOW:]


def causal_conv(ext, w):
    L = ext.shape[1] - (CONV_W - 1)
    w = w.astype(ext.dtype)
    y = ext[:, 0:L] * w[0]
    for i in range(1, CONV_W):
        y = y + ext[:, i:i + L] * w[i]
    return jax.nn.silu(y)


def gdn_prep(qkv, a, b, a_log, dt_bias):
    Bn, L = qkv.shape[:2]
    qkv = qkv.astype(jnp.float32)
    q, k, v = jnp.split(qkv, [GDN_KEY_WIDTH, 2 * GDN_KEY_WIDTH], axis=-1)
    q = l2_norm(q.reshape(Bn, L, GDN_HEADS, GDN_DK)) * (GDN_DK ** -0.5)
    k = l2_norm(k.reshape(Bn, L, GDN_HEADS, GDN_DK))
    v = v.reshape(Bn, L, GDN_HEADS, GDN_DV)
    g = -jnp.exp(a_log.astype(jnp.float32)) * jax.nn.softplus(a.astype(jnp.float32) + dt_bias.astype(jnp.float32))
    beta = jax.nn.sigmoid(b.astype(jnp.float32))
    return q, k, v, g, beta


def gdn_chunked(q, k, v, g, beta):
    Bn, L, H, dk = q.shape
    dv = v.shape[-1]
    N = L // CHUNK

    def to_chunks(x):
        x = x.reshape((Bn, N, CHUNK, H) + x.shape[3:])
        return jnp.moveaxis(x, (1, 3), (0, 2))

    qc, kc, vc, gc, bc = (to_chunks(t) for t in (q, k, v, g, beta))
    G = jnp.cumsum(gc, axis=-1)
    idx = jnp.arange(CHUNK)
    lower_incl = idx[:, None] >= idx[None, :]
    lower_strict = idx[:, None] > idx[None, :]
    diff = G[..., :, None] - G[..., None, :]
    decay = jnp.where(lower_incl, jnp.exp(jnp.where(lower_incl, diff, 0.0)), 0.0)
    kk = jnp.einsum('nbhid,nbhjd->nbhij', kc, kc)
    A = jnp.where(lower_strict, bc[..., :, None] * kk * decay, 0.0)
    eye = jnp.eye(CHUNK, dtype=jnp.float32)
    Tinv = lax.linalg.triangular_solve(eye + A, jnp.broadcast_to(eye, A.shape), left_side=True, lower=True)
    u_base = jnp.einsum('nbhij,nbhjd->nbhid', Tinv, vc * bc[..., None])
    w = jnp.einsum('nbhij,nbhjd->nbhid', Tinv, kc * (bc * jnp.exp(G))[..., None])
    qk = jnp.where(lower_incl, jnp.einsum('nbhid,nbhjd->nbhij', qc, kc) * decay, 0.0)
    q_dec = qc * jnp.exp(G)[..., None]
    k_dec = kc * jnp.exp(G[..., -1:] - G)[..., None]
    last_decay = jnp.exp(G[..., -1])

    def step(S, xs):
        u_b, w_c, qk_c, q_d, k_d, ld = xs
        u = u_b - jnp.einsum('bhcd,bhde->bhce', w_c, S)
        o = jnp.einsum('bhcd,bhde->bhce', q_d, S) + jnp.einsum('bhij,bhje->bhie', qk_c, u)
        S = S * ld[..., None, None] + jnp.einsum('bhcd,bhce->bhde', k_d, u)
        return S, o

    S0 = jnp.zeros((Bn, H, dk, dv), jnp.float32)
    S_fin, o = lax.scan(step, S0, (u_base, w, qk, q_dec, k_dec, last_decay))
    o = jnp.moveaxis(o, (0, 2), (1, 3)).reshape(Bn, L, H, dv)
    return o, S_fin


def gdn_recurrent(q, k, v, g, beta, S0):
    def step(S, xs):
        q_t, k_t, v_t, g_t, b_t = xs
        S = S * jnp.exp(g_t)[..., None, None]
        kv = jnp.einsum('bhd,bhde->bhe', k_t, S)
        u = (v_t - kv) * b_t[..., None]
        S = S + jnp.einsum('bhd,bhe->bhde', k_t, u)
        return S, jnp.einsum('bhd,bhde->bhe', q_t, S)

    xs = tuple(jnp.moveaxis(t, 1, 0) for t in (q, k, v, g, beta))
    S, o = lax.scan(step, S0.astype(jnp.float32), xs)
    return jnp.moveaxis(o, 0, 1), S


def mixer_in(x, lp):
    Bn, L = x.shape[:2]
    h = rms_norm(x, lp['norm_mix_pre'])
    proj = jnp.einsum('bld,de->ble', h, lp['w_in'].astype(h.dtype))
    q_a, k_a, v_a, qkv_g, z_g, a_g, b_g = jnp.split(proj, _split_points(), axis=-1)
    q_a = q_a.reshape(Bn, L, ATTN_HEADS, HEAD_DIM)
    k_a = k_a.reshape(Bn, L, KV_HEADS, HEAD_DIM)
    v_a = v_a.reshape(Bn, L, KV_HEADS, HEAD_DIM)
    return q_a, k_a, v_a, qkv_g, z_g, a_g, b_g


def mixer_out(x, attn_o, gdn_o, z, lp):
    Bn, L = x.shape[:2]
    gate = jax.nn.silu(z.astype(jnp.float32).reshape(Bn, L, GDN_HEADS, GDN_DV))
    gdn_y = rms_norm(gdn_o, lp['gdn_norm']) * gate
    mix = jnp.concatenate([attn_o.astype(x.dtype), gdn_y.reshape(Bn, L, GDN_VAL_WIDTH).astype(x.dtype)], axis=-1)
    x = x + rms_norm(jnp.einsum('blm,md->bld', mix, lp['w_out'].astype(x.dtype)), lp['norm_mix_post'])
    h = rms_norm(x, lp['norm_ffn_pre'])
    u = jax.nn.relu(jnp.einsum('bld,df->blf', h, lp['w_up'].astype(h.dtype)))
    f = jnp.einsum('blf,fd->bld', u * u, lp['w_down'].astype(h.dtype))
    return x + rms_norm(f, lp['norm_ffn_post'])


def prompt_layer(x, lp):
    q_a, k_a, v_a, qkv, z, a, b = mixer_in(x, lp)
    attn_o = attn_prompt(q_a, k_a, v_a, lp['attn_sinks'])
    ext = jnp.concatenate([jnp.zeros_like(qkv[:, :CONV_W - 1]), qkv], axis=1)
    q, k, v, g, beta = gdn_prep(causal_conv(ext, lp['conv_w']), a, b, lp['gdn_a_log'], lp['gdn_dt_bias'])
    gdn_o, S = gdn_chunked(q, k, v, g, beta)
    y = mixer_out(x, attn_o, gdn_o, z, lp)
    return y, ext[:, -(CONV_W - 1):], k_a[:, -WINDOW:], v_a[:, -WINDOW:], S


def sample_layer(x, conv_state, k_buf, v_buf, S0, lp):
    q_a, k_a, v_a, qkv, z, a, b = mixer_in(x, lp)
    attn_o, new_k, new_v = attn_sample(q_a, k_a, v_a, k_buf, v_buf, lp['attn_sinks'])
    ext = jnp.concatenate([conv_state.astype(qkv.dtype), qkv], axis=1)
    q, k, v, g, beta = gdn_prep(causal_conv(ext, lp['conv_w']), a, b, lp['gdn_a_log'], lp['gdn_dt_bias'])
    gdn_o, S = gdn_recurrent(q, k, v, g, beta, S0)
    y = mixer_out(x, attn_o, gdn_o, z, lp)
    return y, ext[:, -(CONV_W - 1):], new_k, new_v, S


def setup_inputs(seed: int = 0) -> dict:
    key = jax.random.key(seed)
    ks = jax.random.split(key, 24)
    f32 = jnp.float32

    def nrm(k, shape, scale):
        return jax.random.normal(k, shape, f32) * scale

    def gain(k, n):
        return 1.0 + 0.01 * jax.random.normal(k, (DEPTH, n), f32)

    dt = jnp.exp(jax.random.uniform(ks[10], (DEPTH, GDN_HEADS), f32, math.log(1e-3), math.log(0.1)))
    return {
        'x_prompt': nrm(ks[0], (BATCH, SEQ, D_MODEL), 1.0),
        'x_sample': nrm(ks[1], (DEC_BATCH, DEC_SEQ, D_MODEL), 1.0),
        'state_conv': nrm(ks[2], (DEPTH, DEC_BATCH, CONV_W - 1, GDN_CONV_DIM), 1.0),
        'cache_win_k': nrm(ks[3], (DEPTH, DEC_BATCH, WINDOW, KV_HEADS, HEAD_DIM), 1.0),
        'cache_win_v': nrm(ks[4], (DEPTH, DEC_BATCH, WINDOW, KV_HEADS, HEAD_DIM), 1.0),
        'state_gdn': nrm(ks[5], (DEPTH, DEC_BATCH, GDN_HEADS, GDN_DK, GDN_DV), GDN_DK ** -0.5),
        'norm_mix_pre': gain(ks[6], D_MODEL),
        'w_in': nrm(ks[7], (DEPTH, D_MODEL, IN_WIDTH), D_MODEL ** -0.5),
        'attn_sinks': nrm(ks[8], (DEPTH, ATTN_HEADS), 1.0),
        'conv_w': nrm(ks[9], (DEPTH, CONV_W, GDN_CONV_DIM), CONV_W ** -0.5),
        'gdn_a_log': jnp.log(jax.random.uniform(ks[11], (DEPTH, GDN_HEADS), f32, 1.0, 16.0)),
        'gdn_dt_bias': dt + jnp.log(-jnp.expm1(-dt)),
        'gdn_norm': gain(ks[12], GDN_DV),
        'w_out': nrm(ks[13], (DEPTH, MIX_WIDTH, D_MODEL), MIX_WIDTH ** -0.5),
        'norm_mix_post': gain(ks[14], D_MODEL),
        'norm_ffn_pre': gain(ks[15], D_MODEL),
        'w_up': nrm(ks[16], (DEPTH, D_MODEL, D_FF), D_MODEL ** -0.5),
        'w_down': nrm(ks[17], (DEPTH, D_FF, D_MODEL), D_FF ** -0.5),
        'norm_ffn_post': gain(ks[18], D_MODEL),
    }


def reference(x_prompt, x_sample, state_conv, cache_win_k, cache_win_v, state_gdn,
              norm_mix_pre, w_in, attn_sinks, conv_w, gdn_a_log, gdn_dt_bias, gdn_norm,
              w_out, norm_mix_post, norm_ffn_pre, w_up, w_down, norm_ffn_post):
    xp, xs = x_prompt, x_sample
    pc, pk, pv, ps = [], [], [], []
    sc, sk, sv, ss = [], [], [], []
    for l in range(DEPTH):
        lp = dict(norm_mix_pre=norm_mix_pre[l], w_in=w_in[l], attn_sinks=attn_sinks[l],
                  conv_w=conv_w[l], gdn_a_log=gdn_a_log[l], gdn_dt_bias=gdn_dt_bias[l],
                  gdn_norm=gdn_norm[l], w_out=w_out[l], norm_mix_post=norm_mix_post[l],
                  norm_ffn_pre=norm_ffn_pre[l], w_up=w_up[l], w_down=w_down[l],
                  norm_ffn_post=norm_ffn_post[l])
        xp, c1, k1, v1, s1 = prompt_layer(xp, lp)
        xs, c2, k2, v2, s2 = sample_layer(xs, state_conv[l], cache_win_k[l], cache_win_v[l], state_gdn[l], lp)
        pc.append(c1.astype(state_conv.dtype)); pk.append(k1.astype(cache_win_k.dtype))
        pv.append(v1.astype(cache_win_v.dtype)); ps.append(s1.astype(state_gdn.dtype))
        sc.append(c2.astype(state_conv.dtype)); sk.append(k2.astype(cache_win_k.dtype))
        sv.append(v2.astype(cache_win_v.dtype)); ss.append(s2.astype(state_gdn.dtype))
    p_state_conv = jnp.stack(pc)
    p_cache_win_k = jnp.stack(pk)
    p_cache_win_v = jnp.stack(pv)
    p_state_gdn = jnp.stack(ps)
    s_state_conv = jnp.stack(sc)
    s_cache_win_k = jnp.stack(sk)
    s_cache_win_v = jnp.stack(sv)
    s_state_gdn = jnp.stack(ss)
    return (xp, xs, p_state_conv, p_cache_win_k, p_cache_win_v, p_state_gdn,
            s_state_conv, s_cache_win_k, s_cache_win_v, s_state_gdn)
```

```python
import contextlib
import numpy as np
import ml_dtypes
import concourse.bass as bass
import concourse.mybir as mybir
from concourse.bass_utils import run_bass_kernel_spmd

F32 = mybir.dt.float32
BF16 = mybir.dt.bfloat16
AF = mybir.ActivationFunctionType
ALU = mybir.AluOpType
AX = mybir.AxisListType

D = 1024
KC = 8
NT = 16
SEG = NT * 128
INW = 2824
EPS = 1e-6
NEG = -30000.0
P1W = (2, 3)
XCHG = True
NPRE = 1


class Res:
    __slots__ = ("name", "w", "r", "dsem", "dcnt")

    def __init__(self, name):
        self.name = name
        self.w = []
        self.r = {}
        self.dsem = None
        self.dcnt = 0


class Eng:
    def __init__(self, name, h, sem):
        self.name = name
        self.h = h
        self.sem = sem
        self.cnt = 0
        self.waited = {}


class Sched:
    def __init__(self, nc):
        self.nc = nc
        self.stack = contextlib.ExitStack()
        self.eng = {}
        for name, h in (("pe", nc.tensor), ("dve", nc.vector), ("act", nc.scalar),
                        ("pool", nc.gpsimd), ("sp", nc.sync)):
            sem = self.stack.enter_context(nc.semaphore("s_" + name))
            self.eng[name] = Eng(name, h, sem)
        self.finals = {}
        self.dtoks = {}
        self.nops = {k: 0 for k in self.eng}
        self.nwaits = 0
        self.ndma = 0

    def sbuf(self, name, shape, dtype, stack=None):
        return (stack or self.stack).enter_context(self.nc.sbuf_tensor("sb_" + name, list(shape), dtype))

    def psum(self, name, shape, dtype):
        return self.stack.enter_context(self.nc.psum_tensor("ps_" + name, list(shape), dtype))

    def dram(self, name, shape, dtype, kind, **kw):
        return self.nc.dram_tensor(name, list(shape), dtype, kind=kind, **kw).ap()

    def _wait(self, E, tok):
        sem, val, src = tok
        k = id(sem)
        if E.waited.get(k, 0) >= val:
            return
        E.waited[k] = val
        E.h.wait_ge(sem, val)
        self.nwaits += 1

    def _deps(self, E, reads, writes, acc):
        eng = E.name
        for r in reads:
            for tok in r.w:
                if tok[2] == eng and eng == "pe":
                    continue
                self._wait(E, tok)
        for w in writes:
            for tok in list(w.r.values()):
                if tok[2] == eng and eng == "pe":
                    continue
                self._wait(E, tok)
            if not acc:
                for tok in w.w:
                    if tok[2] == eng and eng == "pe":
                        continue
                    self._wait(E, tok)

    def _commit(self, tok, reads, writes, acc):
        k = id(tok[0])
        for r in reads:
            old = r.r.get(k)
            if old is None or old[1] < tok[1]:
                r.r[k] = tok
        for w in writes:
            if acc:
                w.w = [t for t in w.w if id(t[0]) != k] + [tok]
            else:
                w.w = [tok]
                w.r = {}

    def op(self, eng, fn, reads=(), writes=(), acc=False):
        E = self.eng[eng]
        self._deps(E, reads, writes, acc)
        ins = fn(E.h)
        E.cnt += 1
        ins.then_inc(E.sem, 1)
        tok = (E.sem, E.cnt, eng)
        self._commit(tok, reads, writes, acc)
        self.nops[eng] += 1
        return ins

    def dma(self, eng, out, in_, dres, reads=(), writes=(), acc=False, final=False, **kw):
        E = self.eng[eng]
        self._deps(E, reads, writes, acc)
        if dres.dsem is None:
            dres.dsem = self.stack.enter_context(self.nc.semaphore("d_" + dres.name))
        ins = E.h.dma_start(out=out, in_=in_, **kw)
        dres.dcnt += 16
        ins.then_inc(dres.dsem, 16)
        tok = (dres.dsem, dres.dcnt, "dma")
        self._commit(tok, reads, writes, acc)
        self.dtoks[id(dres.dsem)] = tok
        if final:
            self.finals[id(dres.dsem)] = tok
        self.ndma += 1
        return ins

    def cc(self, kind, in_ap, out_ap, dres, groups, reads=(), writes=()):
        E = self.eng["pool"]
        self._deps(E, reads, writes, False)
        if dres.dsem is None:
            dres.dsem = self.stack.enter_context(self.nc.semaphore("d_" + dres.name))
        ins = E.h.collective_compute(kind, ALU.bypass, replica_groups=groups, ins=[in_ap], outs=[out_ap])
        dres.dcnt += 1
        ins.then_inc(dres.dsem, 1)
        tok = (dres.dsem, dres.dcnt, "dma")
        self._commit(tok, reads, writes, False)
        self.dtoks[id(dres.dsem)] = tok
        return ins

    def barrier(self):
        toks = [(e.sem, e.cnt, e.name) for e in self.eng.values() if e.cnt > 0]
        toks += list(self.dtoks.values())
        for E in self.eng.values():
            for tok in toks:
                if tok[2] == E.name:
                    continue
                self._wait(E, tok)

    def finish(self):
        E = self.eng["sp"]
        for tok in self.dtoks.values():
            self._wait(E, tok)
        for e in self.eng.values():
            if e.cnt > 0 and e.name != "sp":
                self._wait(E, (e.sem, e.cnt, e.name))

    def close(self):
        self.stack.close()


class Ring:
    def __init__(self, S, name, n, shape, dtype, stack=None):
        self.t = [S.sbuf("%s%d" % (name, i), shape, dtype, stack) for i in range(n)]
        self.r = [Res("%s%d" % (name, i)) for i in range(n)]
        self.i = 0
        self.n = n

    def next(self):
        i = self.i
        self.i = (i + 1) % self.n
        return self.t[i], self.r[i]


def _consts():
    c = {}
    c["idf"] = np.eye(128, dtype=np.float32)
    c["idb"] = np.eye(128).astype(ml_dtypes.bfloat16)
    c["ones"] = np.ones((128, 128), np.float32)
    k = np.arange(64)[:, None]
    i = np.arange(64)[None, :]
    c["m_le"] = (k <= i).astype(np.float32)
    c["m_gt"] = (k > i).astype(np.float32)
    c["m_lt"] = (k < i).astype(np.float32)
    q = np.arange(128)[:, None]
    s = np.arange(256)[None, :]
    dist = 128 + q - s
    valid = (dist >= 0) & (dist <= 128)
    slopes = np.exp2(-8.0 * np.arange(1, 9, dtype=np.float32) / 8).astype(np.float32)
    b = np.where(valid[:, None, :], -slopes[None, :, None] * dist[:, None, :].astype(np.float32), NEG)
    c["biasN"] = b.astype(np.float32)
    b0 = b.copy()
    b0[:, :, :128] = NEG
    c["bias0"] = b0.astype(np.float32)
    c["i16b"] = np.eye(16, dtype=np.float32).reshape(1, 256)
    pos = np.arange(129)[None, :]
    hs = (np.arange(128) % 8)
    bs = -slopes[hs][:, None] * (128 - pos).astype(np.float32)
    c["biass"] = bs.astype(np.float32)
    selm = np.zeros((32, 128), np.float32)
    for sidx in range(16):
        for h in range(8):
            selm[(h // 4) * 16 + sidx, sidx * 8 + h] = 1.0
    c["selm"] = selm
    return c


SAMPLE_CONSTS = ("i16b", "biass")
CONST_SHAPES = {"selm": ([32, 128], F32), "idf": ([128, 128], F32), "idb": ([128, 128], BF16), "ones": ([128, 128], F32),
                "m_le": ([64, 64], F32), "m_gt": ([64, 64], F32), "m_lt": ([64, 64], F32),
                "biasN": ([128, 8, 256], F32), "bias0": ([128, 8, 256], F32)}


def build_program(dbg=(), nt=NT, do_ffn=True, ffn_group=2, npre=NPRE, do_sample=True, sstage=9, xchg=XCHG):
    nc = bass.Bass("TRN2", target_bir_lowering=False)
    S = Sched(nc)
    dbg_outs = {}

    xin = S.dram("xin", [(nt + npre) * 128, D], F32, "ExternalInput")
    w_in_d = S.dram("w_in", [128, KC, INW], F32, "ExternalInput")
    w_out_d = S.dram("w_out", [128, KC, D], F32, "ExternalInput")
    w_up_d = S.dram("w_up", [128, KC, 4096], F32, "ExternalInput")
    w_dn_d = S.dram("w_down", [128, 32, D], F32, "ExternalInput")
    convw_d = S.dram("convw", [128, 12, 4], F32, "ExternalInput")
    g_pre_d = S.dram("g_pre", [1, D], F32, "ExternalInput")
    g_post_d = S.dram("g_post", [1, D], F32, "ExternalInput")
    g_fpre_d = S.dram("g_fpre", [1, D], F32, "ExternalInput")
    g_fpost_d = S.dram("g_fpost", [1, D], F32, "ExternalInput")
    g_gdn_d = S.dram("g_gdn", [1, 128], F32, "ExternalInput")
    sinks_d = S.dram("sinks", [1, 8], F32, "ExternalInput")
    alog_d = S.dram("alog", [1, 4], F32, "ExternalInput")
    dtb_d = S.dram("dtb", [1, 4], F32, "ExternalInput")
    cd = {k: S.dram(k, shp, dt, "ExternalInput") for k, (shp, dt) in CONST_SHAPES.items()}

    NS = 16
    xs_d = S.dram("xs", [NS, D], F32, "ExternalInput")
    cst_d = S.dram("cst", [NS, 3, 1536], F32, "ExternalInput")
    ck_d = S.dram("ck", [NS, 128, 128], F32, "ExternalInput")
    cv_d = S.dram("cv", [NS, 128, 128], F32, "ExternalInput")
    sst_d = S.dram("sst", [NS, 4, 128, 128], F32, "ExternalInput")
    convwt_d = S.dram("convwt", [1, 4 * 1536], F32, "ExternalInput")
    i16b_d = S.dram("i16b", [1, 256], F32, "ExternalInput")
    biass_d = S.dram("biass", [128, 129], F32, "ExternalInput")
    sinksh_d = S.dram("sinksh", [128, 1], F32, "ExternalInput")
    ys_d = S.dram("ys", [NS, D], F32, "ExternalOutput")
    sconv_d = S.dram("s_conv", [NS, 3, 1536], F32, "ExternalOutput")
    sk_d = S.dram("s_k", [NS, 128, 128], F32, "ExternalOutput")
    sv_d = S.dram("s_v", [NS, 128, 128], F32, "ExternalOutput")
    sS_d = S.dram("s_S", [NS, 4, 128, 128], F32, "ExternalOutput")
    scr_q = S.dram("scr_q", [NS, 512], F32, "Internal")
    scr_kv = S.dram("scr_kv", [NS, 256], F32, "Internal")
    scr_ao = S.dram("scr_ao", [128, 64], F32, "Internal")
    sel_d = S.dram("sel", [1, 4], F32, "ExternalInput")
    sp_wT = S.dram("sp_wT", [nt, 128, 8, 64], BF16, "Internal")
    sp_qT = S.dram("sp_qT", [nt, 128, 4, 128], BF16, "Internal")
    sp_kd = S.dram("sp_kd", [nt, 64, 8, 128], BF16, "Internal")
    sp_ub = S.dram("sp_ub", [nt, 64, 8, 128], F32, "Internal")
    sp_qkT = S.dram("sp_qkT", [nt, 64, 8, 64], BF16, "Internal")
    sp_gs = S.dram("sp_gs", [nt, 64, 16, 8], F32, "Internal")
    sp_ld = S.dram("sp_ld", [nt, 128, 8], F32, "Internal")
    sp_zab = S.dram("sp_zab", [nt, 64, 2, 520], F32, "Internal")
    sp_mix = S.dram("sp_mix", [nt, 128, 4, 128], BF16, "Internal")
    xsrc_d = S.dram("xsrc", [128, 1024], F32, "Internal")
    xdst_d = S.dram("xdst", [4 * 128, 1024], F32, "Internal", addr_space="Local")
    y_d = S.dram("y", [nt * 128, D], F32, "ExternalOutput")
    pconv_d = S.dram("p_conv", [3, 1536], F32, "ExternalOutput")
    pk_d = S.dram("p_k", [128, 128], F32, "ExternalOutput")
    pv_d = S.dram("p_v", [128, 128], F32, "ExternalOutput")
    pS_d = S.dram("p_S", [4, 128, 128], F32, "ExternalOutput")

    def dbg_out(name, ap_sb, res, shape):
        if name not in dbg:
            return
        d = S.dram("dbg_" + name, list(shape), F32, "ExternalOutput")
        dbg_outs[name] = d
        S.dma("sp", d[:] if True else d, ap_sb, res, reads=[res], final=True)

    banks = [S.psum("bank%d" % i, [128, 512], F32) for i in range(8)]
    bres = [Res("bank%d" % i) for i in range(8)]
    bstate = {"i": 0}

    POOLS = {"all": list(range(8)), "F": [0, 1, 2], "P": [3, 4, 5], "C": [6, 7],
             "C1": [0, 1], "C2": [2, 3, 4, 5], "O": [6, 7]}
    bstate.update({"pool": "all", "cnt": {k: 0 for k in POOLS}})

    def nb(pool=None):
        p = pool or bstate["pool"]
        lst = POOLS[p]
        k = bstate["cnt"][p]
        bstate["cnt"][p] = k + 1
        i = lst[k % len(lst)]
        return banks[i], bres[i]

    def const_tile(name, shape, dt, src_ap, eng="sp", stack=None):
        t = S.sbuf(name, shape, dt, stack)
        r = Res(name)
        S.dma(eng, t[:], src_ap, r, writes=[r])
        return t, r

    idf, r_idf = const_tile("idf", [128, 128], F32, cd["idf"][:])
    idb, r_idb = const_tile("idb", [128, 128], BF16, cd["idb"][:])
    ones, r_ones = const_tile("ones", [128, 128], F32, cd["ones"][:])
    epsT = S.sbuf("epsT", [128, 1], F32)
    r_eps = Res("eps")
    S.op("dve", lambda e: e.memset(epsT[:], EPS), writes=[r_eps])

    stat = Ring(S, "stat", 12, [128, 16], F32)
    junk = Ring(S, "junk", 2, [128, 1024], BF16)

    def rstd_from_ss(ss_ap, r_ss, n, scale):
        st, rs = stat.next()
        S.op("act", lambda e: e.activation(out=st[0:n, 0:1], in_=ss_ap, func=AF.Ln, bias=epsT[0:n, :], scale=scale),
             reads=[r_ss, r_eps], writes=[rs])
        S.op("act", lambda e: e.activation(out=st[0:n, 1:2], in_=st[0:n, 0:1], func=AF.Exp, scale=-0.5), reads=[rs], writes=[rs])
        return st[0:n, 1:2], rs

    def norm_transpose(x_ap, r_x, gain, r_gain, hT_out_ap, r_hT, n=128):
        jk, rj = junk.next()
        st, rs = stat.next()
        S.op("act", lambda e: e.activation(out=jk[0:n, :], in_=x_ap, func=AF.Square, accum_out=st[0:n, 0:1]),
             reads=[r_x], writes=[rj, rs])
        rstd, rr = rstd_from_ss(st[0:n, 0:1], rs, n, 1.0 / D)
        hb, rh = junk.next()
        S.op("dve", lambda e: e.scalar_tensor_tensor(out=hb[0:n, :], in0=x_ap, scalar=rstd, in1=gain[0:n, :],
                                                     op0=ALU.mult, op1=ALU.mult),
             reads=[r_x, rr, r_gain], writes=[rh])
        bk, rb = nb()
        bkb = bk[:, :].bitcast(BF16)
        for kc in range(KC):
            S.op("pe", lambda e, kc=kc: e.transpose(bkb[:, kc * 128:kc * 128 + n], hb[0:n, kc * 128:(kc + 1) * 128],
                                                    idb[0:n, 0:n]),
                 reads=[rh, r_idb], writes=[rb], acc=(kc > 0))
        S.op("act", lambda e: e.copy(out=hT_out_ap, in_=bkb.rearrange("p (k t) -> p k t", k=KC)[:, :, 0:n]),
             reads=[rb], writes=[r_hT])

    def epilogue(bk0, rb0, bk1, rb1, gain, r_gain, resid_ap, r_resid, out_ap, r_out, n=128):
        jk, rj = junk.next()
        st, rs = stat.next()
        S.op("act", lambda e: e.activation(out=jk[0:n, 0:512], in_=bk0[0:n, :], func=AF.Square, accum_out=st[0:n, 0:1]),
             reads=[rb0], writes=[rj, rs])
        S.op("act", lambda e: e.activation(out=jk[0:n, 512:1024], in_=bk1[0:n, :], func=AF.Square, accum_out=st[0:n, 1:2]),
             reads=[rb1], writes=[rj, rs], acc=True)
        S.op("dve", lambda e: e.tensor_tensor(out=st[0:n, 2:3], in0=st[0:n, 0:1], in1=st[0:n, 1:2], op=ALU.add),
             reads=[rs], writes=[rs])
        rstd, rr = rstd_from_ss(st[0:n, 2:3], rs, n, 1.0 / D)
        S.op("dve", lambda e: e.tensor_tensor(out=out_ap[:, 0:512], in0=bk0[0:n, :], in1=gain[0:n, 0:512], op=ALU.mult),
             reads=[rb0, r_gain], writes=[r_out])
        S.op("dve", lambda e: e.tensor_tensor(out=out_ap[:, 512:1024], in0=bk1[0:n, :], in1=gain[0:n, 512:1024], op=ALU.mult),
             reads=[rb1, r_gain], writes=[r_out], acc=True)
        S.op("dve", lambda e: e.scalar_tensor_tensor(out=out_ap, in0=out_ap, scalar=rstd, in1=resid_ap,
                                                     op0=ALU.mult, op1=ALU.add),
             reads=[r_out, rr, r_resid], writes=[r_out])

    W = contextlib.ExitStack()
    AC = contextlib.ExitStack()
    A = contextlib.ExitStack()
    w_in = S.sbuf("w_in_bf", [128, KC, INW], BF16, W)
    r_win = [Res("w_in%d" % kc) for kc in range(KC)]
    for kc in range(KC):
        S.dma("pool", w_in[:, kc, :], w_in_d[:, kc, :], r_win[kc], writes=[r_win[kc]])
    w_out = S.sbuf("w_out_bf", [128, KC, D], BF16, W)
    r_wout = Res("w_out")
    for kc in range(KC):
        S.dma("pool", w_out[:, kc, :], w_out_d[:, kc, :], r_wout, writes=[r_wout], acc=True)

    g_pre, r_gpre = const_tile("g_pre", [128, D], F32, g_pre_d[:].partition_broadcast(128), stack=W)
    g_post, r_gpost = const_tile("g_post", [128, D], F32, g_post_d[:].partition_broadcast(128), stack=W)
    convw, r_convw = const_tile("convw", [128, 12, 4], F32, convw_d[:], stack=W)
    m_le, r_mle = const_tile("m_le", [64, 64], F32, cd["m_le"][:], stack=W)
    m_gt, r_mgt = const_tile("m_gt", [64, 64], F32, cd["m_gt"][:], stack=W)
    m_lt, r_mlt = const_tile("m_lt", [64, 64], F32, cd["m_lt"][:], stack=W)
    bias = S.sbuf("bias", [128, 8, 256], F32, W)
    r_bias = Res("bias")
    S.dma("sp", bias[:], cd["bias0"][:], r_bias, writes=[r_bias])
    sink, r_sink = const_tile("sink", [128, 8], F32, sinks_d[:].partition_broadcast(128), stack=W)
    nsink = S.sbuf("nsink", [128, 8], F32, W)
    r_nsink = Res("nsink")
    S.op("dve", lambda e: e.tensor_scalar(out=nsink[:], in0=sink[:], scalar1=-1.0, scalar2=None, op0=ALU.mult),
         reads=[r_sink], writes=[r_nsink])
    g_gdn, r_ggdn = const_tile("g_gdn", [64, 128], F32, g_gdn_d[:].partition_broadcast(64), stack=W)
    dtb8 = S.sbuf("dtb8", [64, 8], F32, W)
    r_dtb8 = Res("dtb8")
    S.dma("sp", dtb8[:, 0:4], dtb_d[:].partition_broadcast(64), r_dtb8, writes=[r_dtb8], acc=True)
    S.dma("sp", dtb8[:, 4:8], dtb_d[:].partition_broadcast(64), r_dtb8, writes=[r_dtb8], acc=True)
    negA8 = S.sbuf("negA8", [64, 8], F32, W)
    r_negA8 = Res("negA8")
    S.dma("sp", negA8[:, 0:4], alog_d[:].partition_broadcast(64), r_negA8, writes=[r_negA8], acc=True)
    S.dma("sp", negA8[:, 4:8], alog_d[:].partition_broadcast(64), r_negA8, writes=[r_negA8], acc=True)
    S.op("act", lambda e: e.activation(out=negA8[:], in_=negA8[:], func=AF.Exp), reads=[r_negA8], writes=[r_negA8])
    S.op("dve", lambda e: e.tensor_scalar(out=negA8[:], in0=negA8[:], scalar1=-1.0, scalar2=None, op0=ALU.mult),
         reads=[r_negA8], writes=[r_negA8])

    state = {"prev_xc": None}

    def inproj_tok(hT, r_hT, col_lo, ncols, tok_lo, ntok):
        bk, rb = nb()
        for kc in range(KC):
            S.op("pe", lambda e, kc=kc: e.matmul(bk[0:ntok, 0:ncols], lhsT=hT[:, kc, tok_lo:tok_lo + ntok],
                                                 rhs=w_in[:, kc, col_lo:col_lo + ncols], start=(kc == 0), stop=(kc == KC - 1)),
                 reads=[r_hT, r_win[kc]], writes=[rb], acc=(kc > 0))
        return bk, rb

    def sample_phase():
        r_misc = Res("misc")
        for sidx in range(NS):
            S.dma("sp", sconv_d[sidx, 0:2, :], cst_d[sidx, 1:3, :], r_misc, final=True)
            S.dma("sp", sk_d[sidx, 0:127, :], ck_d[sidx, 1:128, :], r_misc, final=True)
            S.dma("sp", sv_d[sidx, 0:127, :], cv_d[sidx, 1:128, :], r_misc, final=True)
        SA = contextlib.ExitStack()
        xs, r_xs = const_tile("xs", [NS, D], F32, xs_d[:], stack=SA)
        hTs = S.sbuf("hTs", [128, KC, NS], BF16, SA)
        r_hTs = Res("hTs")
        norm_transpose(xs[:, :], r_xs, g_pre, r_gpre, hTs[:, :, :], r_hTs, n=NS)
        Ps = S.sbuf("Ps", [NS, INW], F32, SA)
        r_Ps = Res("Ps")
        col = 0
        ci = 0
        while col < INW:
            ncol = min(512, INW - col)
            bk, rb = inproj_tok(hTs, r_hTs, col, ncol, 0, NS)
            eng = "act" if ci % 2 == 0 else "dve"
            if eng == "act":
                S.op("act", lambda e, bk=bk, col=col, ncol=ncol: e.copy(out=Ps[:, col:col + ncol], in_=bk[0:NS, 0:ncol]),
                     reads=[rb], writes=[r_Ps], acc=True)
            else:
                S.op("dve", lambda e, bk=bk, col=col, ncol=ncol: e.tensor_copy(out=Ps[:, col:col + ncol], in_=bk[0:NS, 0:ncol]),
                     reads=[rb], writes=[r_Ps], acc=True)
            col += ncol
            ci += 1
        S.dma("pool", sconv_d[:, 2, :], Ps[:, 768:2304], r_Ps, reads=[r_Ps], final=True)
        S.dma("pool", sk_d[:, 127, :], Ps[:, 512:640], r_Ps, reads=[r_Ps], final=True)
        S.dma("pool", sv_d[:, 127, :], Ps[:, 640:768], r_Ps, reads=[r_Ps], final=True)
        r_scr = Res("scr_qkv")
        S.dma("sp", scr_q[:, :], Ps[:, 0:512], r_scr, reads=[r_Ps], writes=[r_scr], acc=True)
        S.dma("sp", scr_kv[:, :], Ps[:, 512:768], r_scr, reads=[r_Ps], writes=[r_scr], acc=True)
        mixs = S.sbuf("mixs", [NS, D], BF16, SA)
        r_mixs = Res("mixs")
        i16b_t, r_i16b = const_tile("i16b", [128, 256], F32, i16b_d[:].partition_broadcast(128), stack=SA)
        i16b = i16b_t[:, :].rearrange("p (a b) -> p a b", a=16)
        cvs = S.sbuf("cvs", [NS, 1536], F32, SA)
        r_cvs = Res("cvs")
        sg, r_sg = S.sbuf("sg", [NS, 16, 4], F32, SA), Res("sg")
        rn8, r_rn8 = S.sbuf("rn8", [NS, 8], F32, SA), Res("rn8")

        if sstage < 1:
            S.barrier()
            SA.close()
            return
        S1 = contextlib.ExitStack()
        cst, r_cst = const_tile("cst", [NS, 3, 1536], F32, cst_d[:], stack=S1)
        cwb_t, r_cwb = const_tile("cwb", [NS, 4 * 1536], F32, convwt_d[:].partition_broadcast(NS), stack=S1)
        cwb = cwb_t[:, :].rearrange("p (i c) -> p i c", i=4)
        ctmp, r_ctmp = S.sbuf("ctmp", [NS, 1536], F32, S1), Res("ctmp")
        S.op("dve", lambda e: e.tensor_tensor(out=cvs[:, :], in0=cst[:, 0, :], in1=cwb[:, 0, :], op=ALU.mult),
             reads=[r_cst, r_cwb], writes=[r_cvs])
        for i in range(1, 4):
            src = cst[:, i, :] if i < 3 else Ps[:, 768:2304]
            S.op("dve", lambda e, i=i, src=src: e.tensor_tensor(out=ctmp[:, :], in0=src, in1=cwb[:, i, :], op=ALU.mult),
                 reads=[r_cst, r_cwb, r_Ps], writes=[r_ctmp])
            S.op("dve", lambda e: e.tensor_tensor(out=cvs[:, :], in0=cvs[:, :], in1=ctmp[:, :], op=ALU.add),
                 reads=[r_cvs, r_ctmp], writes=[r_cvs])
        S.op("act", lambda e: e.activation(out=cvs[:, :], in_=cvs[:, :], func=AF.Silu), reads=[r_cvs], writes=[r_cvs])
        S.op("dve", lambda e: e.tensor_tensor(out=ctmp[:, 0:1024], in0=cvs[:, 0:1024], in1=cvs[:, 0:1024], op=ALU.mult),
             reads=[r_cvs], writes=[r_ctmp])
        S.op("dve", lambda e: e.tensor_reduce(out=rn8[:, :], in_=ctmp[:, 0:1024].rearrange("p (h d) -> p h d", h=8),
                                              axis=AX.X, op=ALU.add), reads=[r_ctmp], writes=[r_rn8])
        S.op("act", lambda e: e.activation(out=rn8[:, :], in_=rn8[:, :], func=AF.Ln, bias=epsT[0:NS, :], scale=1.0),
             reads=[r_rn8, r_eps], writes=[r_rn8])
        S.op("act", lambda e: e.activation(out=rn8[:, :], in_=rn8[:, :], func=AF.Exp, scale=-0.5), reads=[r_rn8], writes=[r_rn8])
        S.op("dve", lambda e: e.tensor_scalar(out=rn8[:, 0:4], in0=rn8[:, 0:4], scalar1=128.0 ** -0.5, scalar2=None,
                                              op0=ALU.mult), reads=[r_rn8], writes=[r_rn8])
        S.op("dve", lambda e: e.tensor_tensor(out=cvs[:, 0:1024].rearrange("p (h d) -> p h d", h=8),
                                              in0=cvs[:, 0:1024].rearrange("p (h d) -> p h d", h=8),
                                              in1=rn8[:, :].unsqueeze(2).broadcast_to([NS, 8, 128]), op=ALU.mult),
             reads=[r_cvs, r_rn8], writes=[r_cvs])

        def sgv(i):
            return sg[:, i, :]
        S.op("dve", lambda e: e.tensor_tensor(out=sgv(0), in0=Ps[:, 2816:2820], in1=dtb8[0:NS, 0:4], op=ALU.add),
             reads=[r_Ps, r_dtb8], writes=[r_sg])
        S.op("act", lambda e: e.activation(out=sgv(4), in_=sgv(0), func=AF.Abs), reads=[r_sg], writes=[r_sg])
        S.op("act", lambda e: e.activation(out=sgv(4), in_=sgv(4), func=AF.Exp, scale=-1.0), reads=[r_sg], writes=[r_sg])
        S.op("act", lambda e: e.activation(out=sgv(4), in_=sgv(4), func=AF.Ln, bias=1.0), reads=[r_sg], writes=[r_sg])
        S.op("dve", lambda e: e.scalar_tensor_tensor(out=sgv(0), in0=sgv(0), scalar=0.0, in1=sgv(4), op0=ALU.max, op1=ALU.add),
             reads=[r_sg], writes=[r_sg])
        S.op("dve", lambda e: e.tensor_tensor(out=sgv(0), in0=sgv(0), in1=negA8[0:NS, 0:4], op=ALU.mult),
             reads=[r_sg, r_negA8], writes=[r_sg])
        S.op("act", lambda e: e.activation(out=sgv(2), in_=sgv(0), func=AF.Exp), reads=[r_sg], writes=[r_sg])
        S.op("act", lambda e: e.activation(out=sgv(1), in_=Ps[:, 2820:2824], func=AF.Exp, scale=-1.0), reads=[r_Ps], writes=[r_sg])
        S.op("dve", lambda e: e.tensor_scalar(out=sgv(1), in0=sgv(1), scalar1=1.0, scalar2=None, op0=ALU.add),
             reads=[r_sg], writes=[r_sg])
        S.op("dve", lambda e: e.reciprocal(out=sgv(1), in_=sgv(1)), reads=[r_sg], writes=[r_sg])
        S.op("dve", lambda e: e.tensor_tensor(out=ctmp[:, 0:512], in0=cvs[:, 0:512], in1=cvs[:, 512:1024], op=ALU.mult),
             reads=[r_cvs], writes=[r_ctmp])
        S.op("dve", lambda e: e.tensor_reduce(out=sgv(3), in_=ctmp[:, 0:512].rearrange("p (h d) -> p h d", h=4),
                                              axis=AX.X, op=ALU.add), reads=[r_ctmp], writes=[r_sg])
        S.barrier()
        S1.close()

        if sstage < 2:
            SA.close()
            return
        S2 = contextlib.ExitStack()
        Ssb = S.sbuf("Ssb", [128, NS * 4, 128], F32, S2)
        r_Ssb4 = [Res("Ssb%d" % i) for i in range(4)]
        r_Ssb = [r_Ssb4[i % 4] for i in range(NS)]
        for sidx in range(NS):
            S.dma("sp", Ssb[:, sidx * 4:(sidx + 1) * 4, :], sst_d[sidx, :, :, :].rearrange("h k v -> k h v"),
                  r_Ssb[sidx], writes=[r_Ssb[sidx]], acc=True)
        kqT, r_kqT = S.sbuf("kqT", [128, 8, NS], F32, S2), Res("kqT")
        bk, rb = nb()
        for hh in range(8):
            S.op("pe", lambda e, hh=hh, bk=bk: e.transpose(bk[:, hh * NS:(hh + 1) * NS], cvs[:, hh * 128:(hh + 1) * 128],
                                                           idf[0:NS, 0:NS]), reads=[r_cvs, r_idf], writes=[rb], acc=(hh > 0))
        S.op("act", lambda e, bk=bk: e.copy(out=kqT[:, :, :], in_=bk[:, 0:8 * NS].rearrange("p (h s) -> p h s", h=8)),
             reads=[rb], writes=[r_kqT])
        kqTm, r_kqTm = S.sbuf("kqTm", [128, 8, NS, NS], F32, S2), Res("kqTm")
        S.op("dve", lambda e: e.tensor_tensor(out=kqTm[:, :, :, :],
                                              in0=kqT[:, :, :].unsqueeze(2).broadcast_to([128, 8, NS, NS]),
                                              in1=i16b.unsqueeze(1).broadcast_to([128, 8, NS, NS]), op=ALU.mult),
             reads=[r_kqT, r_i16b], writes=[r_kqTm])
        bkk, rbk = nb()
        bkq2, rbq2 = nb()
        for h in range(4):
            for sidx in range(NS):
                S.op("pe", lambda e, h=h, sidx=sidx: e.matmul(bkk[0:NS, h * 128:(h + 1) * 128], lhsT=kqTm[:, 4 + h, sidx, :],
                                                              rhs=Ssb[:, sidx * 4 + h, :], start=(sidx == 0), stop=(sidx == NS - 1)),
                     reads=[r_kqTm, r_Ssb[sidx]], writes=[rbk], acc=not (h == 0 and sidx == 0))
        for h in range(4):
            for sidx in range(NS):
                S.op("pe", lambda e, h=h, sidx=sidx: e.matmul(bkq2[0:NS, h * 128:(h + 1) * 128], lhsT=kqTm[:, h, sidx, :],
                                                              rhs=Ssb[:, sidx * 4 + h, :], start=(sidx == 0), stop=(sidx == NS - 1)),
                     reads=[r_kqTm, r_Ssb[sidx]], writes=[rbq2], acc=not (h == 0 and sidx == 0))
        us, r_us = S.sbuf("us", [NS, 4, 128], F32, S2), Res("us")
        os_, r_os = S.sbuf("os", [NS, 4, 128], F32, S2), Res("os")
        ot, r_ot = S.sbuf("ot", [NS, 4, 128], F32, S2), Res("ot")

        def bcs(i):
            return sg[:, i, :].unsqueeze(2).broadcast_to([NS, 4, 128])
        v3 = cvs[:, 1024:1536].rearrange("p (h d) -> p h d", h=4)
        S.op("dve", lambda e: e.tensor_tensor(out=us[:, :, :], in0=bkk[0:NS, :].rearrange("p (h d) -> p h d", h=4),
                                              in1=bcs(2), op=ALU.mult), reads=[rbk, r_sg], writes=[r_us])
        S.op("dve", lambda e: e.tensor_tensor(out=us[:, :, :], in0=v3, in1=us[:, :, :], op=ALU.subtract),
             reads=[r_cvs, r_us], writes=[r_us])
        S.op("dve", lambda e: e.tensor_tensor(out=us[:, :, :], in0=us[:, :, :], in1=bcs(1), op=ALU.mult),
             reads=[r_us, r_sg], writes=[r_us])
        S.op("dve", lambda e: e.tensor_tensor(out=os_[:, :, :], in0=bkq2[0:NS, :].rearrange("p (h d) -> p h d", h=4),
                                              in1=bcs(2), op=ALU.mult), reads=[rbq2, r_sg], writes=[r_os])
        S.op("dve", lambda e: e.tensor_tensor(out=ot[:, :, :], in0=us[:, :, :], in1=bcs(3), op=ALU.mult),
             reads=[r_us, r_sg], writes=[r_ot])
        S.op("dve", lambda e: e.tensor_tensor(out=os_[:, :, :], in0=os_[:, :, :], in1=ot[:, :, :], op=ALU.add),
             reads=[r_os, r_ot], writes=[r_os])
        S.op("dve", lambda e: e.tensor_tensor(out=ot[:, :, :], in0=os_[:, :, :], in1=os_[:, :, :], op=ALU.mult),
             reads=[r_os], writes=[r_ot])
        S.op("dve", lambda e: e.tensor_reduce(out=sgv(5), in_=ot[:, :, :], axis=AX.X, op=ALU.add), reads=[r_ot], writes=[r_sg])
        S.op("act", lambda e: e.activation(out=sgv(5), in_=sgv(5), func=AF.Ln, bias=epsT[0:NS, :], scale=1.0 / 128),
             reads=[r_sg, r_eps], writes=[r_sg])
        S.op("act", lambda e: e.activation(out=sgv(5), in_=sgv(5), func=AF.Exp, scale=-0.5), reads=[r_sg], writes=[r_sg])
        S.op("act", lambda e: e.activation(out=ot[:, :, :].rearrange("p h d -> p (h d)"), in_=Ps[:, 2304:2816], func=AF.Silu),
             reads=[r_Ps], writes=[r_ot])
        S.op("dve", lambda e: e.tensor_tensor(out=ot[:, :, :], in0=ot[:, :, :],
                                              in1=g_gdn[0:NS, :].unsqueeze(1).broadcast_to([NS, 4, 128]), op=ALU.mult),
             reads=[r_ot, r_ggdn], writes=[r_ot])
        S.op("dve", lambda e: e.tensor_tensor(out=os_[:, :, :], in0=os_[:, :, :], in1=bcs(5), op=ALU.mult),
             reads=[r_os, r_sg], writes=[r_os])
        S.op("dve", lambda e: e.tensor_tensor(out=mixs[:, 512:1024].rearrange("p (h d) -> p h d", h=4), in0=os_[:, :, :],
                                              in1=ot[:, :, :], op=ALU.mult), reads=[r_os, r_ot], writes=[r_mixs], acc=True)
        um, r_um = S.sbuf("um", [NS, NS, 512], F32, S2), Res("um")
        S.op("dve", lambda e: e.tensor_tensor(out=um[:, :, :],
                                              in0=us[:, :, :].rearrange("p h d -> p (h d)").unsqueeze(1).broadcast_to([NS, NS, 512]),
                                              in1=idf[0:NS, 0:NS].unsqueeze(2).broadcast_to([NS, NS, 512]), op=ALU.mult),
             reads=[r_us, r_idf], writes=[r_um])
        dm, r_dm = S.sbuf("dm", [NS, NS, 4], F32, S2), Res("dm")
        S.op("dve", lambda e: e.tensor_tensor(out=dm[:, :, :], in0=sg[:, 2, :].unsqueeze(1).broadcast_to([NS, NS, 4]),
                                              in1=idf[0:NS, 0:NS].unsqueeze(2).broadcast_to([NS, NS, 4]), op=ALU.mult),
             reads=[r_sg, r_idf], writes=[r_dm])
        bkd, rbd = nb()
        S.op("pe", lambda e: e.matmul(bkd[:, 0:NS * 4], lhsT=ones[0:NS, :], rhs=dm[:, :, :].rearrange("p s h -> p (s h)"),
                                      start=True, stop=True), reads=[r_dm, r_ones], writes=[rbd])
        dec128, r_dec = S.sbuf("dec128", [128, NS * 4], F32, S2), Res("dec128")
        S.op("act", lambda e: e.copy(out=dec128[:, :], in_=bkd[:, 0:NS * 4]), reads=[rbd], writes=[r_dec])
        for sidx in range(NS):
            bk, rb = nb()
            for h in range(4):
                S.op("pe", lambda e, h=h, sidx=sidx, bk=bk: e.matmul(bk[:, h * 128:(h + 1) * 128],
                                                                      lhsT=cvs[:, 512 + h * 128:512 + (h + 1) * 128],
                                                                      rhs=um[:, sidx, h * 128:(h + 1) * 128], start=True, stop=True),
                     reads=[r_cvs, r_um], writes=[rb], acc=(h > 0))
            for h in range(4):
                sh = sidx * 4 + h
                S.op("dve", lambda e, h=h, sh=sh, bk=bk: e.scalar_tensor_tensor(
                    out=Ssb[:, sh, :], in0=Ssb[:, sh, :], scalar=dec128[:, sh:sh + 1], in1=bk[:, h * 128:(h + 1) * 128],
                    op0=ALU.mult, op1=ALU.add), reads=[r_Ssb[sidx], r_dec, rb], writes=[r_Ssb[sidx]])
            S.dma("pool", sS_d[sidx, :, :, :].rearrange("h k v -> k h v"), Ssb[:, sidx * 4:(sidx + 1) * 4, :], r_Ssb[sidx],
                  reads=[r_Ssb[sidx]], final=True)
        S.barrier()
        S2.close()

        if sstage < 3:
            SA.close()
            return
        S3 = contextlib.ExitStack()
        Kc = S.sbuf("Kc", [128, 128, 64], F32, S3)
        Vc = Kc
        r_Kc = Res("Kc")
        r_Vc = r_Kc
        Kc32 = S.sbuf("Kc32", [32, 128, 64], F32, S3)
        Vc32 = Kc32
        r_Kc32 = Res("Kc32")
        r_Vc32 = r_Kc32
        for kh in range(2):
            S.dma("sp", Kc32[kh * 16:(kh + 1) * 16, :, :], ck_d[:, :, kh * 64:(kh + 1) * 64], r_Kc32, writes=[r_Kc32], acc=True)
        selm, r_selm = const_tile("selm", [32, 128], F32, cd["selm"][:], stack=S3)
        qsh, r_qsh = S.sbuf("qsh", [128, 64], F32, S3), Res("qsh")
        knew, r_knew = S.sbuf("knew", [128, 64], F32, S3), Res("knew")
        vnew, r_vnew = S.sbuf("vnew", [128, 64], F32, S3), Res("vnew")
        S.dma("sp", qsh[:, :], scr_q[:, :].rearrange("s (h d) -> (s h) d", h=8), r_qsh, reads=[r_scr], writes=[r_qsh])
        for sidx in range(NS):
            for kh in range(2):
                p0 = sidx * 8 + kh * 4
                S.dma("sp", knew[p0:p0 + 4, :], scr_kv[sidx, kh * 64:(kh + 1) * 64].partition_broadcast(4),
                      r_knew, reads=[r_scr], writes=[r_knew], acc=True)
                S.dma("sp", vnew[p0:p0 + 4, :], scr_kv[sidx, 128 + kh * 64:128 + (kh + 1) * 64].partition_broadcast(4),
                      r_vnew, reads=[r_scr], writes=[r_vnew], acc=True)
        biass, r_biass = const_tile("biass", [128, 129], F32, biass_d[:], stack=S3)
        sinksh, r_sinksh = const_tile("sinksh", [128, 1], F32, sinksh_d[:], stack=S3)
        scs, r_scs = S.sbuf("scs", [128, 132], F32, S3), Res("scs")
        ps_, r_ps = S.sbuf("ps", [128, 132], F32, S3), Res("ps")
        tq, r_tq = S.sbuf("tq", [128, 64], F32, S3), Res("tq")
        aos, r_aos = S.sbuf("aos", [128, 64], F32, S3), Res("aos")
        for jb in range(16):
            bk, rb = nb()
            S.op("pe", lambda e, jb=jb, bk=bk: e.matmul(bk[:, :], lhsT=selm[:, :],
                                                        rhs=Kc32[:, jb * 8:(jb + 1) * 8, :].rearrange("p a d -> p (a d)"),
                                                        start=True, stop=True), reads=[r_selm, r_Kc32], writes=[rb])
            S.op("dve", lambda e, jb=jb, bk=bk: e.tensor_tensor(
                out=Kc[:, jb * 8:(jb + 1) * 8, :], in0=bk[:, :].rearrange("p (a d) -> p a d", a=8),
                in1=qsh[:, :].unsqueeze(1).broadcast_to([128, 8, 64]), op=ALU.mult),
                reads=[rb, r_qsh], writes=[r_Kc], acc=(jb > 0))
        for kh in range(2):
            S.dma("sp", Vc32[kh * 16:(kh + 1) * 16, :, :], cv_d[:, :, kh * 64:(kh + 1) * 64], r_Vc32, writes=[r_Vc32],
                  acc=(kh > 0))
        S.op("dve", lambda e: e.tensor_reduce(out=scs[:, 0:128], in_=Kc[:, :, :], axis=AX.X, op=ALU.add),
             reads=[r_Kc], writes=[r_scs])
        S.op("dve", lambda e: e.tensor_tensor(out=tq[:, :], in0=qsh[:, :], in1=knew[:, :], op=ALU.mult),
             reads=[r_qsh, r_knew], writes=[r_tq])
        S.op("dve", lambda e: e.tensor_reduce(out=scs[:, 128:129], in_=tq[:, :], axis=AX.X, op=ALU.add),
             reads=[r_tq], writes=[r_scs], acc=True)
        S.op("dve", lambda e: e.scalar_tensor_tensor(out=scs[:, 0:129], in0=scs[:, 0:129], scalar=0.125, in1=biass[:, :],
                                                     op0=ALU.mult, op1=ALU.add), reads=[r_scs, r_biass], writes=[r_scs])
        st, rs = stat.next()
        S.op("dve", lambda e: e.tensor_reduce(out=st[:, 0:1], in_=scs[:, 0:129], axis=AX.X, op=ALU.max), reads=[r_scs], writes=[rs])
        S.op("dve", lambda e: e.tensor_tensor(out=st[:, 0:1], in0=st[:, 0:1], in1=sinksh[:, :], op=ALU.max),
             reads=[rs, r_sinksh], writes=[rs])
        S.op("dve", lambda e: e.tensor_scalar(out=st[:, 1:2], in0=st[:, 0:1], scalar1=-1.0, scalar2=None, op0=ALU.mult),
             reads=[rs], writes=[rs])
        S.op("act", lambda e: e.activation(out=ps_[:, 0:129], in_=scs[:, 0:129], func=AF.Exp, bias=st[:, 1:2],
                                           accum_out=st[:, 2:3]), reads=[r_scs, rs], writes=[r_ps, rs])
        S.op("act", lambda e: e.activation(out=st[:, 3:4], in_=sinksh[:, :], func=AF.Exp, bias=st[:, 1:2]),
             reads=[r_sinksh, rs], writes=[rs])
        S.op("dve", lambda e: e.tensor_tensor(out=st[:, 4:5], in0=st[:, 2:3], in1=st[:, 3:4], op=ALU.add), reads=[rs], writes=[rs])
        S.op("dve", lambda e: e.reciprocal(out=st[:, 5:6], in_=st[:, 4:5]), reads=[rs], writes=[rs])
        for jb in range(16):
            bk, rb = nb()
            S.op("pe", lambda e, jb=jb, bk=bk: e.matmul(bk[:, :], lhsT=selm[:, :],
                                                        rhs=Vc32[:, jb * 8:(jb + 1) * 8, :].rearrange("p a d -> p (a d)"),
                                                        start=True, stop=True), reads=[r_selm, r_Vc32], writes=[rb])
            S.op("dve", lambda e, jb=jb, bk=bk: e.tensor_tensor(
                out=Vc[:, jb * 8:(jb + 1) * 8, :], in0=bk[:, :].rearrange("p (a d) -> p a d", a=8),
                in1=ps_[:, jb * 8:(jb + 1) * 8].unsqueeze(2).broadcast_to([128, 8, 64]), op=ALU.mult),
                reads=[rb, r_ps], writes=[r_Vc], acc=(jb > 0))
        S.op("dve", lambda e: e.tensor_reduce(out=aos[:, :], in_=Vc[:, :, :].rearrange("p s d -> p d s"), axis=AX.X, op=ALU.add),
             reads=[r_Vc], writes=[r_aos])
        S.op("dve", lambda e: e.scalar_tensor_tensor(out=aos[:, :], in0=vnew[:, :], scalar=ps_[:, 128:129], in1=aos[:, :],
                                                     op0=ALU.mult, op1=ALU.add), reads=[r_vnew, r_ps, r_aos], writes=[r_aos])
        S.op("dve", lambda e: e.tensor_scalar(out=aos[:, :], in0=aos[:, :], scalar1=st[:, 5:6], scalar2=None, op0=ALU.mult),
             reads=[r_aos, rs], writes=[r_aos])
        r_sao = Res("scr_ao")
        S.dma("sp", scr_ao[:, :], aos[:, :], r_sao, reads=[r_aos], writes=[r_sao])
        aot, r_aot = S.sbuf("aot", [NS, 512], F32, S3), Res("aot")
        S.dma("sp", aot[:, :], scr_ao[:, :].rearrange("(s h) d -> s (h d)", h=8), r_aot, reads=[r_sao], writes=[r_aot])
        S.op("dve", lambda e: e.tensor_copy(out=mixs[:, 0:512], in_=aot[:, :]), reads=[r_aot], writes=[r_mixs], acc=True)
        mixTs, r_mixTs = S.sbuf("mixTs", [128, KC, NS], BF16, S3), Res("mixTs")
        bk, rb = nb()
        bkb = bk[:, :].bitcast(BF16)
        for kc in range(KC):
            S.op("pe", lambda e, kc=kc: e.transpose(bkb[:, kc * NS:(kc + 1) * NS], mixs[:, kc * 128:(kc + 1) * 128],
                                                    idb[0:NS, 0:NS]), reads=[r_mixs, r_idb], writes=[rb], acc=(kc > 0))
        S.op("act", lambda e: e.copy(out=mixTs[:, :, :], in_=bkb[:, 0:KC * NS].rearrange("p (k s) -> p k s", k=KC)),
             reads=[rb], writes=[r_mixTs])
        bk0, rb0 = nb()
        bk1, rb1 = nb()
        for n, (bk, rb) in enumerate(((bk0, rb0), (bk1, rb1))):
            for kc in range(KC):
                S.op("pe", lambda e, kc=kc, n=n, bk=bk: e.matmul(bk[0:NS, :], lhsT=mixTs[:, kc, :],
                                                                 rhs=w_out[:, kc, n * 512:(n + 1) * 512],
                                                                 start=(kc == 0), stop=(kc == KC - 1)),
                     reads=[r_mixTs, r_wout], writes=[rb], acc=(kc > 0))
        x1s, r_x1s = S.sbuf("x1s", [NS, D], F32, S3), Res("x1s")
        epilogue(bk0, rb0, bk1, rb1, g_post, r_gpost, xs[:, :], r_xs, x1s[:, :], r_x1s, n=NS)
        r_ys = Res("ys")
        S.dma("sp", ys_d[:, :], x1s[:, :], r_x1s, reads=[r_x1s], writes=[r_ys])
        state["r_ys"] = r_ys
        S.barrier()
        S3.close()
        SA.close()

    if do_sample:
        sample_phase()
    S.barrier()

    xring = Ring(S, "xt", 2, [128, D], F32, AC)
    def gbuf(name, shape, dt, st=None):
        return S.sbuf(name, shape, dt, st or A), Res(name)

    mixT_ring = Ring(S, "mixT", 2, [128, KC, 128], BF16, AC)
    x1_ring = None if xchg else Ring(S, "x1t", 1, [128, D], F32, AC)
    Sst = S.sbuf("Sst", [128, 4, 256], F32, AC)
    S_bf, r_Sbf = gbuf("S_bf", [128, 4, 256], BF16, AC)
    u_bf, r_u = gbuf("u_bf", [64, 4, 256], BF16, AC)
    o1, r_o1 = gbuf("o1", [64, 4, 128], F32, AC)
    o2, r_o2 = gbuf("o2", [64, 4, 128], F32, AC)
    sz, r_sz = gbuf("sz", [64, 4, 128], F32, AC)
    y_bf, r_ybf = gbuf("y_bf", [64, 4, 128], BF16, AC)
    r_S = [Res("S%d" % h) for h in range(4)]
    hTring = Ring(S, "hT", 3 if xchg else 2, [128, KC, 128], BF16, A)
    patt_ring = Ring(S, "patt", 1, [128, 768], F32, A)
    zab_ring = Ring(S, "zab", 2, [64, 2, 520], F32, A)
    xc_ring = Ring(S, "xc", 2, [128, 12, 131], F32, A)
    kT_ring = Ring(S, "kTr", 3, [64, 2, 128], BF16, A)
    v_ring = Ring(S, "vr", 3, [128, 128], BF16, A)

    acc_t = S.sbuf("convacc", [128, 12, 128], F32, A)
    r_acc = [Res("acc%d" % m) for m in range(12)]
    qkvT = acc_t
    r_qkvT = Res("qkvT")

    qTa = S.sbuf("qTa", [64, 8, 128], BF16, A)
    r_qTa = Res("qTa")
    sc = S.sbuf("sc", [128, 8, 256], F32, A)
    r_sc = [Res("sc%d" % i) for i in range(4)]
    pbf = S.sbuf("pbf", [128, 8, 256], BF16, A)
    r_pbf = [Res("pbf%d" % i) for i in range(8)]
    pT = S.sbuf("pT", [128, 16, 128], BF16, A)
    r_pT = [Res("pT0"), Res("pT1")]
    attn_o = S.sbuf("attn_o", [128, 512], BF16, A)
    r_attn_o = Res("attn_o")


    sqT = S.sbuf("sqT", [128, 8, 128], F32, A)
    r_sqTl = [Res("sqT")]
    gs, r_gs = gbuf("gsc", [64, 16, 8], F32)
    kt_bf, r_kt = gbuf("kt_bf", [64, 8, 128], BF16)
    kd_bf, r_kd = gbuf("kd_bf", [64, 8, 128], BF16)
    kn_bf, r_kn = gbuf("kn_bf", [64, 8, 128], BF16)
    v_bf, r_vbf = gbuf("v_bf", [64, 8, 128], BF16)
    knT, r_knT = gbuf("knT", [128, 8, 64], BF16)
    qT_bf, r_qTbf = gbuf("qT_bf", [128, 4, 128], BF16)
    gM, r_gM = gbuf("gM", [64, 8, 64], F32)
    Emat, r_E = gbuf("Emat", [64, 8, 64], F32)
    mb, r_mb = gM, r_gM
    Es, r_Es = gbuf("Es", [64, 8, 64], F32)
    Ei, r_Ei = gbuf("Ei", [64, 8, 64], F32)
    Bp = [gbuf("Bp%d" % i, [64, 8, 64], F32) for i in range(2)]
    Ap = [gbuf("Ap%d" % i, [64, 8, 64], F32) for i in range(2)]
    Zm, r_Z = Emat, r_E
    dgb, r_dgb = gM, r_gM
    qkT, r_qkT = gbuf("qkT", [64, 8, 64], BF16)
    NTb, r_NTb = gbuf("NTb", [64, 8, 64], BF16)
    ub, r_ub = gbuf("ub", [64, 8, 128], F32)
    wT_bf, r_wT = gbuf("wT_bf", [128, 8, 64], BF16)
    osq, r_osq = o2, r_o2

    S.op("dve", lambda e: e.memset(gs[:, :, :], 0.0), writes=[r_gs])
    S.op("dve", lambda e: e.memset(Sst[:], 0.0), writes=r_S)
    if xchg:
        S.op("dve", lambda e: e.tensor_copy(out=Sst[:, :, 128:256], in_=idf[:, :].unsqueeze(1).broadcast_to([128, 4, 128])),
             reads=[r_idf] + r_S, writes=r_S)
    S.op("act", lambda e: e.copy(out=S_bf[:, :, :], in_=Sst[:, :, :]), reads=r_S, writes=[r_Sbf])


    def load_x(t):
        xt, rx = xring.next()
        S.dma("sp", xt[:], xin[(t + npre) * 128:(t + npre + 1) * 128, :], rx, writes=[rx])
        return xt, rx

    def inproj_feat(hT, r_hT, xc, r_xc, tok_lo, ntok, out_off, groups=(0, 1, 2)):
        for grp in groups:
            bk, rb = nb()
            first = True
            for mm in range(4):
                m = grp * 4 + mm
                for kc in range(KC):
                    S.op("pe", lambda e, kc=kc, m=m, mm=mm: e.matmul(
                        bk[:, mm * 128:mm * 128 + ntok], lhsT=w_in[:, kc, 768 + m * 128:768 + (m + 1) * 128],
                        rhs=hT[:, kc, tok_lo:tok_lo + ntok], start=(kc == 0), stop=(kc == KC - 1)),
                        reads=[r_hT, r_win[kc]], writes=[rb], acc=not first)
                    first = False
            S.op("act", lambda e, grp=grp: e.copy(
                out=xc[:, grp * 4:(grp + 1) * 4, out_off:out_off + ntok],
                in_=bk[:, :].rearrange("p (m t) -> p m t", m=4)[:, :, 0:ntok]),
                reads=[rb], writes=[r_xc], acc=True)

    def kv_prep(patt, r_patt):
        kT, r_kT = kT_ring.next()
        vv, r_v = v_ring.next()
        bk, rb = nb()
        for kh in range(2):
            S.op("pe", lambda e, kh=kh: e.transpose(bk[0:64, kh * 128:(kh + 1) * 128],
                                                    patt[:, 512 + kh * 64:512 + (kh + 1) * 64], idf[:, :]),
                 reads=[r_patt, r_idf], writes=[rb], acc=(kh > 0))
        S.op("act", lambda e: e.copy(out=kT[:, :, :], in_=bk[0:64, 0:256].rearrange("p (k t) -> p k t", k=2)),
             reads=[rb], writes=[r_kT])
        S.op("dve", lambda e: e.tensor_copy(out=vv[:, :], in_=patt[:, 640:768]), reads=[r_patt], writes=[r_v])
        return (kT, r_kT, vv, r_v)

    def attention(patt, r_patt, prev_kv, cur_kv, mixT, r_mixT):
        kTp, r_kTp, vp, r_vp = prev_kv
        kTc, r_kTc, vc, r_vc = cur_kv
        for half in range(2):
            bk, rb = nb()
            for hh in range(4):
                h = half * 4 + hh
                S.op("pe", lambda e, h=h, hh=hh: e.transpose(bk[0:64, hh * 128:(hh + 1) * 128],
                                                             patt[:, h * 64:(h + 1) * 64], idf[:, :]),
                     reads=[r_patt, r_idf], writes=[rb], acc=(hh > 0))
            S.op("act", lambda e, half=half: e.activation(
                out=qTa[:, half * 4:(half + 1) * 4, :], in_=bk[0:64, :].rearrange("p (h t) -> p h t", h=4),
                func=AF.Copy, scale=0.125), reads=[rb], writes=[r_qTa], acc=(half > 0))
        yield
        for pr in range(4):
            bk, rb = nb()
            first = True
            for hh in range(2):
                h = pr * 2 + hh
                kh = h // 4
                S.op("pe", lambda e, h=h, hh=hh, kh=kh: e.matmul(bk[:, hh * 256:hh * 256 + 128], lhsT=qTa[:, h, :],
                                                                 rhs=kTp[:, kh, :], start=True, stop=True),
                     reads=[r_qTa, r_kTp], writes=[rb], acc=not first)
                first = False
                S.op("pe", lambda e, h=h, hh=hh, kh=kh: e.matmul(bk[:, hh * 256 + 128:hh * 256 + 256], lhsT=qTa[:, h, :],
                                                                 rhs=kTc[:, kh, :], start=True, stop=True),
                     reads=[r_qTa, r_kTc], writes=[rb], acc=True)
            S.op("dve", lambda e, pr=pr: e.tensor_tensor(
                out=sc[:, pr * 2:(pr + 1) * 2, :], in0=bk[:, :].rearrange("p (h s) -> p h s", h=2),
                in1=bias[:, pr * 2:(pr + 1) * 2, :], op=ALU.add), reads=[rb, r_bias], writes=[r_sc[pr]])
        yield
        st, rs = stat.next()
        S.op("dve", lambda e: e.tensor_reduce(out=st[:, 0:8], in_=sc[:, :, :], axis=AX.X, op=ALU.max),
             reads=r_sc, writes=[rs])
        yield
        S.op("dve", lambda e: e.scalar_tensor_tensor(out=st[:, 8:16], in0=st[:, 0:8], scalar=-1.0, in1=nsink[:, :],
                                                     op0=ALU.mult, op1=ALU.min), reads=[rs, r_nsink], writes=[rs])
        yield
        st2, rs2 = stat.next()
        for h in range(8):
            S.op("act", lambda e, h=h: e.activation(out=pbf[:, h, :], in_=sc[:, h, :], func=AF.Exp,
                                                    bias=st[:, 8 + h:9 + h], accum_out=st2[:, h:h + 1]),
                 reads=[r_sc[h // 2], rs], writes=[r_pbf[h], rs2], acc=(h > 0))
        yield
        S.op("dve", lambda e: e.tensor_tensor(out=st[:, 0:8], in0=st[:, 8:16], in1=sink[:, :], op=ALU.add),
             reads=[rs, r_sink], writes=[rs])
        yield
        S.op("act", lambda e: e.activation(out=st2[:, 8:16], in_=st[:, 0:8], func=AF.Exp), reads=[rs], writes=[rs2], acc=True)
        yield
        S.op("dve", lambda e: e.tensor_tensor(out=st2[:, 0:8], in0=st2[:, 0:8], in1=st2[:, 8:16], op=ALU.add),
             reads=[rs2], writes=[rs2])
        yield
        S.op("dve", lambda e: e.reciprocal(out=st2[:, 8:16], in_=st2[:, 0:8]), reads=[rs2], writes=[rs2])
        yield
        for half in range(2):
            bk, rb = nb()
            bkb = bk[:, :].bitcast(BF16)
            first = True
            for hh in range(4):
                h = half * 4 + hh
                for sh in range(2):
                    idx = hh * 2 + sh
                    S.op("pe", lambda e, h=h, sh=sh, idx=idx: e.transpose(
                        bkb[:, idx * 128:(idx + 1) * 128], pbf[:, h, sh * 128:(sh + 1) * 128], idb[:, :]),
                        reads=[r_pbf[h], r_idb], writes=[rb], acc=not first)
                    first = False
            eng = "act" if half == 0 else "dve"
            if eng == "act":
                S.op("act", lambda e, half=half: e.copy(out=pT[:, half * 8:(half + 1) * 8, :],
                                                        in_=bkb.rearrange("p (i q) -> p i q", i=8)),
                     reads=[rb], writes=[r_pT[half]])
            else:
                S.op("dve", lambda e, half=half: e.tensor_copy(out=pT[:, half * 8:(half + 1) * 8, :],
                                                               in_=bkb.rearrange("p (i q) -> p i q", i=8)),
                     reads=[rb], writes=[r_pT[half]])
        yield
        bk, rb = nb()
        first = True
        for h in range(8):
            kh = h // 4
            S.op("pe", lambda e, h=h, kh=kh: e.matmul(bk[:, h * 64:(h + 1) * 64], lhsT=pT[:, h * 2, :],
                                                      rhs=vp[:, kh * 64:(kh + 1) * 64], start=True, stop=False),
                 reads=[r_pT[h // 4], r_vp], writes=[rb], acc=not first)
            first = False
            S.op("pe", lambda e, h=h, kh=kh: e.matmul(bk[:, h * 64:(h + 1) * 64], lhsT=pT[:, h * 2 + 1, :],
                                                      rhs=vc[:, kh * 64:(kh + 1) * 64], start=False, stop=True),
                 reads=[r_pT[h // 4], r_vc], writes=[rb], acc=True)
        yield
        S.op("dve", lambda e: e.tensor_tensor(
            out=attn_o[:, :].rearrange("p (h d) -> p h d", h=8), in0=bk[:, :].rearrange("p (h d) -> p h d", h=8),
            in1=st2[:, 8:16].unsqueeze(2).broadcast_to([128, 8, 64]), op=ALU.mult),
            reads=[rb, rs2], writes=[r_attn_o])
        yield
        bk, rb = nb()
        bkb = bk[:, :].bitcast(BF16)
        for c4 in range(4):
            S.op("pe", lambda e, c4=c4: e.transpose(bkb[:, c4 * 128:(c4 + 1) * 128], attn_o[:, c4 * 128:(c4 + 1) * 128],
                                                    idb[:, :]), reads=[r_attn_o, r_idb], writes=[rb], acc=(c4 > 0))
        yield
        S.op("act", lambda e: e.copy(out=mixT[:, 0:4, :], in_=bkb[:, 0:512].rearrange("p (c t) -> p c t", c=4)),
             reads=[rb], writes=[r_mixT], acc=True)
        yield

    wT_bf_g, r_wT_g, qT_bf_g, r_qTbf_g, kd_bf_g, r_kd_g = wT_bf, r_wT, qT_bf, r_qTbf, kd_bf, r_kd
    ub_g, r_ub_g, qkT_g, r_qkT_g, gs_g, r_gs_g = ub, r_ub, qkT, r_qkT, gs, r_gs

    def gdn_prep(xc, r_xc, zab, r_zab, t, full):
        yield from gdn_prep1(xc, r_xc, zab, r_zab, t, full)
        yield from gdn_prep2(t, full)

    def par(gens, pools):
        gl = [(g, pools[i]) for i, g in enumerate(gens) if g is not None]
        outer = bstate["pool"]
        while gl:
            for item in list(gl):
                g, pl = item
                bstate["pool"] = pl
                try:
                    next(g)
                    bstate["pool"] = outer
                    yield
                except StopIteration:
                    gl.remove(item)
        bstate["pool"] = outer

    def gdn(xc, r_xc, zab, r_zab, mixT, r_mixT, t, full=True, aug=False, prepq=False):
        yield from gdn_prep(xc, r_xc, zab, r_zab, t, full or prepq)
        yield from gdn_scan(zab, r_zab, mixT, r_mixT, t, full, aug)

    def gsv(i):
        return gs[:, i, :]

    def bc_u(ap8, n):
        return ap8.unsqueeze(2).broadcast_to([64, ap8.shape[1], n])

    def gdn_prep1(xc, r_xc, zab, r_zab, t, full):
        m0 = 0 if full else 4
        for m in range(m0, 12):
            S.op("act", lambda e, m=m: e.activation(out=acc_t[:, m, :], in_=xc[:, m, 0:128], func=AF.Copy,
                                                    scale=convw[:, m, 0:1]), reads=[r_xc, r_convw], writes=[r_acc[m], r_qkvT], acc=(m > m0))
        yield
        for i in range(1, 4):
            for m in range(m0, 12):
                S.op("dve", lambda e, m=m, i=i: e.scalar_tensor_tensor(
                    out=acc_t[:, m, :], in0=xc[:, m, i:i + 128], scalar=convw[:, m, i:i + 1], in1=acc_t[:, m, :],
                    op0=ALU.mult, op1=ALU.add), reads=[r_xc, r_convw, r_acc[m]], writes=[r_acc[m]])
        yield
        S.op("act", lambda e: e.activation(out=qkvT[:, m0:12, :], in_=acc_t[:, m0:12, :], func=AF.Silu),
             reads=r_acc[m0:], writes=[r_qkvT] + r_acc[m0:])
        yield
        S.op("act", lambda e: e.activation(out=sqT[:, m0:8, :], in_=qkvT[:, m0:8, :], func=AF.Square),
             reads=[r_qkvT], writes=r_sqTl)
        yield
        bkq, rbq = nb()
        first = True
        for qk_ in range(0 if full else 1, 2):
            for c in range(2):
                for h in range(4):
                    col = qk_ * 8 + c * 4 + h
                    S.op("pe", lambda e, qk_=qk_, c=c, h=h, col=col: e.matmul(
                        bkq[0:64, col:col + 1], lhsT=sqT[:, qk_ * 4 + h, c * 64:(c + 1) * 64], rhs=ones[:, 0:1],
                        start=True, stop=True), reads=r_sqTl + [r_ones], writes=[rbq], acc=not first)
                    first = False
        yield
        q0 = 0 if full else 1
        S.op("act", lambda e: e.activation(out=gs[:, q0:2, :].rearrange("p a b -> p (a b)"), in_=bkq[0:64, q0 * 8:16],
                                           func=AF.Ln, bias=epsT[0:64, :], scale=1.0), reads=[rbq, r_eps], writes=[r_gs])
        yield
        S.op("act", lambda e: e.activation(out=gs[:, q0:2, :], in_=gs[:, q0:2, :], func=AF.Exp, scale=-0.5), reads=[r_gs], writes=[r_gs])
        yield
        a_ap = zab[:, :, 512:516]
        b_ap = zab[:, :, 516:520]

        def v8(i):
            return gs[:, i, :].rearrange("p (c h) -> p c h", c=2)
        S.op("dve", lambda e: e.tensor_tensor(out=v8(2), in0=a_ap, in1=dtb8[:, :].rearrange("p (c h) -> p c h", c=2),
                                              op=ALU.add), reads=[r_zab, r_dtb8], writes=[r_gs])
        yield
        S.op("act", lambda e: e.activation(out=gsv(3), in_=gsv(2), func=AF.Abs),
             reads=[r_gs], writes=[r_gs])
        yield
        S.op("act", lambda e: e.activation(out=gsv(3), in_=gsv(3), func=AF.Exp, scale=-1.0), reads=[r_gs], writes=[r_gs])
        yield
        S.op("act", lambda e: e.activation(out=gsv(3), in_=gsv(3), func=AF.Ln, bias=1.0), reads=[r_gs], writes=[r_gs])
        yield
        S.op("dve", lambda e: e.scalar_tensor_tensor(out=gsv(2), in0=gsv(2), scalar=0.0, in1=gsv(3),
                                                     op0=ALU.max, op1=ALU.add), reads=[r_gs], writes=[r_gs])
        yield
        S.op("dve", lambda e: e.tensor_tensor(out=gsv(2), in0=gsv(2), in1=negA8[:, :], op=ALU.mult),
             reads=[r_gs, r_negA8], writes=[r_gs])
        yield
        S.op("act", lambda e: e.activation(out=v8(3), in_=b_ap, func=AF.Exp, scale=-1.0), reads=[r_zab], writes=[r_gs])
        yield
        S.op("dve", lambda e: e.tensor_scalar(out=gsv(3), in0=gsv(3), scalar1=1.0, scalar2=None, op0=ALU.add),
             reads=[r_gs], writes=[r_gs])
        yield
        S.op("dve", lambda e: e.reciprocal(out=gsv(3), in_=gsv(3)), reads=[r_gs], writes=[r_gs])
        yield
        bkg, rbg = nb()
        S.op("pe", lambda e: e.matmul(bkg[0:64, 0:8], lhsT=m_le[:, :], rhs=gsv(2), start=True, stop=True),
             reads=[r_gs, r_mle], writes=[rbg])
        yield
        S.op("pe", lambda e: e.matmul(bkg[0:64, 8:16], lhsT=ones[0:64, 0:64], rhs=gsv(2), start=True, stop=True),
             reads=[r_gs, r_ones], writes=[rbg], acc=True)
        yield
        S.op("pe", lambda e: e.matmul(bkg[:, 16:24], lhsT=ones[0:64, :], rhs=gsv(2), start=True, stop=True),
             reads=[r_gs, r_ones], writes=[rbg], acc=True)
        yield
        S.op("act", lambda e: e.copy(out=gs[:, 4:6, :].rearrange("p a b -> p (a b)"), in_=bkg[0:64, 0:16]),
             reads=[rbg], writes=[r_gs])
        yield
        ld128, r_ld = stat.next()
        S.op("act", lambda e: e.activation(out=ld128[:, 0:8], in_=bkg[:, 16:24], func=AF.Exp), reads=[rbg], writes=[r_ld])
        yield
        S.op("act", lambda e: e.activation(out=gsv(6), in_=gsv(4), func=AF.Exp), reads=[r_gs], writes=[r_gs])
        yield
        S.op("dve", lambda e: e.tensor_tensor(out=gsv(7), in0=gsv(5), in1=gsv(4), op=ALU.subtract),
             reads=[r_gs], writes=[r_gs])
        yield
        S.op("act", lambda e: e.activation(out=gsv(7), in_=gsv(7), func=AF.Exp), reads=[r_gs], writes=[r_gs])
        yield
        S.op("dve", lambda e: e.tensor_tensor(out=gsv(8), in0=gsv(1), in1=gsv(6), op=ALU.mult), reads=[r_gs], writes=[r_gs])
        yield
        S.op("dve", lambda e: e.tensor_tensor(out=gsv(9), in0=gsv(1), in1=gsv(7), op=ALU.mult), reads=[r_gs], writes=[r_gs])
        yield
        if full:
            S.op("dve", lambda e: e.tensor_scalar(out=gsv(10), in0=gsv(0), scalar1=128.0 ** -0.5, scalar2=None, op0=ALU.mult),
                 reads=[r_gs], writes=[r_gs])
            S.op("dve", lambda e: e.tensor_tensor(out=gsv(11), in0=gsv(10), in1=gsv(6), op=ALU.mult), reads=[r_gs], writes=[r_gs])
        yield
        state["ld"] = (ld128, r_ld)

    def gdn_prep2(t, full):
        for c in range(2):
            bk, rb = nb()
            for h in range(4):
                S.op("pe", lambda e, c=c, h=h: e.transpose(bk[0:64, h * 128:(h + 1) * 128],
                                                           qkvT[:, 4 + h, c * 64:(c + 1) * 64], idf[:, :]),
                     reads=[r_qkvT, r_idf], writes=[rb], acc=(h > 0))
            bk3 = bk[0:64, :].rearrange("p (h d) -> p h d", h=4)
            us = slice(c * 4, (c + 1) * 4)
            S.op("dve", lambda e, bk3=bk3, us=us: e.tensor_tensor(out=kt_bf[:, us, :], in0=bk3, in1=bc_u(gs[:, 8, us], 128),
                                                                  op=ALU.mult), reads=[rb, r_gs], writes=[r_kt], acc=(c > 0))
            S.op("dve", lambda e, bk3=bk3, us=us: e.tensor_tensor(out=kd_bf[:, us, :], in0=bk3, in1=bc_u(gs[:, 9, us], 128),
                                                                  op=ALU.mult), reads=[rb, r_gs], writes=[r_kd], acc=(c > 0))
            S.op("dve", lambda e, bk3=bk3, us=us: e.tensor_tensor(out=kn_bf[:, us, :], in0=bk3, in1=bc_u(gs[:, 1, us], 128),
                                                                  op=ALU.mult), reads=[rb, r_gs], writes=[r_kn], acc=(c > 0))
        yield
        for c in range(2):
            bk, rb = nb()
            for h in range(4):
                S.op("pe", lambda e, c=c, h=h: e.transpose(bk[0:64, h * 128:(h + 1) * 128],
                                                           qkvT[:, 8 + h, c * 64:(c + 1) * 64], idf[:, :]),
                     reads=[r_qkvT, r_idf], writes=[rb], acc=(h > 0))
            S.op("act", lambda e, c=c, bk=bk: e.copy(out=v_bf[:, c * 4:(c + 1) * 4, :],
                                                     in_=bk[0:64, :].rearrange("p (h d) -> p h d", h=4)),
                 reads=[rb], writes=[r_vbf], acc=(c > 0))
        yield
        bk, rb = nb()
        bkb = bk[:, :].bitcast(BF16)
        for u in range(8):
            S.op("pe", lambda e, u=u: e.transpose(bkb[:, u * 64:(u + 1) * 64], kn_bf[:, u, :], idb[0:64, 0:64]),
                 reads=[r_kn, r_idb], writes=[rb], acc=(u > 0))
        yield
        S.op("act", lambda e: e.copy(out=knT[:, :, :], in_=bkb[:, 0:512].rearrange("p (u t) -> p u t", u=8)),
             reads=[rb], writes=[r_knT])
        yield
        if full:
            S.op("dve", lambda e: e.tensor_copy(out=qT_bf[:, :, :], in_=qkvT[:, 0:4, :]), reads=[r_qkvT], writes=[r_qTbf])
        yield
        S.op("dve", lambda e: e.tensor_tensor(out=gM[:, :, :], in0=m_gt[:, :].unsqueeze(1).broadcast_to([64, 8, 64]),
                                              in1=bc_u(gsv(2), 64), op=ALU.mult), reads=[r_mgt, r_gs], writes=[r_gM])
        yield
        bk, rb = nb()
        for u in range(8):
            S.op("pe", lambda e, u=u: e.matmul(bk[0:64, u * 64:(u + 1) * 64], lhsT=gM[:, u, :], rhs=m_le[:, :],
                                               start=True, stop=True), reads=[r_gM, r_mle], writes=[rb], acc=(u > 0))
        yield
        S.op("act", lambda e: e.activation(out=Emat[:, :, :], in_=bk[0:64, :].rearrange("p (u i) -> p u i", u=8),
                                           func=AF.Exp), reads=[rb], writes=[r_E])
        yield
        S.op("dve", lambda e: e.tensor_tensor(out=mb[:, :, :], in0=m_lt[:, :].unsqueeze(1).broadcast_to([64, 8, 64]),
                                              in1=bc_u(gsv(3), 64), op=ALU.mult), reads=[r_mlt, r_gs], writes=[r_mb])
        yield
        S.op("dve", lambda e: e.tensor_tensor(out=Es[:, :, :], in0=Emat[:, :, :], in1=mb[:, :, :], op=ALU.mult),
             reads=[r_E, r_mb], writes=[r_Es])
        yield
        if full:
            S.op("dve", lambda e: e.tensor_tensor(out=Ei[:, :, :], in0=Emat[:, :, :],
                                                  in1=m_le[:, :].unsqueeze(1).broadcast_to([64, 8, 64]), op=ALU.mult),
                 reads=[r_E, r_mle], writes=[r_Ei])
        yield
        bk, rb = nb()
        for u in range(8):
            S.op("pe", lambda e, u=u: e.matmul(bk[0:64, u * 64:(u + 1) * 64], lhsT=knT[:, u, :], rhs=knT[:, u, :],
                                               start=True, stop=True), reads=[r_knT], writes=[rb], acc=(u > 0))
        yield
        B0, rB0 = Bp[0]
        S.op("dve", lambda e: e.tensor_tensor(out=B0[:, :, :], in0=bk[0:64, :].rearrange("p (u i) -> p u i", u=8),
                                              in1=Es[:, :, :], op=ALU.mult), reads=[rb, r_Es], writes=[rB0])
        yield
        if full:
            bk, rb = nb()
            for u in range(8):
                c, h = u // 4, u % 4
                S.op("pe", lambda e, u=u, c=c, h=h, bk=bk: e.matmul(bk[0:64, u * 64:(u + 1) * 64], lhsT=knT[:, u, :],
                                                                    rhs=qT_bf[:, h, c * 64:(c + 1) * 64], start=True, stop=True),
                     reads=[r_knT, r_qTbf], writes=[rb], acc=(u > 0))
            S.op("dve", lambda e, bk=bk: e.tensor_tensor(out=qkT[:, :, :], in0=bk[0:64, :].rearrange("p (u i) -> p u i", u=8),
                                                         in1=Ei[:, :, :], op=ALU.mult), reads=[rb, r_Ei], writes=[r_qkT])
        yield
        bk, rb = nb()
        for u in range(8):
            S.op("pe", lambda e, u=u: e.transpose(bk[0:64, u * 64:(u + 1) * 64], B0[:, u, :], idf[0:64, 0:64]),
                 reads=[rB0, r_idf], writes=[rb], acc=(u > 0))
        yield
        A0, rA0 = Ap[0]
        S.op("act", lambda e: e.copy(out=A0[:, :, :], in_=bk[0:64, :].rearrange("p (u i) -> p u i", u=8)),
             reads=[rb], writes=[rA0])
        yield
        S.op("dve", lambda e: e.scalar_tensor_tensor(out=Zm[:, :, :], in0=A0[:, :, :], scalar=-1.0,
                                                     in1=idf[0:64, 0:64].unsqueeze(1).broadcast_to([64, 8, 64]),
                                                     op0=ALU.mult, op1=ALU.add), reads=[rA0, r_idf], writes=[r_Z])
        yield
        cur = 0
        for lvl in range(5):
            Bc, rBc = Bp[cur]
            Ac, rAc = Ap[cur]
            Bn, rBn = Bp[1 - cur]
            An, rAn = Ap[1 - cur]
            bkB, rbB = nb()
            for u in range(8):
                S.op("pe", lambda e, u=u: e.matmul(bkB[0:64, u * 64:(u + 1) * 64], lhsT=Ac[:, u, :], rhs=Bc[:, u, :],
                                                   start=True, stop=True), reads=[rAc, rBc], writes=[rbB], acc=(u > 0))
            if lvl < 4:
                bkA, rbA = nb()
                for u in range(8):
                    S.op("pe", lambda e, u=u: e.matmul(bkA[0:64, u * 64:(u + 1) * 64], lhsT=Bc[:, u, :], rhs=Ac[:, u, :],
                                                       start=True, stop=True), reads=[rAc, rBc], writes=[rbA], acc=(u > 0))
            S.op("act", lambda e: e.copy(out=Bn[:, :, :], in_=bkB[0:64, :].rearrange("p (u i) -> p u i", u=8)),
                 reads=[rbB], writes=[rBn])
            if lvl < 4:
                S.op("dve", lambda e: e.tensor_copy(out=An[:, :, :], in_=bkA[0:64, :].rearrange("p (u i) -> p u i", u=8)),
                     reads=[rbA], writes=[rAn])
            bkZ, rbZ = nb()
            for u in range(8):
                S.op("pe", lambda e, u=u: e.matmul(bkZ[0:64, u * 64:(u + 1) * 64], lhsT=Bn[:, u, :], rhs=Zm[:, u, :],
                                                   start=True, stop=True), reads=[rBn, r_Z], writes=[rbZ], acc=(u > 0))
            S.op("dve", lambda e: e.tensor_tensor(out=Zm[:, :, :], in0=bkZ[0:64, :].rearrange("p (u i) -> p u i", u=8),
                                                  in1=Zm[:, :, :], op=ALU.add), reads=[rbZ, r_Z], writes=[r_Z])
            cur = 1 - cur
        yield
        S.op("dve", lambda e: e.tensor_tensor(out=dgb[:, :, :], in0=idf[0:64, 0:64].unsqueeze(1).broadcast_to([64, 8, 64]),
                                              in1=bc_u(gsv(3), 64), op=ALU.mult), reads=[r_idf, r_gs], writes=[r_dgb])
        yield
        bk, rb = nb()
        for u in range(8):
            S.op("pe", lambda e, u=u: e.matmul(bk[0:64, u * 64:(u + 1) * 64], lhsT=Zm[:, u, :], rhs=dgb[:, u, :],
                                               start=True, stop=True), reads=[r_Z, r_dgb], writes=[rb], acc=(u > 0))
        yield
        S.op("act", lambda e: e.copy(out=NTb[:, :, :], in_=bk[0:64, :].rearrange("p (u i) -> p u i", u=8)),
             reads=[rb], writes=[r_NTb])
        yield
        for c in range(2):
            bk, rb = nb()
            for h in range(4):
                u = c * 4 + h
                S.op("pe", lambda e, u=u, h=h: e.matmul(bk[0:64, h * 128:(h + 1) * 128], lhsT=NTb[:, u, :], rhs=v_bf[:, u, :],
                                                        start=True, stop=True), reads=[r_NTb, r_vbf], writes=[rb], acc=(h > 0))
            S.op("act", lambda e, c=c, bk=bk: e.copy(out=ub[:, c * 4:(c + 1) * 4, :],
                                                     in_=bk[0:64, :].rearrange("p (h d) -> p h d", h=4)),
                 reads=[rb], writes=[r_ub], acc=(c > 0))
        yield
        bk, rb = nb()
        for u in range(8):
            S.op("pe", lambda e, u=u: e.matmul(bk[:, u * 64:(u + 1) * 64], lhsT=kt_bf[:, u, :], rhs=NTb[:, u, :],
                                               start=True, stop=True), reads=[r_kt, r_NTb], writes=[rb], acc=(u > 0))
        yield
        S.op("dve", lambda e: e.tensor_copy(out=wT_bf[:, :, :], in_=bk[:, :].rearrange("p (u i) -> p u i", u=8)),
             reads=[rb], writes=[r_wT])
        yield

    def gdn_scan(zab, r_zab, mixT, r_mixT, t, full, aug, BB=None):
        if BB is None:
            BB = dict(wT=(wT_bf_g, r_wT_g), qT=(qT_bf_g, r_qTbf_g), kd=(kd_bf_g, r_kd_g), ub=(ub_g, r_ub_g),
                      qkT=(qkT_g, r_qkT_g), gs=(gs_g, r_gs_g), ld=state["ld"])
        wT_bf, r_wT = BB["wT"]
        qT_bf, r_qTbf = BB["qT"]
        kd_bf, r_kd = BB["kd"]
        ub, r_ub = BB["ub"]
        qkT, r_qkT = BB["qkT"]
        gs, r_gs = BB["gs"]
        ld128, r_ld = BB["ld"]
        for c in range(2):
            us = slice(c * 4, (c + 1) * 4)
            if aug:
                bkas = [nb(), nb()]
                for h in range(4):
                    bk_, rb_ = bkas[h // 2]
                    S.op("pe", lambda e, h=h, c=c, bk_=bk_: e.matmul(bk_[0:64, (h % 2) * 256:(h % 2) * 256 + 256],
                                                                     lhsT=wT_bf[:, c * 4 + h, :], rhs=S_bf[:, h, 0:256],
                                                                     start=True, stop=True),
                         reads=[r_wT, r_Sbf], writes=[rb_], acc=(h % 2 > 0))
                yield
                for pr in range(2):
                    bk_, rb_ = bkas[pr]
                    b3 = bk_[0:64, :].rearrange("p (h w) -> p h w", h=2)
                    S.op("dve", lambda e, pr=pr, b3=b3, c=c: e.tensor_tensor(
                        out=u_bf[:, pr * 2:pr * 2 + 2, 0:128], in0=ub[:, c * 4 + pr * 2:c * 4 + pr * 2 + 2, :],
                        in1=b3[:, :, 0:128], op=ALU.subtract), reads=[r_ub, rb_], writes=[r_u], acc=(pr > 0))
                    S.op("dve", lambda e, pr=pr, b3=b3: e.tensor_scalar(
                        out=u_bf[:, pr * 2:pr * 2 + 2, 128:256], in0=b3[:, :, 128:256], scalar1=-1.0, scalar2=None,
                        op0=ALU.mult), reads=[rb_], writes=[r_u], acc=True)
                yield
                bkss = [nb(), nb()]
                for h in range(4):
                    bk_, rb_ = bkss[h // 2]
                    S.op("pe", lambda e, h=h, c=c, bk_=bk_: e.matmul(bk_[:, (h % 2) * 256:(h % 2) * 256 + 256],
                                                                     lhsT=kd_bf[:, c * 4 + h, :], rhs=u_bf[:, h, 0:256],
                                                                     start=True, stop=True),
                         reads=[r_kd, r_u], writes=[rb_], acc=(h % 2 > 0))
                yield
                for h in range(4):
                    bk_, rb_ = bkss[h // 2]
                    S.op("dve", lambda e, h=h, c=c, bk_=bk_: e.scalar_tensor_tensor(
                        out=Sst[:, h, 0:256], in0=Sst[:, h, 0:256], scalar=ld128[:, c * 4 + h:c * 4 + h + 1],
                        in1=bk_[:, (h % 2) * 256:(h % 2) * 256 + 256], op0=ALU.mult, op1=ALU.add),
                        reads=[r_S[h], r_ld, rb_], writes=[r_S[h]])
                yield
                S.op("act", lambda e: e.copy(out=S_bf[:, :, :], in_=Sst[:, :, :]), reads=r_S, writes=[r_Sbf])
                yield
                continue
            bka, rba = nb()
            for h in range(4):
                S.op("pe", lambda e, h=h, c=c: e.matmul(bka[0:64, h * 128:(h + 1) * 128], lhsT=wT_bf[:, c * 4 + h, :],
                                                        rhs=S_bf[:, h, 0:128], start=True, stop=True),
                     reads=[r_wT, r_Sbf], writes=[rba], acc=(h > 0))
            yield
            if full:
                bko, rbo = nb()
                for h in range(4):
                    S.op("pe", lambda e, h=h, c=c: e.matmul(bko[0:64, h * 128:(h + 1) * 128], lhsT=qT_bf[:, h, c * 64:(c + 1) * 64],
                                                            rhs=S_bf[:, h, 0:128], start=True, stop=True),
                         reads=[r_qTbf, r_Sbf], writes=[rbo], acc=(h > 0))
            yield
            S.op("dve", lambda e, us=us: e.tensor_tensor(out=u_bf[:, :, 0:128], in0=ub[:, us, :],
                                                         in1=bka[0:64, :].rearrange("p (h d) -> p h d", h=4), op=ALU.subtract),
                 reads=[r_ub, rba], writes=[r_u])
            yield
            if full:
                bk2, rb2 = nb()
                for h in range(4):
                    S.op("pe", lambda e, h=h, c=c: e.matmul(bk2[0:64, h * 128:(h + 1) * 128], lhsT=qkT[:, c * 4 + h, :],
                                                            rhs=u_bf[:, h, 0:128], start=True, stop=True),
                         reads=[r_qkT, r_u], writes=[rb2], acc=(h > 0))
            yield
            bks, rbs = nb()
            for h in range(4):
                S.op("pe", lambda e, h=h, c=c: e.matmul(bks[:, h * 128:(h + 1) * 128], lhsT=kd_bf[:, c * 4 + h, :],
                                                        rhs=u_bf[:, h, 0:128], start=True, stop=True),
                     reads=[r_kd, r_u], writes=[rbs], acc=(h > 0))
            yield
            for h in range(4):
                S.op("dve", lambda e, h=h, c=c: e.scalar_tensor_tensor(
                    out=Sst[:, h, 0:128], in0=Sst[:, h, 0:128], scalar=ld128[:, c * 4 + h:c * 4 + h + 1],
                    in1=bks[:, h * 128:(h + 1) * 128], op0=ALU.mult, op1=ALU.add),
                    reads=[r_S[h], r_ld, rbs], writes=[r_S[h]])
            yield
            S.op("act", lambda e: e.copy(out=S_bf[:, :, 0:128], in_=Sst[:, :, 0:128]), reads=r_S, writes=[r_Sbf])
            yield
            if not full:
                continue
            S.op("dve", lambda e, us=us: e.tensor_tensor(out=o1[:, :, :], in0=bko[0:64, :].rearrange("p (h d) -> p h d", h=4),
                                                         in1=bc_u(gs[:, 11, us], 128), op=ALU.mult),
                 reads=[rbo, r_gs], writes=[r_o1])
            yield
            S.op("dve", lambda e, us=us: e.tensor_tensor(out=o2[:, :, :], in0=bk2[0:64, :].rearrange("p (h d) -> p h d", h=4),
                                                         in1=bc_u(gs[:, 10, us], 128), op=ALU.mult),
                 reads=[rb2, r_gs], writes=[r_o2])
            yield
            S.op("dve", lambda e: e.tensor_tensor(out=o1[:, :, :], in0=o1[:, :, :], in1=o2[:, :, :], op=ALU.add),
                 reads=[r_o1, r_o2], writes=[r_o1])
            yield
            if "gdn_o" in dbg and t == 0 and c == 0:
                dbg_out("gdn_o", o1[:, :, :], r_o1, [64, 4, 128])
            S.op("dve", lambda e: e.tensor_tensor(out=osq[:, :, :], in0=o1[:, :, :], in1=o1[:, :, :], op=ALU.mult),
                 reads=[r_o1], writes=[r_osq])
            yield
            st, rs = stat.next()
            S.op("dve", lambda e: e.tensor_reduce(out=st[0:64, 0:4], in_=osq[:, :, :], axis=AX.X, op=ALU.add),
                 reads=[r_osq], writes=[rs])
            yield
            S.op("act", lambda e: e.activation(out=st[0:64, 4:8], in_=st[0:64, 0:4], func=AF.Ln, bias=epsT[0:64, :],
                                               scale=1.0 / 128), reads=[rs, r_eps], writes=[rs])
            yield
            S.op("act", lambda e: e.activation(out=st[0:64, 8:12], in_=st[0:64, 4:8], func=AF.Exp, scale=-0.5), reads=[rs], writes=[rs])
            yield
            S.op("act", lambda e, c=c: e.activation(out=sz[:, :, :].rearrange("p h d -> p (h d)"), in_=zab[:, c, 0:512],
                                                    func=AF.Silu), reads=[r_zab], writes=[r_sz])
            yield
            S.op("dve", lambda e: e.tensor_tensor(out=sz[:, :, :], in0=sz[:, :, :],
                                                  in1=g_gdn[:, :].unsqueeze(1).broadcast_to([64, 4, 128]), op=ALU.mult),
                 reads=[r_sz, r_ggdn], writes=[r_sz])
            yield
            S.op("dve", lambda e: e.tensor_tensor(out=o1[:, :, :], in0=o1[:, :, :], in1=bc_u(st[0:64, 8:12], 128),
                                                  op=ALU.mult), reads=[r_o1, rs], writes=[r_o1])
            yield
            S.op("dve", lambda e: e.tensor_tensor(out=y_bf[:, :, :], in0=o1[:, :, :], in1=sz[:, :, :], op=ALU.mult),
                 reads=[r_o1, r_sz], writes=[r_ybf])
            yield
            bk, rb = nb()
            bkb = bk[:, :].bitcast(BF16)
            for h in range(4):
                S.op("pe", lambda e, h=h: e.transpose(bkb[:, h * 64:(h + 1) * 64], y_bf[:, h, :], idb[0:64, 0:64]),
                     reads=[r_ybf, r_idb], writes=[rb], acc=(h > 0))
            yield
            S.op("act", lambda e, c=c, bkb=bkb: e.copy(out=mixT[:, 4:8, c * 64:(c + 1) * 64],
                                                       in_=bkb[:, 0:256].rearrange("p (h t) -> p h t", h=4)),
                 reads=[rb], writes=[r_mixT], acc=True)
            yield

    def drive(*gens, weights=None, pools=None):
        gl = [(g, (weights[i] if weights else 1), (pools[i] if pools else "all")) for i, g in enumerate(gens) if g is not None]
        while gl:
            for item in list(gl):
                g, w, pl = item
                bstate["pool"] = pl
                try:
                    for _ in range(w):
                        next(g)
                except StopIteration:
                    gl.remove(item)
        bstate["pool"] = "all"

    ctx = {}

    def pre_front(t, light=False, halo=False, do_gdn=True):
        xt, rx = load_x(t)
        hT, r_hT = hTring.next()
        norm_transpose(xt[:, :], rx, g_pre, r_gpre, hT[:, :, :], r_hT)
        yield
        xc, r_xc = xc_ring.next()
        if light:
            inproj_feat(hT, r_hT, xc, r_xc, 125, 3, 128, groups=(1, 2))
            state["prev_xc"] = (xc, r_xc)
            return
        if state["prev_xc"] is not None:
            pxc, r_pxc = state["prev_xc"]
            S.op("dve", lambda e: e.tensor_copy(out=xc[:, 4:12, 0:3], in_=pxc[:, 4:12, 128:131]), reads=[r_pxc], writes=[r_xc])
        else:
            S.op("dve", lambda e: e.memset(xc[:, :, 0:3], 0.0), writes=[r_xc])
        if do_gdn:
            inproj_feat(hT, r_hT, xc, r_xc, 0, 128, 3, groups=(1,))
            yield
            inproj_feat(hT, r_hT, xc, r_xc, 0, 128, 3, groups=(2,))
            yield
        else:
            inproj_feat(hT, r_hT, xc, r_xc, 125, 3, 128, groups=(1, 2))
        if halo:
            inproj_feat(hT, r_hT, xc, r_xc, 125, 3, 128, groups=(0,))
            patt, r_patt = patt_ring.next()
            bk, rb = inproj_tok(hT, r_hT, 512, 256, 0, 128)
            S.op("act", lambda e: e.copy(out=patt[:, 512:768], in_=bk[:, 0:256]), reads=[rb], writes=[r_patt])
            state["prev_kv"] = kv_prep(patt, r_patt)
            yield
        state["prev_xc"] = (xc, r_xc)
        if do_gdn:
            zab, r_zab = zab_ring.next()
            for c in range(2):
                bk, rb = inproj_tok(hT, r_hT, 2816, 8, c * 64, 64)
                S.op("dve", lambda e, c=c, bk=bk: e.tensor_copy(out=zab[:, c, 512:520], in_=bk[0:64, 0:8]), reads=[rb],
                     writes=[r_zab], acc=True)
            ctx[("p", t)] = (xc, r_xc, zab, r_zab)
        yield

    def pre_back(t, aug):
        xc, r_xc, zab, r_zab = ctx.pop(("p", t))
        yield from gdn(xc, r_xc, zab, r_zab, None, None, t, full=False, aug=aug)

    def wout_epi(t, xt, rx, mixT, r_mixT):
        bk0, rb0 = nb()
        bk1, rb1 = nb()
        for n, (bk, rb) in enumerate(((bk0, rb0), (bk1, rb1))):
            for kc in range(KC):
                S.op("pe", lambda e, kc=kc, n=n, bk=bk: e.matmul(bk[:, :], lhsT=mixT[:, kc, :],
                                                                 rhs=w_out[:, kc, n * 512:(n + 1) * 512],
                                                                 start=(kc == 0), stop=(kc == KC - 1)),
                     reads=[r_mixT, r_wout], writes=[rb], acc=(kc > 0))
            yield
        x1, r_x1 = x1_ring.next()
        epilogue(bk0, rb0, bk1, rb1, g_post, r_gpost, xt[:, :], rx, x1[:, :], r_x1)
        ryp = state.setdefault("ry_pool", {})
        if (t % 4) not in ryp:
            ryp[t % 4] = Res("y%d" % (t % 4))
        ryt = ryp[t % 4]
        S.dma("pool", y_d[t * 128:(t + 1) * 128, :], x1[:, :], r_x1, reads=[r_x1], writes=[ryt], acc=True)
        state.setdefault("ry", {})[t] = ryt
        yield

    def pconv_out(hT, r_hT):
        pc = acc_t[:, :, :].rearrange("p m t -> p (m t)")
        r_pc = Res("pc")
        for n3 in range(3):
            bk, rb = inproj_tok(hT, r_hT, 768 + n3 * 512, 512, 0, 128)
            S.op("act", lambda e, n3=n3, bk=bk: e.copy(out=pc[:, n3 * 512:(n3 + 1) * 512], in_=bk[:, :]),
                 reads=[rb], writes=[r_pc, r_qkvT] + r_acc, acc=(n3 > 0))
        S.dma("sp", pconv_d[:, :], pc[125:128, 0:1536], r_pc, reads=[r_pc], final=True)

    def front_a(t):
        xt, rx = load_x(t)
        hT, r_hT = hTring.next()
        norm_transpose(xt[:, :], rx, g_pre, r_gpre, hT[:, :, :], r_hT)
        ctx[("a", t)] = (xt, rx, hT, r_hT)
        yield

    def own_front(t):
        if ("a", t) not in ctx:
            yield from front_a(t)
        xt, rx, hT, r_hT = ctx.pop(("a", t))
        patt, r_patt = patt_ring.next()
        bk, rb = inproj_tok(hT, r_hT, 0, 512, 0, 128)
        S.op("act", lambda e: e.copy(out=patt[:, 0:512], in_=bk[:, :]), reads=[rb], writes=[r_patt])
        yield
        bk, rb = inproj_tok(hT, r_hT, 512, 256, 0, 128)
        S.op("dve", lambda e: e.tensor_copy(out=patt[:, 512:768], in_=bk[:, 0:256]), reads=[rb], writes=[r_patt], acc=True)
        yield
        zab, r_zab = zab_ring.next()
        for c in range(2):
            bk, rb = inproj_tok(hT, r_hT, 2304, 512, c * 64, 64)
            S.op("act", lambda e, c=c, bk=bk: e.copy(out=zab[:, c, 0:512], in_=bk[0:64, :]), reads=[rb], writes=[r_zab], acc=True)
            bk, rb = inproj_tok(hT, r_hT, 2816, 8, c * 64, 64)
            S.op("dve", lambda e, c=c, bk=bk: e.tensor_copy(out=zab[:, c, 512:520], in_=bk[0:64, 0:8]), reads=[rb],
                 writes=[r_zab], acc=True)
            yield
        xc, r_xc = xc_ring.next()
        pxc, r_pxc = state["prev_xc"]
        S.op("dve", lambda e: e.tensor_copy(out=xc[:, :, 0:3], in_=pxc[:, :, 128:131]), reads=[r_pxc], writes=[r_xc])
        for grp in range(3):
            inproj_feat(hT, r_hT, xc, r_xc, 0, 128, 3, groups=(grp,))
            yield
        state["prev_xc"] = (xc, r_xc)
        if t == 0:
            dbg_out("patt", patt[:, :], r_patt, [128, 768])
            dbg_out("zab", zab[:, :, :], r_zab, [64, 2, 520])
            dbg_out("xc", xc[:, :, :], r_xc, [128, 12, 131])
        mixT, r_mixT = mixT_ring.next()
        cur_kv = kv_prep(patt, r_patt)
        yield
        yield from attention(patt, r_patt, state["prev_kv"], cur_kv, mixT, r_mixT)
        state["prev_kv"] = cur_kv
        if t == 0:
            S.dma("sp", bias[:], cd["biasN"][:], r_bias, writes=[r_bias])
        if t == nt - 1:
            S.dma("sp", pk_d[:, :], patt[:, 512:640], r_patt, reads=[r_patt], final=True)
            S.dma("sp", pv_d[:, :], patt[:, 640:768], r_patt, reads=[r_patt], final=True)
        ctx[("o", t)] = (xt, rx, hT, r_hT, zab, r_zab, xc, r_xc, mixT, r_mixT)
        yield

    def own_back(t):
        xt, rx, hT, r_hT, zab, r_zab, xc, r_xc, mixT, r_mixT = ctx.pop(("o", t))
        yield from gdn(xc, r_xc, zab, r_zab, mixT, r_mixT, t)
        if t == 0 and "mixT" in dbg:
            d = S.dram("dbg_mixT", [128, KC, 128], BF16, "ExternalOutput")
            dbg_outs["mixT"] = d
            S.dma("sp", d[:], mixT[:, :, :], r_mixT, reads=[r_mixT], final=True)
        yield from wout_epi(t, xt, rx, mixT, r_mixT)
        if t == nt - 1:
            pconv_out(hT, r_hT)
            for h in range(4):
                S.dma("sp", pS_d[h, :, :], Sst[:, h, 0:128], r_S[h], reads=[r_S[h]], final=True)

    state["prev_kv"] = None
    state["A_closed"] = False
    if xchg:
        drive(pre_front(-1, halo=True, do_gdn=False))
        r_sp = {}
        sp_pool = {}

        def back1(t):
            xt, rx, hT, r_hT, zab, r_zab, xc, r_xc, mixT, r_mixT = ctx.pop(("o", t))
            prev_scan = state.pop("pending_scan", None)
            yield from par([gdn_prep1(xc, r_xc, zab, r_zab, t, True), prev_scan], ["P", "C"])
            yield from gdn_prep2(t, True)
            ld128, r_ld = state["ld"]
            state["pending_scan"] = gdn_scan(None, None, None, None, t, False, True,
                                             dict(wT=(wT_bf, r_wT), qT=(qT_bf, r_qTbf), kd=(kd_bf, r_kd), ub=(ub, r_ub),
                                                  qkT=(qkT, r_qkT), gs=(gs, r_gs), ld=(ld128, r_ld)))
            if (t % 4) not in sp_pool:
                sp_pool[t % 4] = Res("sp%d" % (t % 4))
            rsp = sp_pool[t % 4]
            r_sp[t] = rsp
            srcs_ = []
            for dst, src, rr in ((sp_wT[t], wT_bf[:, :, :], r_wT), (sp_qT[t], qT_bf[:, :, :], r_qTbf),
                                 (sp_kd[t], kd_bf[:, :, :], r_kd), (sp_ub[t], ub[:, :, :], r_ub),
                                 (sp_qkT[t], qkT[:, :, :], r_qkT), (sp_gs[t], gs[:, :, :], r_gs),
                                 (sp_ld[t], ld128[:, 0:8], r_ld), (sp_zab[t], zab[:, :, :], r_zab),
                                 (sp_mix[t], mixT[:, 0:4, :], r_mixT)):
                S.dma("pool", dst, src, rsp, reads=[rr], writes=[rsp], acc=True)
                srcs_.append(rr)
            for rr in srcs_:
                rr.r[id(rsp.dsem)] = (rsp.dsem, rsp.dcnt, "dma")
            if t == nt - 1:
                pconv_out(hT, r_hT)
            yield

        def seq(*gens):
            for g in gens:
                if g is not None:
                    yield from g

        drive(front_a(0))
        for t in range(nt + 1):
            fr = seq(own_front(t), front_a(t + 1) if t + 1 < nt else None) if t < nt else None
            drive(fr, back1(t - 1) if t > 0 else None, weights=P1W, pools=("F", "P"))
        drive(state.pop("pending_scan"))
        r_xsrc, r_xdst = Res("xsrc"), Res("xdst")
        S.dma("sp", xsrc_d[:, :], Sst[:, :, :].rearrange("p h w -> p (h w)"), r_xsrc, reads=r_S, writes=[r_xsrc])
        S.cc("AllGather", xsrc_d[:, :], xdst_d[:, :], r_xdst, [[0, 1, 2, 3], [4, 5, 6, 7]], reads=[r_xsrc], writes=[r_xdst])
        Gt = sc[:, 0:4, :].rearrange("p a b -> p (a b)")
        r_G = r_sc[0]
        r_G2 = r_sc[1]
        sel128, r_sel = const_tile("sel128", [128, 4], F32, sel_d[:].partition_broadcast(128), stack=A)
        cs, r_cs = sc[:, 4:6, :].rearrange("p a (h d) -> p (a h) d", h=2), r_sc[2]
        PhiT, r_PhiT = sc[:, 6:8, :].rearrange("p a (h d) -> p (a h) d", h=2), r_sc[3]
        S.op("dve", lambda e: e.memset(Sst[:, :, 0:128], 0.0), writes=r_S)
        for i in range(3):
            S.dma("sp", Gt, xdst_d[i * 128:(i + 1) * 128, :], r_G, reads=[r_xdst], writes=[r_G, r_G2])
            Gi = Gt.rearrange("p (h w) -> p h w", h=4)
            if i == 0:
                S.op("dve", lambda e, Gi=Gi: e.tensor_copy(out=cs, in_=Gi[:, :, 0:128]), reads=[r_G, r_G2], writes=[r_cs])
            else:
                bkT, rbT = nb()
                for h in range(4):
                    S.op("pe", lambda e, h=h, Gi=Gi: e.transpose(bkT[:, h * 128:(h + 1) * 128], Gi[:, h, 128:256], idf[:, :]),
                         reads=[r_G, r_G2, r_idf], writes=[rbT], acc=(h > 0))
                S.op("act", lambda e: e.copy(out=PhiT, in_=bkT[:, :].rearrange("p (h d) -> p h d", h=4)),
                     reads=[rbT], writes=[r_PhiT])
                bkM, rbM = nb()
                for h in range(4):
                    S.op("pe", lambda e, h=h: e.matmul(bkM[:, h * 128:(h + 1) * 128], lhsT=PhiT[:, h, :], rhs=cs[:, h, :],
                                                       start=True, stop=True), reads=[r_PhiT, r_cs], writes=[rbM], acc=(h > 0))
                S.op("dve", lambda e, Gi=Gi: e.tensor_tensor(out=cs, in0=bkM[:, :].rearrange("p (h d) -> p h d", h=4),
                                                            in1=Gi[:, :, 0:128], op=ALU.add), reads=[rbM, r_G, r_G2], writes=[r_cs])
            S.op("dve", lambda e, i=i: e.scalar_tensor_tensor(out=Sst[:, :, 0:128], in0=cs, scalar=sel128[:, i + 1:i + 2],
                                                              in1=Sst[:, :, 0:128], op0=ALU.mult, op1=ALU.add),
                 reads=[r_cs, r_sel] + r_S, writes=r_S)
        S.op("act", lambda e: e.copy(out=S_bf[:, :, 0:128], in_=Sst[:, :, 0:128]), reads=r_S, writes=[r_Sbf])
        S.barrier()
        A.close()
        state["A_closed"] = True
        A2 = contextlib.ExitStack()
        x1_ring = Ring(S, "x1t", 1, [128, D], F32, A2)
        wT2 = Ring(S, "wT2", 2, [128, 8, 64], BF16, A2)
        qT2 = Ring(S, "qT2", 2, [128, 4, 128], BF16, A2)
        kd2 = Ring(S, "kd2", 2, [64, 8, 128], BF16, A2)
        ub2 = Ring(S, "ub2", 2, [64, 8, 128], F32, A2)
        qkT2 = Ring(S, "qkT2", 2, [64, 8, 64], BF16, A2)
        gs2 = Ring(S, "gs2", 2, [64, 16, 8], F32, A2)
        ld2 = Ring(S, "ld2", 2, [128, 8], F32, A2)
        zab2 = Ring(S, "zab2", 2, [64, 2, 520], F32, A2)

        def front2(t):
            xt, rx = load_x(t)
            BB = {}
            for key, ring, src in (("wT", wT2, sp_wT), ("qT", qT2, sp_qT), ("kd", kd2, sp_kd), ("ub", ub2, sp_ub),
                                   ("qkT", qkT2, sp_qkT), ("gs", gs2, sp_gs), ("ld", ld2, sp_ld), ("zab", zab2, sp_zab)):
                tl, rr = ring.next()
                S.dma("sp", tl[:], src[t], rr, reads=[r_sp[t]], writes=[rr])
                BB[key] = (tl, rr)
            mixT, r_mixT = mixT_ring.next()
            S.dma("sp", mixT[:, 0:4, :], sp_mix[t], r_mixT, reads=[r_sp[t]], writes=[r_mixT])
            ctx[("2", t)] = (xt, rx, BB, mixT, r_mixT)
            yield

        def chain2(t, c):
            xt, rx, BB, mixT, r_mixT = ctx[("2", t)]
            wT_bf, r_wT = BB["wT"]
            qT_bf, r_qTbf = BB["qT"]
            kd_bf, r_kd = BB["kd"]
            ub, r_ub = BB["ub"]
            qkT, r_qkT = BB["qkT"]
            ld128, r_ld = BB["ld"]
            us = slice(c * 4, (c + 1) * 4)
            bka, rba = nb()
            for h in range(4):
                S.op("pe", lambda e, h=h: e.matmul(bka[0:64, h * 128:(h + 1) * 128], lhsT=wT_bf[:, c * 4 + h, :],
                                                   rhs=S_bf[:, h, 0:128], start=True, stop=True),
                     reads=[r_wT, r_Sbf], writes=[rba], acc=(h > 0))
            yield
            bko, rbo = nb("C2")
            for h in range(4):
                S.op("pe", lambda e, h=h: e.matmul(bko[0:64, h * 128:(h + 1) * 128], lhsT=qT_bf[:, h, c * 64:(c + 1) * 64],
                                                   rhs=S_bf[:, h, 0:128], start=True, stop=True),
                     reads=[r_qTbf, r_Sbf], writes=[rbo], acc=(h > 0))
            yield
            S.op("dve", lambda e: e.tensor_tensor(out=u_bf[:, :, 0:128], in0=ub[:, us, :],
                                                  in1=bka[0:64, :].rearrange("p (h d) -> p h d", h=4), op=ALU.subtract),
                 reads=[r_ub, rba], writes=[r_u])
            yield
            bks, rbs = nb()
            for h in range(4):
                S.op("pe", lambda e, h=h: e.matmul(bks[:, h * 128:(h + 1) * 128], lhsT=kd_bf[:, c * 4 + h, :],
                                                   rhs=u_bf[:, h, 0:128], start=True, stop=True),
                     reads=[r_kd, r_u], writes=[rbs], acc=(h > 0))
            yield
            bk2, rb2 = nb("C2")
            for h in range(4):
                S.op("pe", lambda e, h=h: e.matmul(bk2[0:64, h * 128:(h + 1) * 128], lhsT=qkT[:, c * 4 + h, :],
                                                   rhs=u_bf[:, h, 0:128], start=True, stop=True),
                     reads=[r_qkT, r_u], writes=[rb2], acc=(h > 0))
            yield
            for h in range(4):
                S.op("dve", lambda e, h=h: e.scalar_tensor_tensor(
                    out=Sst[:, h, 0:128], in0=Sst[:, h, 0:128], scalar=ld128[:, c * 4 + h:c * 4 + h + 1],
                    in1=bks[:, h * 128:(h + 1) * 128], op0=ALU.mult, op1=ALU.add),
                    reads=[r_S[h], r_ld, rbs], writes=[r_S[h]])
                yield
            S.op("act", lambda e: e.copy(out=S_bf[:, :, 0:128], in_=Sst[:, :, 0:128]), reads=r_S, writes=[r_Sbf])
            ctx[("c", t, c)] = (bko, rbo, bk2, rb2)
            yield

        def outs2(t, c):
            xt, rx, BB, mixT, r_mixT = ctx[("2", t)]
            gs, r_gs = BB["gs"]
            zab, r_zab = BB["zab"]
            bko, rbo, bk2, rb2 = ctx.pop(("c", t, c))
            us = slice(c * 4, (c + 1) * 4)
            S.op("dve", lambda e: e.tensor_tensor(out=o1[:, :, :], in0=bko[0:64, :].rearrange("p (h d) -> p h d", h=4),
                                                  in1=bc_u(gs[:, 11, us], 128), op=ALU.mult),
                 reads=[rbo, r_gs], writes=[r_o1])
            yield
            S.op("dve", lambda e: e.tensor_tensor(out=o2[:, :, :], in0=bk2[0:64, :].rearrange("p (h d) -> p h d", h=4),
                                                  in1=bc_u(gs[:, 10, us], 128), op=ALU.mult),
                 reads=[rb2, r_gs], writes=[r_o2])
            yield
            S.op("act", lambda e: e.activation(out=sz[:, :, :].rearrange("p h d -> p (h d)"), in_=zab[:, c, 0:512],
                                               func=AF.Silu), reads=[r_zab], writes=[r_sz])
            yield
            S.op("dve", lambda e: e.tensor_tensor(out=o1[:, :, :], in0=o1[:, :, :], in1=o2[:, :, :], op=ALU.add),
                 reads=[r_o1, r_o2], writes=[r_o1])
            yield
            S.op("dve", lambda e: e.tensor_tensor(out=osq[:, :, :], in0=o1[:, :, :], in1=o1[:, :, :], op=ALU.mult),
                 reads=[r_o1], writes=[r_osq])
            yield
            st, rs = stat.next()
            S.op("dve", lambda e: e.tensor_reduce(out=st[0:64, 0:4], in_=osq[:, :, :], axis=AX.X, op=ALU.add),
                 reads=[r_osq], writes=[rs])
            yield
            S.op("act", lambda e: e.activation(out=st[0:64, 4:8], in_=st[0:64, 0:4], func=AF.Ln, bias=epsT[0:64, :],
                                               scale=1.0 / 128), reads=[rs, r_eps], writes=[rs])
            yield
            S.op("act", lambda e: e.activation(out=st[0:64, 8:12], in_=st[0:64, 4:8], func=AF.Exp, scale=-0.5), reads=[rs], writes=[rs])
            yield
            S.op("dve", lambda e: e.tensor_tensor(out=sz[:, :, :], in0=sz[:, :, :],
                                                  in1=g_gdn[:, :].unsqueeze(1).broadcast_to([64, 4, 128]), op=ALU.mult),
                 reads=[r_sz, r_ggdn], writes=[r_sz])
            yield
            S.op("dve", lambda e: e.tensor_tensor(out=o1[:, :, :], in0=o1[:, :, :], in1=bc_u(st[0:64, 8:12], 128),
                                                  op=ALU.mult), reads=[r_o1, rs], writes=[r_o1])
            yield
            S.op("dve", lambda e: e.tensor_tensor(out=y_bf[:, :, :], in0=o1[:, :, :], in1=sz[:, :, :], op=ALU.mult),
                 reads=[r_o1, r_sz], writes=[r_ybf])
            yield
            bk, rb = nb()
            bkb = bk[:, :].bitcast(BF16)
            for h in range(4):
                S.op("pe", lambda e, h=h: e.transpose(bkb[:, h * 64:(h + 1) * 64], y_bf[:, h, :], idb[0:64, 0:64]),
                     reads=[r_ybf, r_idb], writes=[rb], acc=(h > 0))
            yield
            S.op("act", lambda e: e.copy(out=mixT[:, 4:8, c * 64:(c + 1) * 64],
                                         in_=bkb[:, 0:256].rearrange("p (h t) -> p h t", h=4)),
                 reads=[rb], writes=[r_mixT], acc=True)
            yield
            if c == 1:
                ctx.pop(("2", t))
                yield from wout_epi(t, xt, rx, mixT, r_mixT)

        steps = [(t, c) for t in range(nt) for c in range(2)]
        drive(front2(0))
        for k in range(len(steps) + 1):
            gens = []
            pls = []
            if k < len(steps):
                t_, c_ = steps[k]
                gens.append(chain2(t_, c_))
                pls.append("C1")
                if c_ == 1 and t_ + 1 < nt:
                    gens.append(front2(t_ + 1))
                    pls.append("all")
            if k > 0:
                gens.append(outs2(*steps[k - 1]))
                pls.append("O")
            drive(*gens, pools=pls)
        for h in range(4):
            S.dma("sp", pS_d[h, :, :], Sst[:, h, 0:128], r_S[h], reads=[r_S[h]], final=True)
        S.barrier()
        A2.close()
    else:
        for t in range(-npre, 0):
            drive(pre_front(t, halo=(t == -1)))
            drive(pre_back(t, False))
        for t in range(nt + 1):
            drive(own_front(t) if t < nt else None, own_back(t - 1) if t > 0 else None)

    S.barrier()
    if not state["A_closed"]:
        A.close()
    AC.close()

    S.barrier()
    W.close()

    if do_ffn:
        Bs = contextlib.ExitStack()
        w_up = S.sbuf("w_up_bf", [128, KC, 4096], BF16, Bs)
        r_wup2 = [Res("w_up0"), Res("w_up1")]
        for hf in range(2):
            for kc in range(KC):
                S.dma("pool", w_up[:, kc, hf * 2048:(hf + 1) * 2048], w_up_d[:, kc, hf * 2048:(hf + 1) * 2048], r_wup2[hf],
                      writes=[r_wup2[hf]], acc=True)
        w_dn = S.sbuf("w_dn_bf", [128, 32, D], BF16, Bs)
        r_wdn4 = [Res("w_dn%d" % i) for i in range(4)]
        for fc in range(32):
            S.dma("pool", w_dn[:, fc, :], w_dn_d[:, fc, :], r_wdn4[fc // 8], writes=[r_wdn4[fc // 8]], acc=True)
        g_fpre, r_gfpre = const_tile("g_fpre", [128, D], F32, g_fpre_d[:].partition_broadcast(128), stack=Bs)
        g_fpost, r_gfpost = const_tile("g_fpost", [128, D], F32, g_fpost_d[:].partition_broadcast(128), stack=Bs)
        G = ffn_group
        NTOK = G * 128
        x1g_ring = Ring(S, "x1g", 2, [128, G, D], F32, Bs)
        h2T_ring = Ring(S, "h2T", 2, [128, KC, NTOK], BF16, Bs)
        u2T = S.sbuf("u2T", [128, 32, NTOK], BF16, Bs)
        r_u2T = [Res("u2T%d" % i) for i in range(32)]
        rl_ring = Ring(S, "rl", 3, [128, NTOK], F32, Bs)
        yo_ring = Ring(S, "yo", 2, [128, D], F32, Bs)
        for g in range(nt // G):
            x1g, r_x1g = x1g_ring.next()
            h2T, r_h2T = h2T_ring.next()
            for ti in range(G):
                t = g * G + ti
                S.dma("sp", x1g[:, ti, :], y_d[t * 128:(t + 1) * 128, :], r_x1g, reads=[state["ry"][t]], writes=[r_x1g],
                      acc=(ti > 0))
            for ti in range(G):
                state["ry"][g * G + ti].r[id(r_x1g.dsem)] = (r_x1g.dsem, r_x1g.dcnt, "dma")
            for ti in range(G):
                norm_transpose(x1g[:, ti, :], r_x1g, g_fpre, r_gfpre, h2T[:, :, ti * 128:(ti + 1) * 128], r_h2T)
            for fc in range(32):
                bk, rb = nb()
                for kc in range(KC):
                    S.op("pe", lambda e, kc=kc, fc=fc, bk=bk: e.matmul(bk[:, 0:NTOK], lhsT=w_up[:, kc, fc * 128:(fc + 1) * 128],
                                                                       rhs=h2T[:, kc, :], start=(kc == 0), stop=(kc == KC - 1)),
                         reads=[r_h2T, r_wup2[fc // 16]], writes=[rb], acc=(kc > 0))
                rl, r_rl = rl_ring.next()
                S.op("act", lambda e, bk=bk, rl=rl: e.activation(out=rl[:, :], in_=bk[:, 0:NTOK], func=AF.Relu),
                     reads=[rb], writes=[r_rl])
                S.op("dve", lambda e, fc=fc, rl=rl: e.tensor_tensor(out=u2T[:, fc, :], in0=rl[:, :], in1=rl[:, :], op=ALU.mult),
                     reads=[r_rl], writes=[r_u2T[fc]])
            for ti in range(G):
                t = g * G + ti
                bk0, rb0 = nb()
                bk1, rb1 = nb()
                for n, (bk, rb) in enumerate(((bk0, rb0), (bk1, rb1))):
                    for fc in range(32):
                        S.op("pe", lambda e, fc=fc, n=n, bk=bk, ti=ti: e.matmul(
                            bk[:, :], lhsT=u2T[:, fc, ti * 128:(ti + 1) * 128], rhs=w_dn[:, fc, n * 512:(n + 1) * 512],
                            start=(fc == 0), stop=(fc == 31)), reads=[r_u2T[fc], r_wdn4[fc // 8]], writes=[rb], acc=(fc > 0))
                yo, r_yo = yo_ring.next()
                epilogue(bk0, rb0, bk1, rb1, g_fpost, r_gfpost, x1g[:, ti, :], r_x1g, yo[:, :], r_yo)
                S.dma("pool", y_d[t * 128:(t + 1) * 128, :], yo[:, :], r_yo, reads=[r_yo], writes=[state["ry"][t]], final=True)
        if do_sample and state.get("r_ys") is not None:
            x1s2, r_x1s2 = S.sbuf("x1s2", [NS, D], F32, Bs), Res("x1s2")
            S.dma("sp", x1s2[:, :], ys_d[:, :], r_x1s2, reads=[state["r_ys"]], writes=[r_x1s2])
            h2Ts, r_h2Ts = S.sbuf("h2Ts", [128, KC, NS], BF16, Bs), Res("h2Ts")
            norm_transpose(x1s2[:, :], r_x1s2, g_fpre, r_gfpre, h2Ts[:, :, :], r_h2Ts, n=NS)
            for fc in range(32):
                bk, rb = nb()
                for kc in range(KC):
                    S.op("pe", lambda e, kc=kc, fc=fc, bk=bk: e.matmul(bk[:, 0:NS], lhsT=w_up[:, kc, fc * 128:(fc + 1) * 128],
                                                                       rhs=h2Ts[:, kc, :], start=(kc == 0), stop=(kc == KC - 1)),
                         reads=[r_h2Ts, r_wup2[fc // 16]], writes=[rb], acc=(kc > 0))
                rl, r_rl = rl_ring.next()
                S.op("act", lambda e, bk=bk, rl=rl: e.activation(out=rl[:, 0:NS], in_=bk[:, 0:NS], func=AF.Relu),
                     reads=[rb], writes=[r_rl])
                S.op("dve", lambda e, fc=fc, rl=rl: e.tensor_tensor(out=u2T[:, fc, 0:NS], in0=rl[:, 0:NS], in1=rl[:, 0:NS],
                                                                    op=ALU.mult), reads=[r_rl], writes=[r_u2T[fc]])
            bk0, rb0 = nb()
            bk1, rb1 = nb()
            for n, (bk, rb) in enumerate(((bk0, rb0), (bk1, rb1))):
                for fc in range(32):
                    S.op("pe", lambda e, fc=fc, n=n, bk=bk: e.matmul(
                        bk[0:NS, :], lhsT=u2T[:, fc, 0:NS], rhs=w_dn[:, fc, n * 512:(n + 1) * 512],
                        start=(fc == 0), stop=(fc == 31)), reads=[r_u2T[fc], r_wdn4[fc // 8]], writes=[rb], acc=(fc > 0))
            yso, r_yso = S.sbuf("yso", [NS, D], F32, Bs), Res("yso")
            epilogue(bk0, rb0, bk1, rb1, g_fpost, r_gfpost, x1s2[:, :], r_x1s2, yso[:, :], r_yso, n=NS)
            S.dma("sp", ys_d[:, :], yso[:, :], r_yso, reads=[r_yso], writes=[state["r_ys"]], final=True)
        S.barrier()
        Bs.close()
    S.finish()
    S.close()
    info = dict(nops=S.nops, nwaits=S.nwaits, ndma=S.ndma)
    return nc, dbg_outs, info


def _core_inputs(inp, consts, b, j, nt=NT, npre=NPRE):
    seg = nt * 128
    x = inp["x_prompt"][b]
    lo = j * seg
    xin = np.zeros(((npre + nt) * 128, D), np.float32)
    npz = min(lo, npre * 128)
    if npz > 0:
        xin[npre * 128 - npz:npre * 128] = x[lo - npz:lo]
    xin[npre * 128:] = x[lo:lo + seg]
    m = {"xin": xin}
    m.update(consts)
    if j != 0:
        m["bias0"] = consts["biasN"]
    core = b * 4 + j
    sel = np.zeros((1, 4), np.float32)
    sel[0, j] = 1.0
    m["sel"] = sel
    s0, s1 = core * 16, (core + 1) * 16
    m["xs"] = np.ascontiguousarray(inp["x_sample"][s0:s1, 0, :])
    m["cst"] = np.ascontiguousarray(inp["state_conv"][0, s0:s1])
    m["ck"] = np.ascontiguousarray(inp["cache_win_k"][0, s0:s1]).reshape(16, 128, 128)
    m["cv"] = np.ascontiguousarray(inp["cache_win_v"][0, s0:s1]).reshape(16, 128, 128)
    m["sst"] = np.ascontiguousarray(inp["state_gdn"][0, s0:s1])
    m["convwt"] = np.ascontiguousarray(inp["conv_w"][0]).reshape(1, 4 * 1536)
    m["sinksh"] = np.ascontiguousarray(np.tile(inp["attn_sinks"][0], 16).reshape(128, 1))
    return m


def _shared_inputs(inp):
    w_in = np.ascontiguousarray(inp["w_in"][0].reshape(KC, 128, INW).transpose(1, 0, 2))
    w_out = np.ascontiguousarray(inp["w_out"][0].reshape(KC, 128, D).transpose(1, 0, 2))
    w_up = np.ascontiguousarray(inp["w_up"][0].reshape(KC, 128, 4096).transpose(1, 0, 2))
    w_dn = np.ascontiguousarray(inp["w_down"][0].reshape(32, 128, D).transpose(1, 0, 2))
    convw = np.ascontiguousarray(inp["conv_w"][0].T.reshape(12, 128, 4).transpose(1, 0, 2))
    sh = {"w_in": w_in, "w_out": w_out, "w_up": w_up, "w_down": w_dn, "convw": convw,
          "g_pre": inp["norm_mix_pre"], "g_post": inp["norm_mix_post"], "g_fpre": inp["norm_ffn_pre"],
          "g_fpost": inp["norm_ffn_post"], "g_gdn": inp["gdn_norm"], "sinks": inp["attn_sinks"],
          "alog": inp["gdn_a_log"], "dtb": inp["gdn_dt_bias"]}
    return {k: np.ascontiguousarray(v, dtype=np.float32) for k, v in sh.items()}


def kernel(**inputs):
    inp = {k: np.asarray(v) for k, v in inputs.items()}
    consts = _consts()
    shared = _shared_inputs(inp)
    nc, _, _ = build_program()
    in_maps = []
    for core in range(8):
        b, j = core // 4, core % 4
        m = _core_inputs(inp, consts, b, j)
        m.update(shared)
        in_maps.append(m)
    res = run_bass_kernel_spmd(nc, in_maps, core_ids=list(range(8)))
    R = res.results
    y_prompt = np.zeros((2, 8192, D), np.float32)
    for core in range(8):
        b, j = core // 4, core % 4
        y_prompt[b, j * SEG:(j + 1) * SEG] = R[core]["y"]
    p_conv = np.stack([R[3]["p_conv"], R[7]["p_conv"]])[None]
    p_k = np.stack([R[3]["p_k"], R[7]["p_k"]]).reshape(1, 2, 128, 2, 64)
    p_v = np.stack([R[3]["p_v"], R[7]["p_v"]]).reshape(1, 2, 128, 2, 64)
    p_S = np.stack([R[3]["p_S"], R[7]["p_S"]])[None]
    y_sample = np.concatenate([R[c]["ys"] for c in range(8)], 0).reshape(128, 1, D)
    s_conv = np.concatenate([R[c]["s_conv"] for c in range(8)], 0)[None]
    s_k = np.concatenate([R[c]["s_k"] for c in range(8)], 0).reshape(1, 128, 128, 2, 64)
    s_v = np.concatenate([R[c]["s_v"] for c in range(8)], 0).reshape(1, 128, 128, 2, 64)
    s_S = np.concatenate([R[c]["s_S"] for c in range(8)], 0)[None]
    return (y_prompt, y_sample, p_conv, p_k, p_v, p_S, s_conv, s_k, s_v, s_S)
```

```python
import contextlib
import numpy as np
import ml_dtypes
import concourse.bass as bass
import concourse.mybir as mybir
from concourse.bass_utils import run_bass_kernel_spmd

F32 = mybir.dt.float32
BF16 = mybir.dt.bfloat16
AF = mybir.ActivationFunctionType
ALU = mybir.AluOpType
AX = mybir.AxisListType

D = 1024
KC = 8
NT = 16
SEG = NT * 128
INW = 2824
EPS = 1e-6
NEG = -30000.0
P1W = (1, 2)
XCHG = True
NPRE = 1


class Res:
    __slots__ = ("name", "w", "r", "dsem", "dcnt")

    def __init__(self, name):
        self.name = name
        self.w = []
        self.r = {}
        self.dsem = None
        self.dcnt = 0


class Eng:
    def __init__(self, name, h, sem):
        self.name = name
        self.h = h
        self.sem = sem
        self.cnt = 0
        self.waited = {}


class Sched:
    def __init__(self, nc):
        self.nc = nc
        self.stack = contextlib.ExitStack()
        self.eng = {}
        for name, h in (("pe", nc.tensor), ("dve", nc.vector), ("act", nc.scalar),
                        ("pool", nc.gpsimd), ("sp", nc.sync)):
            sem = self.stack.enter_context(nc.semaphore("s_" + name))
            self.eng[name] = Eng(name, h, sem)
        self.finals = {}
        self.dtoks = {}
        self.nops = {k: 0 for k in self.eng}
        self.nwaits = 0
        self.ndma = 0

    def sbuf(self, name, shape, dtype, stack=None):
        return (stack or self.stack).enter_context(self.nc.sbuf_tensor("sb_" + name, list(shape), dtype))

    def psum(self, name, shape, dtype):
        return self.stack.enter_context(self.nc.psum_tensor("ps_" + name, list(shape), dtype))

    def dram(self, name, shape, dtype, kind, **kw):
        return self.nc.dram_tensor(name, list(shape), dtype, kind=kind, **kw).ap()

    def _wait(self, E, tok):
        sem, val, src = tok
        k = id(sem)
        if E.waited.get(k, 0) >= val:
            return
        E.waited[k] = val
        E.h.wait_ge(sem, val)
        self.nwaits += 1

    def _deps(self, E, reads, writes, acc):
        eng = E.name
        for r in reads:
            for tok in r.w:
                if tok[2] == eng and eng == "pe":
                    continue
                self._wait(E, tok)
        for w in writes:
            for tok in list(w.r.values()):
                if tok[2] == eng and eng == "pe":
                    continue
                self._wait(E, tok)
            if not acc:
                for tok in w.w:
                    if tok[2] == eng and eng == "pe":
                        continue
                    self._wait(E, tok)

    def _commit(self, tok, reads, writes, acc):
        k = id(tok[0])
        for r in reads:
            old = r.r.get(k)
            if old is None or old[1] < tok[1]:
                r.r[k] = tok
        for w in writes:
            if acc:
                w.w = [t for t in w.w if id(t[0]) != k] + [tok]
            else:
                w.w = [tok]
                w.r = {}

    def op(self, eng, fn, reads=(), writes=(), acc=False):
        E = self.eng[eng]
        self._deps(E, reads, writes, acc)
        ins = fn(E.h)
        E.cnt += 1
        ins.then_inc(E.sem, 1)
        tok = (E.sem, E.cnt, eng)
        self._commit(tok, reads, writes, acc)
        self.nops[eng] += 1
        return ins

    def dma(self, eng, out, in_, dres, reads=(), writes=(), acc=False, final=False, **kw):
        E = self.eng[eng]
        self._deps(E, reads, writes, acc)
        if dres.dsem is None:
            dres.dsem = self.stack.enter_context(self.nc.semaphore("d_" + dres.name))
        ins = E.h.dma_start(out=out, in_=in_, **kw)
        dres.dcnt += 16
        ins.then_inc(dres.dsem, 16)
        tok = (dres.dsem, dres.dcnt, "dma")
        self._commit(tok, reads, writes, acc)
        self.dtoks[id(dres.dsem)] = tok
        if final:
            self.finals[id(dres.dsem)] = tok
        self.ndma += 1
        return ins

    def cc(self, kind, in_ap, out_ap, dres, groups, reads=(), writes=()):
        E = self.eng["pool"]
        self._deps(E, reads, writes, False)
        if dres.dsem is None:
            dres.dsem = self.stack.enter_context(self.nc.semaphore("d_" + dres.name))
        ins = E.h.collective_compute(kind, ALU.bypass, replica_groups=groups, ins=[in_ap], outs=[out_ap])
        dres.dcnt += 1
        ins.then_inc(dres.dsem, 1)
        tok = (dres.dsem, dres.dcnt, "dma")
        self._commit(tok, reads, writes, False)
        self.dtoks[id(dres.dsem)] = tok
        return ins

    def barrier(self):
        toks = [(e.sem, e.cnt, e.name) for e in self.eng.values() if e.cnt > 0]
        toks += list(self.dtoks.values())
        for E in self.eng.values():
            for tok in toks:
                if tok[2] == E.name:
                    continue
                self._wait(E, tok)

    def finish(self):
        E = self.eng["sp"]
        for tok in self.dtoks.values():
            self._wait(E, tok)
        for e in self.eng.values():
            if e.cnt > 0 and e.name != "sp":
                self._wait(E, (e.sem, e.cnt, e.name))

    def close(self):
        self.stack.close()


class Ring:
    def __init__(self, S, name, n, shape, dtype, stack=None):
        self.t = [S.sbuf("%s%d" % (name, i), shape, dtype, stack) for i in range(n)]
        self.r = [Res("%s%d" % (name, i)) for i in range(n)]
        self.i = 0
        self.n = n

    def next(self):
        i = self.i
        self.i = (i + 1) % self.n
        return self.t[i], self.r[i]


def _consts():
    c = {}
    c["idf"] = np.eye(128, dtype=np.float32)
    c["idb"] = np.eye(128).astype(ml_dtypes.bfloat16)
    c["ones"] = np.ones((128, 128), np.float32)
    k = np.arange(64)[:, None]
    i = np.arange(64)[None, :]
    c["m_le"] = (k <= i).astype(np.float32)
    c["m_gt"] = (k > i).astype(np.float32)
    c["m_lt"] = (k < i).astype(np.float32)
    q = np.arange(128)[:, None]
    s = np.arange(256)[None, :]
    dist = 128 + q - s
    valid = (dist >= 0) & (dist <= 128)
    slopes = np.exp2(-8.0 * np.arange(1, 9, dtype=np.float32) / 8).astype(np.float32)
    b = np.where(valid[:, None, :], -slopes[None, :, None] * dist[:, None, :].astype(np.float32), NEG)
    c["biasN"] = b.astype(np.float32)
    b0 = b.copy()
    b0[:, :, :128] = NEG
    c["bias0"] = b0.astype(np.float32)
    c["i16b"] = np.eye(16, dtype=np.float32).reshape(1, 256)
    pos = np.arange(129)[None, :]
    hs = (np.arange(128) % 8)
    bs = -slopes[hs][:, None] * (128 - pos).astype(np.float32)
    c["biass"] = bs.astype(np.float32)
    selm = np.zeros((32, 128), np.float32)
    for sidx in range(16):
        for h in range(8):
            selm[(h // 4) * 16 + sidx, sidx * 8 + h] = 1.0
    c["selm"] = selm
    return c


SAMPLE_CONSTS = ("i16b", "biass")
CONST_SHAPES = {"selm": ([32, 128], F32), "idf": ([128, 128], F32), "idb": ([128, 128], BF16), "ones": ([128, 128], F32),
                "m_le": ([64, 64], F32), "m_gt": ([64, 64], F32), "m_lt": ([64, 64], F32),
                "biasN": ([128, 8, 256], F32), "bias0": ([128, 8, 256], F32)}


def build_program(dbg=(), nt=NT, do_ffn=True, ffn_group=2, npre=NPRE, do_sample=True, sstage=9, xchg=XCHG):
    nc = bass.Bass("TRN2", target_bir_lowering=False)
    S = Sched(nc)
    dbg_outs = {}

    xin = S.dram("xin", [(nt + npre) * 128, D], F32, "ExternalInput")
    w_in_d = S.dram("w_in", [128, KC, INW], F32, "ExternalInput")
    w_out_d = S.dram("w_out", [128, KC, D], F32, "ExternalInput")
    w_up_d = S.dram("w_up", [128, KC, 4096], F32, "ExternalInput")
    w_dn_d = S.dram("w_down", [128, 32, D], F32, "ExternalInput")
    convw_d = S.dram("convw", [128, 12, 4], F32, "ExternalInput")
    g_pre_d = S.dram("g_pre", [1, D], F32, "ExternalInput")
    g_post_d = S.dram("g_post", [1, D], F32, "ExternalInput")
    g_fpre_d = S.dram("g_fpre", [1, D], F32, "ExternalInput")
    g_fpost_d = S.dram("g_fpost", [1, D], F32, "ExternalInput")
    g_gdn_d = S.dram("g_gdn", [1, 128], F32, "ExternalInput")
    sinks_d = S.dram("sinks", [1, 8], F32, "ExternalInput")
    alog_d = S.dram("alog", [1, 4], F32, "ExternalInput")
    dtb_d = S.dram("dtb", [1, 4], F32, "ExternalInput")
    cd = {k: S.dram(k, shp, dt, "ExternalInput") for k, (shp, dt) in CONST_SHAPES.items()}

    NS = 16
    xs_d = S.dram("xs", [NS, D], F32, "ExternalInput")
    cst_d = S.dram("cst", [NS, 3, 1536], F32, "ExternalInput")
    ck_d = S.dram("ck", [NS, 128, 128], F32, "ExternalInput")
    cv_d = S.dram("cv", [NS, 128, 128], F32, "ExternalInput")
    sst_d = S.dram("sst", [NS, 4, 128, 128], F32, "ExternalInput")
    convwt_d = S.dram("convwt", [1, 4 * 1536], F32, "ExternalInput")
    i16b_d = S.dram("i16b", [1, 256], F32, "ExternalInput")
    biass_d = S.dram("biass", [128, 129], F32, "ExternalInput")
    sinksh_d = S.dram("sinksh", [128, 1], F32, "ExternalInput")
    ys_d = S.dram("ys", [NS, D], F32, "ExternalOutput")
    sconv_d = S.dram("s_conv", [NS, 3, 1536], F32, "ExternalOutput")
    sk_d = S.dram("s_k", [NS, 128, 128], F32, "ExternalOutput")
    sv_d = S.dram("s_v", [NS, 128, 128], F32, "ExternalOutput")
    sS_d = S.dram("s_S", [NS, 4, 128, 128], F32, "ExternalOutput")
    scr_q = S.dram("scr_q", [NS, 512], F32, "Internal")
    scr_kv = S.dram("scr_kv", [NS, 256], F32, "Internal")
    scr_ao = S.dram("scr_ao", [128, 64], F32, "Internal")
    sel_d = S.dram("sel", [1, 4], F32, "ExternalInput")
    sp_wT = S.dram("sp_wT", [nt, 128, 8, 64], BF16, "Internal")
    sp_qT = S.dram("sp_qT", [nt, 128, 4, 128], BF16, "Internal")
    sp_kd = S.dram("sp_kd", [nt, 64, 8, 128], BF16, "Internal")
    sp_ub = S.dram("sp_ub", [nt, 64, 8, 128], F32, "Internal")
    sp_qkT = S.dram("sp_qkT", [nt, 64, 8, 64], BF16, "Internal")
    sp_gs = S.dram("sp_gs", [nt, 64, 16, 8], F32, "Internal")
    sp_ld = S.dram("sp_ld", [nt, 128, 8], F32, "Internal")
    sp_zab = S.dram("sp_zab", [nt, 64, 2, 520], F32, "Internal")
    sp_mix = S.dram("sp_mix", [nt, 128, 4, 128], BF16, "Internal")
    xsrc_d = S.dram("xsrc", [128, 1024], F32, "Internal")
    xdst_d = S.dram("xdst", [4 * 128, 1024], F32, "Internal", addr_space="Local")
    y_d = S.dram("y", [nt * 128, D], F32, "ExternalOutput")
    pconv_d = S.dram("p_conv", [3, 1536], F32, "ExternalOutput")
    pk_d = S.dram("p_k", [128, 128], F32, "ExternalOutput")
    pv_d = S.dram("p_v", [128, 128], F32, "ExternalOutput")
    pS_d = S.dram("p_S", [4, 128, 128], F32, "ExternalOutput")

    def dbg_out(name, ap_sb, res, shape):
        if name not in dbg:
            return
        d = S.dram("dbg_" + name, list(shape), F32, "ExternalOutput")
        dbg_outs[name] = d
        S.dma("sp", d[:] if True else d, ap_sb, res, reads=[res], final=True)

    banks = [S.psum("bank%d" % i, [128, 512], F32) for i in range(8)]
    bres = [Res("bank%d" % i) for i in range(8)]
    bstate = {"i": 0}

    POOLS = {"all": list(range(8)), "F": [0, 1, 2], "P": [3, 4, 5], "C": [6, 7],
             "C1": [0, 1], "C2": [2, 3, 4, 5], "O": [6, 7]}
    bstate.update({"pool": "all", "cnt": {k: 0 for k in POOLS}})

    def nb(pool=None):
        p = pool or bstate["pool"]
        lst = POOLS[p]
        k = bstate["cnt"][p]
        bstate["cnt"][p] = k + 1
        i = lst[k % len(lst)]
        return banks[i], bres[i]

    def const_tile(name, shape, dt, src_ap, eng="sp", stack=None):
        t = S.sbuf(name, shape, dt, stack)
        r = Res(name)
        S.dma(eng, t[:], src_ap, r, writes=[r])
        return t, r

    idf, r_idf = const_tile("idf", [128, 128], F32, cd["idf"][:])
    idb, r_idb = const_tile("idb", [128, 128], BF16, cd["idb"][:])
    ones, r_ones = const_tile("ones", [128, 128], F32, cd["ones"][:])
    epsT = S.sbuf("epsT", [128, 1], F32)
    r_eps = Res("eps")
    S.op("dve", lambda e: e.memset(epsT[:], EPS), writes=[r_eps])

    stat = Ring(S, "stat", 12, [128, 16], F32)
    junk = Ring(S, "junk", 2, [128, 1024], BF16)

    def rstd_from_ss(ss_ap, r_ss, n, scale):
        st, rs = stat.next()
        S.op("act", lambda e: e.activation(out=st[0:n, 0:1], in_=ss_ap, func=AF.Ln, bias=epsT[0:n, :], scale=scale),
             reads=[r_ss, r_eps], writes=[rs])
        S.op("act", lambda e: e.activation(out=st[0:n, 1:2], in_=st[0:n, 0:1], func=AF.Exp, scale=-0.5), reads=[rs], writes=[rs])
        return st[0:n, 1:2], rs

    def norm_transpose(x_ap, r_x, gain, r_gain, hT_out_ap, r_hT, n=128):
        jk, rj = junk.next()
        st, rs = stat.next()
        S.op("act", lambda e: e.activation(out=jk[0:n, :], in_=x_ap, func=AF.Square, accum_out=st[0:n, 0:1]),
             reads=[r_x], writes=[rj, rs])
        rstd, rr = rstd_from_ss(st[0:n, 0:1], rs, n, 1.0 / D)
        hb, rh = junk.next()
        S.op("dve", lambda e: e.scalar_tensor_tensor(out=hb[0:n, :], in0=x_ap, scalar=rstd, in1=gain[0:n, :],
                                                     op0=ALU.mult, op1=ALU.mult),
             reads=[r_x, rr, r_gain], writes=[rh])
        bk, rb = nb()
        bkb = bk[:, :].bitcast(BF16)
        for kc in range(KC):
            S.op("pe", lambda e, kc=kc: e.transpose(bkb[:, kc * 128:kc * 128 + n], hb[0:n, kc * 128:(kc + 1) * 128],
                                                    idb[0:n, 0:n]),
                 reads=[rh, r_idb], writes=[rb], acc=(kc > 0))
        S.op("act", lambda e: e.copy(out=hT_out_ap, in_=bkb.rearrange("p (k t) -> p k t", k=KC)[:, :, 0:n]),
             reads=[rb], writes=[r_hT])

    def epilogue(bk0, rb0, bk1, rb1, gain, r_gain, resid_ap, r_resid, out_ap, r_out, n=128):
        jk, rj = junk.next()
        st, rs = stat.next()
        S.op("act", lambda e: e.activation(out=jk[0:n, 0:512], in_=bk0[0:n, :], func=AF.Square, accum_out=st[0:n, 0:1]),
             reads=[rb0], writes=[rj, rs])
        S.op("act", lambda e: e.activation(out=jk[0:n, 512:1024], in_=bk1[0:n, :], func=AF.Square, accum_out=st[0:n, 1:2]),
             reads=[rb1], writes=[rj, rs], acc=True)
        S.op("dve", lambda e: e.tensor_tensor(out=st[0:n, 2:3], in0=st[0:n, 0:1], in1=st[0:n, 1:2], op=ALU.add),
             reads=[rs], writes=[rs])
        rstd, rr = rstd_from_ss(st[0:n, 2:3], rs, n, 1.0 / D)
        S.op("dve", lambda e: e.tensor_tensor(out=out_ap[:, 0:512], in0=bk0[0:n, :], in1=gain[0:n, 0:512], op=ALU.mult),
             reads=[rb0, r_gain], writes=[r_out])
        S.op("dve", lambda e: e.tensor_tensor(out=out_ap[:, 512:1024], in0=bk1[0:n, :], in1=gain[0:n, 512:1024], op=ALU.mult),
             reads=[rb1, r_gain], writes=[r_out], acc=True)
        S.op("dve", lambda e: e.scalar_tensor_tensor(out=out_ap, in0=out_ap, scalar=rstd, in1=resid_ap,
                                                     op0=ALU.mult, op1=ALU.add),
             reads=[r_out, rr, r_resid], writes=[r_out])

    W = contextlib.ExitStack()
    AC = contextlib.ExitStack()
    A = contextlib.ExitStack()
    w_in = S.sbuf("w_in_bf", [128, KC, INW], BF16, W)
    r_win = [Res("w_in%d" % kc) for kc in range(KC)]
    for kc in range(KC):
        S.dma("pool", w_in[:, kc, :], w_in_d[:, kc, :], r_win[kc], writes=[r_win[kc]])
    w_out = S.sbuf("w_out_bf", [128, KC, D], BF16, W)
    r_wout = Res("w_out")
    for kc in range(KC):
        S.dma("pool", w_out[:, kc, :], w_out_d[:, kc, :], r_wout, writes=[r_wout], acc=True)

    g_pre, r_gpre = const_tile("g_pre", [128, D], F32, g_pre_d[:].partition_broadcast(128), stack=W)
    g_post, r_gpost = const_tile("g_post", [128, D], F32, g_post_d[:].partition_broadcast(128), stack=W)
    convw, r_convw = const_tile("convw", [128, 12, 4], F32, convw_d[:], stack=W)
    m_le, r_mle = const_tile("m_le", [64, 64], F32, cd["m_le"][:], stack=W)
    m_gt, r_mgt = const_tile("m_gt", [64, 64], F32, cd["m_gt"][:], stack=W)
    m_lt, r_mlt = const_tile("m_lt", [64, 64], F32, cd["m_lt"][:], stack=W)
    bias = S.sbuf("bias", [128, 8, 256], F32, W)
    r_bias = Res("bias")
    S.dma("sp", bias[:], cd["bias0"][:], r_bias, writes=[r_bias])
    sink, r_sink = const_tile("sink", [128, 8], F32, sinks_d[:].partition_broadcast(128), stack=W)
    nsink = S.sbuf("nsink", [128, 8], F32, W)
    r_nsink = Res("nsink")
    S.op("dve", lambda e: e.tensor_scalar(out=nsink[:], in0=sink[:], scalar1=-1.0, scalar2=None, op0=ALU.mult),
         reads=[r_sink], writes=[r_nsink])
    g_gdn, r_ggdn = const_tile("g_gdn", [64, 128], F32, g_gdn_d[:].partition_broadcast(64), stack=W)
    dtb8 = S.sbuf("dtb8", [64, 8], F32, W)
    r_dtb8 = Res("dtb8")
    S.dma("sp", dtb8[:, 0:4], dtb_d[:].partition_broadcast(64), r_dtb8, writes=[r_dtb8], acc=True)
    S.dma("sp", dtb8[:, 4:8], dtb_d[:].partition_broadcast(64), r_dtb8, writes=[r_dtb8], acc=True)
    negA8 = S.sbuf("negA8", [64, 8], F32, W)
    r_negA8 = Res("negA8")
    S.dma("sp", negA8[:, 0:4], alog_d[:].partition_broadcast(64), r_negA8, writes=[r_negA8], acc=True)
    S.dma("sp", negA8[:, 4:8], alog_d[:].partition_broadcast(64), r_negA8, writes=[r_negA8], acc=True)
    S.op("act", lambda e: e.activation(out=negA8[:], in_=negA8[:], func=AF.Exp), reads=[r_negA8], writes=[r_negA8])
    S.op("dve", lambda e: e.tensor_scalar(out=negA8[:], in0=negA8[:], scalar1=-1.0, scalar2=None, op0=ALU.mult),
         reads=[r_negA8], writes=[r_negA8])

    state = {"prev_xc": None}

    def inproj_tok(hT, r_hT, col_lo, ncols, tok_lo, ntok):
        bk, rb = nb()
        for kc in range(KC):
            S.op("pe", lambda e, kc=kc: e.matmul(bk[0:ntok, 0:ncols], lhsT=hT[:, kc, tok_lo:tok_lo + ntok],
                                                 rhs=w_in[:, kc, col_lo:col_lo + ncols], start=(kc == 0), stop=(kc == KC - 1)),
                 reads=[r_hT, r_win[kc]], writes=[rb], acc=(kc > 0))
        return bk, rb

    def sample_phase():
        r_misc = Res("misc")
        for sidx in range(NS):
            S.dma("sp", sconv_d[sidx, 0:2, :], cst_d[sidx, 1:3, :], r_misc, final=True)
            S.dma("sp", sk_d[sidx, 0:127, :], ck_d[sidx, 1:128, :], r_misc, final=True)
            S.dma("sp", sv_d[sidx, 0:127, :], cv_d[sidx, 1:128, :], r_misc, final=True)
        SA = contextlib.ExitStack()
        xs, r_xs = const_tile("xs", [NS, D], F32, xs_d[:], stack=SA)
        hTs = S.sbuf("hTs", [128, KC, NS], BF16, SA)
        r_hTs = Res("hTs")
        norm_transpose(xs[:, :], r_xs, g_pre, r_gpre, hTs[:, :, :], r_hTs, n=NS)
        Ps = S.sbuf("Ps", [NS, INW], F32, SA)
        r_Ps = Res("Ps")
        col = 0
        ci = 0
        while col < INW:
            ncol = min(512, INW - col)
            bk, rb = inproj_tok(hTs, r_hTs, col, ncol, 0, NS)
            eng = "act" if ci % 2 == 0 else "dve"
            if eng == "act":
                S.op("act", lambda e, bk=bk, col=col, ncol=ncol: e.copy(out=Ps[:, col:col + ncol], in_=bk[0:NS, 0:ncol]),
                     reads=[rb], writes=[r_Ps], acc=True)
            else:
                S.op("dve", lambda e, bk=bk, col=col, ncol=ncol: e.tensor_copy(out=Ps[:, col:col + ncol], in_=bk[0:NS, 0:ncol]),
                     reads=[rb], writes=[r_Ps], acc=True)
            col += ncol
            ci += 1
        S.dma("pool", sconv_d[:, 2, :], Ps[:, 768:2304], r_Ps, reads=[r_Ps], final=True)
        S.dma("pool", sk_d[:, 127, :], Ps[:, 512:640], r_Ps, reads=[r_Ps], final=True)
        S.dma("pool", sv_d[:, 127, :], Ps[:, 640:768], r_Ps, reads=[r_Ps], final=True)
        r_scr = Res("scr_qkv")
        S.dma("sp", scr_q[:, :], Ps[:, 0:512], r_scr, reads=[r_Ps], writes=[r_scr], acc=True)
        S.dma("sp", scr_kv[:, :], Ps[:, 512:768], r_scr, reads=[r_Ps], writes=[r_scr], acc=True)
        mixs = S.sbuf("mixs", [NS, D], BF16, SA)
        r_mixs = Res("mixs")
        i16b_t, r_i16b = const_tile("i16b", [128, 256], F32, i16b_d[:].partition_broadcast(128), stack=SA)
        i16b = i16b_t[:, :].rearrange("p (a b) -> p a b", a=16)
        cvs = S.sbuf("cvs", [NS, 1536], F32, SA)
        r_cvs = Res("cvs")
        sg, r_sg = S.sbuf("sg", [NS, 16, 4], F32, SA), Res("sg")
        rn8, r_rn8 = S.sbuf("rn8", [NS, 8], F32, SA), Res("rn8")

        if sstage < 1:
            S.barrier()
            SA.close()
            return
        S1 = contextlib.ExitStack()
        cst, r_cst = const_tile("cst", [NS, 3, 1536], F32, cst_d[:], stack=S1)
        cwb_t, r_cwb = const_tile("cwb", [NS, 4 * 1536], F32, convwt_d[:].partition_broadcast(NS), stack=S1)
        cwb = cwb_t[:, :].rearrange("p (i c) -> p i c", i=4)
        ctmp, r_ctmp = S.sbuf("ctmp", [NS, 1536], F32, S1), Res("ctmp")
        S.op("dve", lambda e: e.tensor_tensor(out=cvs[:, :], in0=cst[:, 0, :], in1=cwb[:, 0, :], op=ALU.mult),
             reads=[r_cst, r_cwb], writes=[r_cvs])
        for i in range(1, 4):
            src = cst[:, i, :] if i < 3 else Ps[:, 768:2304]
            S.op("dve", lambda e, i=i, src=src: e.tensor_tensor(out=ctmp[:, :], in0=src, in1=cwb[:, i, :], op=ALU.mult),
                 reads=[r_cst, r_cwb, r_Ps], writes=[r_ctmp])
            S.op("dve", lambda e: e.tensor_tensor(out=cvs[:, :], in0=cvs[:, :], in1=ctmp[:, :], op=ALU.add),
                 reads=[r_cvs, r_ctmp], writes=[r_cvs])
        S.op("act", lambda e: e.activation(out=cvs[:, :], in_=cvs[:, :], func=AF.Silu), reads=[r_cvs], writes=[r_cvs])
        S.op("dve", lambda e: e.tensor_tensor(out=ctmp[:, 0:1024], in0=cvs[:, 0:1024], in1=cvs[:, 0:1024], op=ALU.mult),
             reads=[r_cvs], writes=[r_ctmp])
        S.op("dve", lambda e: e.tensor_reduce(out=rn8[:, :], in_=ctmp[:, 0:1024].rearrange("p (h d) -> p h d", h=8),
                                              axis=AX.X, op=ALU.add), reads=[r_ctmp], writes=[r_rn8])
        S.op("act", lambda e: e.activation(out=rn8[:, :], in_=rn8[:, :], func=AF.Ln, bias=epsT[0:NS, :], scale=1.0),
             reads=[r_rn8, r_eps], writes=[r_rn8])
        S.op("act", lambda e: e.activation(out=rn8[:, :], in_=rn8[:, :], func=AF.Exp, scale=-0.5), reads=[r_rn8], writes=[r_rn8])
        S.op("dve", lambda e: e.tensor_scalar(out=rn8[:, 0:4], in0=rn8[:, 0:4], scalar1=128.0 ** -0.5, scalar2=None,
                                              op0=ALU.mult), reads=[r_rn8], writes=[r_rn8])
        S.op("dve", lambda e: e.tensor_tensor(out=cvs[:, 0:1024].rearrange("p (h d) -> p h d", h=8),
                                              in0=cvs[:, 0:1024].rearrange("p (h d) -> p h d", h=8),
                                              in1=rn8[:, :].unsqueeze(2).broadcast_to([NS, 8, 128]), op=ALU.mult),
             reads=[r_cvs, r_rn8], writes=[r_cvs])

        def sgv(i):
            return sg[:, i, :]
        S.op("dve", lambda e: e.tensor_tensor(out=sgv(0), in0=Ps[:, 2816:2820], in1=dtb8[0:NS, 0:4], op=ALU.add),
             reads=[r_Ps, r_dtb8], writes=[r_sg])
        S.op("act", lambda e: e.activation(out=sgv(4), in_=sgv(0), func=AF.Abs), reads=[r_sg], writes=[r_sg])
        S.op("act", lambda e: e.activation(out=sgv(4), in_=sgv(4), func=AF.Exp, scale=-1.0), reads=[r_sg], writes=[r_sg])
        S.op("act", lambda e: e.activation(out=sgv(4), in_=sgv(4), func=AF.Ln, bias=1.0), reads=[r_sg], writes=[r_sg])
        S.op("dve", lambda e: e.scalar_tensor_tensor(out=sgv(0), in0=sgv(0), scalar=0.0, in1=sgv(4), op0=ALU.max, op1=ALU.add),
             reads=[r_sg], writes=[r_sg])
        S.op("dve", lambda e: e.tensor_tensor(out=sgv(0), in0=sgv(0), in1=negA8[0:NS, 0:4], op=ALU.mult),
             reads=[r_sg, r_negA8], writes=[r_sg])
        S.op("act", lambda e: e.activation(out=sgv(2), in_=sgv(0), func=AF.Exp), reads=[r_sg], writes=[r_sg])
        S.op("act", lambda e: e.activation(out=sgv(1), in_=Ps[:, 2820:2824], func=AF.Exp, scale=-1.0), reads=[r_Ps], writes=[r_sg])
        S.op("dve", lambda e: e.tensor_scalar(out=sgv(1), in0=sgv(1), scalar1=1.0, scalar2=None, op0=ALU.add),
             reads=[r_sg], writes=[r_sg])
        S.op("dve", lambda e: e.reciprocal(out=sgv(1), in_=sgv(1)), reads=[r_sg], writes=[r_sg])
        S.op("dve", lambda e: e.tensor_tensor(out=ctmp[:, 0:512], in0=cvs[:, 0:512], in1=cvs[:, 512:1024], op=ALU.mult),
             reads=[r_cvs], writes=[r_ctmp])
        S.op("dve", lambda e: e.tensor_reduce(out=sgv(3), in_=ctmp[:, 0:512].rearrange("p (h d) -> p h d", h=4),
                                              axis=AX.X, op=ALU.add), reads=[r_ctmp], writes=[r_sg])
        S.barrier()
        S1.close()

        if sstage < 2:
            SA.close()
            return
        S2 = contextlib.ExitStack()
        Ssb = S.sbuf("Ssb", [128, NS * 4, 128], F32, S2)
        r_Ssb4 = [Res("Ssb%d" % i) for i in range(4)]
        r_Ssb = [r_Ssb4[i % 4] for i in range(NS)]
        for sidx in range(NS):
            S.dma("sp", Ssb[:, sidx * 4:(sidx + 1) * 4, :], sst_d[sidx, :, :, :].rearrange("h k v -> k h v"),
                  r_Ssb[sidx], writes=[r_Ssb[sidx]], acc=True)
        kqT, r_kqT = S.sbuf("kqT", [128, 8, NS], F32, S2), Res("kqT")
        bk, rb = nb()
        for hh in range(8):
            S.op("pe", lambda e, hh=hh, bk=bk: e.transpose(bk[:, hh * NS:(hh + 1) * NS], cvs[:, hh * 128:(hh + 1) * 128],
                                                           idf[0:NS, 0:NS]), reads=[r_cvs, r_idf], writes=[rb], acc=(hh > 0))
        S.op("act", lambda e, bk=bk: e.copy(out=kqT[:, :, :], in_=bk[:, 0:8 * NS].rearrange("p (h s) -> p h s", h=8)),
             reads=[rb], writes=[r_kqT])
        kqTm, r_kqTm = S.sbuf("kqTm", [128, 8, NS, NS], F32, S2), Res("kqTm")
        S.op("dve", lambda e: e.tensor_tensor(out=kqTm[:, :, :, :],
                                              in0=kqT[:, :, :].unsqueeze(2).broadcast_to([128, 8, NS, NS]),
                                              in1=i16b.unsqueeze(1).broadcast_to([128, 8, NS, NS]), op=ALU.mult),
             reads=[r_kqT, r_i16b], writes=[r_kqTm])
        bkk, rbk = nb()
        bkq2, rbq2 = nb()
        for h in range(4):
            for sidx in range(NS):
                S.op("pe", lambda e, h=h, sidx=sidx: e.matmul(bkk[0:NS, h * 128:(h + 1) * 128], lhsT=kqTm[:, 4 + h, sidx, :],
                                                              rhs=Ssb[:, sidx * 4 + h, :], start=(sidx == 0), stop=(sidx == NS - 1)),
                     reads=[r_kqTm, r_Ssb[sidx]], writes=[rbk], acc=not (h == 0 and sidx == 0))
        for h in range(4):
            for sidx in range(NS):
                S.op("pe", lambda e, h=h, sidx=sidx: e.matmul(bkq2[0:NS, h * 128:(h + 1) * 128], lhsT=kqTm[:, h, sidx, :],
                                                              rhs=Ssb[:, sidx * 4 + h, :], start=(sidx == 0), stop=(sidx == NS - 1)),
                     reads=[r_kqTm, r_Ssb[sidx]], writes=[rbq2], acc=not (h == 0 and sidx == 0))
        us, r_us = S.sbuf("us", [NS, 4, 128], F32, S2), Res("us")
        os_, r_os = S.sbuf("os", [NS, 4, 128], F32, S2), Res("os")
        ot, r_ot = S.sbuf("ot", [NS, 4, 128], F32, S2), Res("ot")

        def bcs(i):
            return sg[:, i, :].unsqueeze(2).broadcast_to([NS, 4, 128])
        v3 = cvs[:, 1024:1536].rearrange("p (h d) -> p h d", h=4)
        S.op("dve", lambda e: e.tensor_tensor(out=us[:, :, :], in0=bkk[0:NS, :].rearrange("p (h d) -> p h d", h=4),
                                              in1=bcs(2), op=ALU.mult), reads=[rbk, r_sg], writes=[r_us])
        S.op("dve", lambda e: e.tensor_tensor(out=us[:, :, :], in0=v3, in1=us[:, :, :], op=ALU.subtract),
             reads=[r_cvs, r_us], writes=[r_us])
        S.op("dve", lambda e: e.tensor_tensor(out=us[:, :, :], in0=us[:, :, :], in1=bcs(1), op=ALU.mult),
             reads=[r_us, r_sg], writes=[r_us])
        S.op("dve", lambda e: e.tensor_tensor(out=os_[:, :, :], in0=bkq2[0:NS, :].rearrange("p (h d) -> p h d", h=4),
                                              in1=bcs(2), op=ALU.mult), reads=[rbq2, r_sg], writes=[r_os])
        S.op("dve", lambda e: e.tensor_tensor(out=ot[:, :, :], in0=us[:, :, :], in1=bcs(3), op=ALU.mult),
             reads=[r_us, r_sg], writes=[r_ot])
        S.op("dve", lambda e: e.tensor_tensor(out=os_[:, :, :], in0=os_[:, :, :], in1=ot[:, :, :], op=ALU.add),
             reads=[r_os, r_ot], writes=[r_os])
        S.op("dve", lambda e: e.tensor_tensor(out=ot[:, :, :], in0=os_[:, :, :], in1=os_[:, :, :], op=ALU.mult),
             reads=[r_os], writes=[r_ot])
        S.op("dve", lambda e: e.tensor_reduce(out=sgv(5), in_=ot[:, :, :], axis=AX.X, op=ALU.add), reads=[r_ot], writes=[r_sg])
        S.op("act", lambda e: e.activation(out=sgv(5), in_=sgv(5), func=AF.Ln, bias=epsT[0:NS, :], scale=1.0 / 128),
             reads=[r_sg, r_eps], writes=[r_sg])
        S.op("act", lambda e: e.activation(out=sgv(5), in_=sgv(5), func=AF.Exp, scale=-0.5), reads=[r_sg], writes=[r_sg])
        S.op("act", lambda e: e.activation(out=ot[:, :, :].rearrange("p h d -> p (h d)"), in_=Ps[:, 2304:2816], func=AF.Silu),
             reads=[r_Ps], writes=[r_ot])
        S.op("dve", lambda e: e.tensor_tensor(out=ot[:, :, :], in0=ot[:, :, :],
                                              in1=g_gdn[0:NS, :].unsqueeze(1).broadcast_to([NS, 4, 128]), op=ALU.mult),
             reads=[r_ot, r_ggdn], writes=[r_ot])
        S.op("dve", lambda e: e.tensor_tensor(out=os_[:, :, :], in0=os_[:, :, :], in1=bcs(5), op=ALU.mult),
             reads=[r_os, r_sg], writes=[r_os])
        S.op("dve", lambda e: e.tensor_tensor(out=mixs[:, 512:1024].rearrange("p (h d) -> p h d", h=4), in0=os_[:, :, :],
                                              in1=ot[:, :, :], op=ALU.mult), reads=[r_os, r_ot], writes=[r_mixs], acc=True)
        um, r_um = S.sbuf("um", [NS, NS, 512], F32, S2), Res("um")
        S.op("dve", lambda e: e.tensor_tensor(out=um[:, :, :],
                                              in0=us[:, :, :].rearrange("p h d -> p (h d)").unsqueeze(1).broadcast_to([NS, NS, 512]),
                                              in1=idf[0:NS, 0:NS].unsqueeze(2).broadcast_to([NS, NS, 512]), op=ALU.mult),
             reads=[r_us, r_idf], writes=[r_um])
        dm, r_dm = S.sbuf("dm", [NS, NS, 4], F32, S2), Res("dm")
        S.op("dve", lambda e: e.tensor_tensor(out=dm[:, :, :], in0=sg[:, 2, :].unsqueeze(1).broadcast_to([NS, NS, 4]),
                                              in1=idf[0:NS, 0:NS].unsqueeze(2).broadcast_to([NS, NS, 4]), op=ALU.mult),
             reads=[r_sg, r_idf], writes=[r_dm])
        bkd, rbd = nb()
        S.op("pe", lambda e: e.matmul(bkd[:, 0:NS * 4], lhsT=ones[0:NS, :], rhs=dm[:, :, :].rearrange("p s h -> p (s h)"),
                                      start=True, stop=True), reads=[r_dm, r_ones], writes=[rbd])
        dec128, r_dec = S.sbuf("dec128", [128, NS * 4], F32, S2), Res("dec128")
        S.op("act", lambda e: e.copy(out=dec128[:, :], in_=bkd[:, 0:NS * 4]), reads=[rbd], writes=[r_dec])
        for sidx in range(NS):
            bk, rb = nb()
            for h in range(4):
                S.op("pe", lambda e, h=h, sidx=sidx, bk=bk: e.matmul(bk[:, h * 128:(h + 1) * 128],
                                                                      lhsT=cvs[:, 512 + h * 128:512 + (h + 1) * 128],
                                                                      rhs=um[:, sidx, h * 128:(h + 1) * 128], start=True, stop=True),
                     reads=[r_cvs, r_um], writes=[rb], acc=(h > 0))
            for h in range(4):
                sh = sidx * 4 + h
                S.op("dve", lambda e, h=h, sh=sh, bk=bk: e.scalar_tensor_tensor(
                    out=Ssb[:, sh, :], in0=Ssb[:, sh, :], scalar=dec128[:, sh:sh + 1], in1=bk[:, h * 128:(h + 1) * 128],
                    op0=ALU.mult, op1=ALU.add), reads=[r_Ssb[sidx], r_dec, rb], writes=[r_Ssb[sidx]])
            S.dma("pool", sS_d[sidx, :, :, :].rearrange("h k v -> k h v"), Ssb[:, sidx * 4:(sidx + 1) * 4, :], r_Ssb[sidx],
                  reads=[r_Ssb[sidx]], final=True)
        S.barrier()
        S2.close()

        if sstage < 3:
            SA.close()
            return
        S3 = contextlib.ExitStack()
        Kc = S.sbuf("Kc", [128, 128, 64], F32, S3)
        Vc = Kc
        r_Kc = Res("Kc")
        r_Vc = r_Kc
        Kc32 = S.sbuf("Kc32", [32, 128, 64], F32, S3)
        Vc32 = Kc32
        r_Kc32 = Res("Kc32")
        r_Vc32 = r_Kc32
        for kh in range(2):
            S.dma("sp", Kc32[kh * 16:(kh + 1) * 16, :, :], ck_d[:, :, kh * 64:(kh + 1) * 64], r_Kc32, writes=[r_Kc32], acc=True)
        selm, r_selm = const_tile("selm", [32, 128], F32, cd["selm"][:], stack=S3)
        qsh, r_qsh = S.sbuf("qsh", [128, 64], F32, S3), Res("qsh")
        knew, r_knew = S.sbuf("knew", [128, 64], F32, S3), Res("knew")
        vnew, r_vnew = S.sbuf("vnew", [128, 64], F32, S3), Res("vnew")
        S.dma("sp", qsh[:, :], scr_q[:, :].rearrange("s (h d) -> (s h) d", h=8), r_qsh, reads=[r_scr], writes=[r_qsh])
        for sidx in range(NS):
            for kh in range(2):
                p0 = sidx * 8 + kh * 4
                S.dma("sp", knew[p0:p0 + 4, :], scr_kv[sidx, kh * 64:(kh + 1) * 64].partition_broadcast(4),
                      r_knew, reads=[r_scr], writes=[r_knew], acc=True)
                S.dma("sp", vnew[p0:p0 + 4, :], scr_kv[sidx, 128 + kh * 64:128 + (kh + 1) * 64].partition_broadcast(4),
                      r_vnew, reads=[r_scr], writes=[r_vnew], acc=True)
        biass, r_biass = const_tile("biass", [128, 129], F32, biass_d[:], stack=S3)
        sinksh, r_sinksh = const_tile("sinksh", [128, 1], F32, sinksh_d[:], stack=S3)
        scs, r_scs = S.sbuf("scs", [128, 132], F32, S3), Res("scs")
        ps_, r_ps = S.sbuf("ps", [128, 132], F32, S3), Res("ps")
        tq, r_tq = S.sbuf("tq", [128, 64], F32, S3), Res("tq")
        aos, r_aos = S.sbuf("aos", [128, 64], F32, S3), Res("aos")
        for jb in range(16):
            bk, rb = nb()
            S.op("pe", lambda e, jb=jb, bk=bk: e.matmul(bk[:, :], lhsT=selm[:, :],
                                                        rhs=Kc32[:, jb * 8:(jb + 1) * 8, :].rearrange("p a d -> p (a d)"),
                                                        start=True, stop=True), reads=[r_selm, r_Kc32], writes=[rb])
            S.op("dve", lambda e, jb=jb, bk=bk: e.tensor_tensor(
                out=Kc[:, jb * 8:(jb + 1) * 8, :], in0=bk[:, :].rearrange("p (a d) -> p a d", a=8),
                in1=qsh[:, :].unsqueeze(1).broadcast_to([128, 8, 64]), op=ALU.mult),
                reads=[rb, r_qsh], writes=[r_Kc], acc=(jb > 0))
        for kh in range(2):
            S.dma("sp", Vc32[kh * 16:(kh + 1) * 16, :, :], cv_d[:, :, kh * 64:(kh + 1) * 64], r_Vc32, writes=[r_Vc32],
                  acc=(kh > 0))
        S.op("dve", lambda e: e.tensor_reduce(out=scs[:, 0:128], in_=Kc[:, :, :], axis=AX.X, op=ALU.add),
             reads=[r_Kc], writes=[r_scs])
        S.op("dve", lambda e: e.tensor_tensor(out=tq[:, :], in0=qsh[:, :], in1=knew[:, :], op=ALU.mult),
             reads=[r_qsh, r_knew], writes=[r_tq])
        S.op("dve", lambda e: e.tensor_reduce(out=scs[:, 128:129], in_=tq[:, :], axis=AX.X, op=ALU.add),
             reads=[r_tq], writes=[r_scs], acc=True)
        S.op("dve", lambda e: e.scalar_tensor_tensor(out=scs[:, 0:129], in0=scs[:, 0:129], scalar=0.125, in1=biass[:, :],
                                                     op0=ALU.mult, op1=ALU.add), reads=[r_scs, r_biass], writes=[r_scs])
        st, rs = stat.next()
        S.op("dve", lambda e: e.tensor_reduce(out=st[:, 0:1], in_=scs[:, 0:129], axis=AX.X, op=ALU.max), reads=[r_scs], writes=[rs])
        S.op("dve", lambda e: e.tensor_tensor(out=st[:, 0:1], in0=st[:, 0:1], in1=sinksh[:, :], op=ALU.max),
             reads=[rs, r_sinksh], writes=[rs])
        S.op("dve", lambda e: e.tensor_scalar(out=st[:, 1:2], in0=st[:, 0:1], scalar1=-1.0, scalar2=None, op0=ALU.mult),
             reads=[rs], writes=[rs])
        S.op("act", lambda e: e.activation(out=ps_[:, 0:129], in_=scs[:, 0:129], func=AF.Exp, bias=st[:, 1:2],
                                           accum_out=st[:, 2:3]), reads=[r_scs, rs], writes=[r_ps, rs])
        S.op("act", lambda e: e.activation(out=st[:, 3:4], in_=sinksh[:, :], func=AF.Exp, bias=st[:, 1:2]),
             reads=[r_sinksh, rs], writes=[rs])
        S.op("dve", lambda e: e.tensor_tensor(out=st[:, 4:5], in0=st[:, 2:3], in1=st[:, 3:4], op=ALU.add), reads=[rs], writes=[rs])
        S.op("dve", lambda e: e.reciprocal(out=st[:, 5:6], in_=st[:, 4:5]), reads=[rs], writes=[rs])
        for jb in range(16):
            bk, rb = nb()
            S.op("pe", lambda e, jb=jb, bk=bk: e.matmul(bk[:, :], lhsT=selm[:, :],
                                                        rhs=Vc32[:, jb * 8:(jb + 1) * 8, :].rearrange("p a d -> p (a d)"),
                                                        start=True, stop=True), reads=[r_selm, r_Vc32], writes=[rb])
            S.op("dve", lambda e, jb=jb, bk=bk: e.tensor_tensor(
                out=Vc[:, jb * 8:(jb + 1) * 8, :], in0=bk[:, :].rearrange("p (a d) -> p a d", a=8),
                in1=ps_[:, jb * 8:(jb + 1) * 8].unsqueeze(2).broadcast_to([128, 8, 64]), op=ALU.mult),
                reads=[rb, r_ps], writes=[r_Vc], acc=(jb > 0))
        S.op("dve", lambda e: e.tensor_reduce(out=aos[:, :], in_=Vc[:, :, :].rearrange("p s d -> p d s"), axis=AX.X, op=ALU.add),
             reads=[r_Vc], writes=[r_aos])
        S.op("dve", lambda e: e.scalar_tensor_tensor(out=aos[:, :], in0=vnew[:, :], scalar=ps_[:, 128:129], in1=aos[:, :],
                                                     op0=ALU.mult, op1=ALU.add), reads=[r_vnew, r_ps, r_aos], writes=[r_aos])
        S.op("dve", lambda e: e.tensor_scalar(out=aos[:, :], in0=aos[:, :], scalar1=st[:, 5:6], scalar2=None, op0=ALU.mult),
             reads=[r_aos, rs], writes=[r_aos])
        r_sao = Res("scr_ao")
        S.dma("sp", scr_ao[:, :], aos[:, :], r_sao, reads=[r_aos], writes=[r_sao])
        aot, r_aot = S.sbuf("aot", [NS, 512], F32, S3), Res("aot")
        S.dma("sp", aot[:, :], scr_ao[:, :].rearrange("(s h) d -> s (h d)", h=8), r_aot, reads=[r_sao], writes=[r_aot])
        S.op("dve", lambda e: e.tensor_copy(out=mixs[:, 0:512], in_=aot[:, :]), reads=[r_aot], writes=[r_mixs], acc=True)
        mixTs, r_mixTs = S.sbuf("mixTs", [128, KC, NS], BF16, S3), Res("mixTs")
        bk, rb = nb()
        bkb = bk[:, :].bitcast(BF16)
        for kc in range(KC):
            S.op("pe", lambda e, kc=kc: e.transpose(bkb[:, kc * NS:(kc + 1) * NS], mixs[:, kc * 128:(kc + 1) * 128],
                                                    idb[0:NS, 0:NS]), reads=[r_mixs, r_idb], writes=[rb], acc=(kc > 0))
        S.op("act", lambda e: e.copy(out=mixTs[:, :, :], in_=bkb[:, 0:KC * NS].rearrange("p (k s) -> p k s", k=KC)),
             reads=[rb], writes=[r_mixTs])
        bk0, rb0 = nb()
        bk1, rb1 = nb()
        for n, (bk, rb) in enumerate(((bk0, rb0), (bk1, rb1))):
            for kc in range(KC):
                S.op("pe", lambda e, kc=kc, n=n, bk=bk: e.matmul(bk[0:NS, :], lhsT=mixTs[:, kc, :],
                                                                 rhs=w_out[:, kc, n * 512:(n + 1) * 512],
                                                                 start=(kc == 0), stop=(kc == KC - 1)),
                     reads=[r_mixTs, r_wout], writes=[rb], acc=(kc > 0))
        x1s, r_x1s = S.sbuf("x1s", [NS, D], F32, S3), Res("x1s")
        epilogue(bk0, rb0, bk1, rb1, g_post, r_gpost, xs[:, :], r_xs, x1s[:, :], r_x1s, n=NS)
        r_ys = Res("ys")
        S.dma("sp", ys_d[:, :], x1s[:, :], r_x1s, reads=[r_x1s], writes=[r_ys])
        state["r_ys"] = r_ys
        S.barrier()
        S3.close()
        SA.close()

    if do_sample:
        sample_phase()
    S.barrier()

    xring = Ring(S, "xt", 2, [128, D], F32, AC)
    def gbuf(name, shape, dt, st=None):
        return S.sbuf(name, shape, dt, st or A), Res(name)

    mixT_ring = Ring(S, "mixT", 2, [128, KC, 128], BF16, AC)
    x1_ring = None if xchg else Ring(S, "x1t", 1, [128, D], F32, AC)
    Sst = S.sbuf("Sst", [128, 4, 256], F32, AC)
    S_bf, r_Sbf = gbuf("S_bf", [128, 4, 256], BF16, AC)
    u_bf, r_u = gbuf("u_bf", [64, 4, 256], BF16, AC)
    o1, r_o1 = gbuf("o1", [64, 4, 128], F32, AC)
    o2, r_o2 = gbuf("o2", [64, 4, 128], F32, AC)
    sz, r_sz = gbuf("sz", [64, 4, 128], F32, AC)
    y_bf, r_ybf = gbuf("y_bf", [64, 4, 128], BF16, AC)
    r_S = [Res("S%d" % h) for h in range(4)]
    hTring = Ring(S, "hT", 3 if xchg else 2, [128, KC, 128], BF16, A)
    patt_ring = Ring(S, "patt", 1, [128, 768], F32, A)
    zab_ring = Ring(S, "zab", 2, [64, 2, 520], F32, A)
    xc_ring = Ring(S, "xc", 2, [128, 12, 131], F32, A)
    kT_ring = Ring(S, "kTr", 3, [64, 2, 128], BF16, A)
    v_ring = Ring(S, "vr", 3, [128, 128], BF16, A)

    acc_t = S.sbuf("convacc", [128, 12, 128], F32, A)
    r_acc = [Res("acc%d" % m) for m in range(12)]
    qkvT = acc_t
    r_qkvT = Res("qkvT")

    qTa = S.sbuf("qTa", [64, 8, 128], BF16, A)
    r_qTa = Res("qTa")
    sc = S.sbuf("sc", [128, 8, 256], F32, A)
    r_sc = [Res("sc%d" % i) for i in range(4)]
    pbf = S.sbuf("pbf", [128, 8, 256], BF16, A)
    r_pbf = [Res("pbf%d" % i) for i in range(8)]
    pT = S.sbuf("pT", [128, 16, 128], BF16, A)
    r_pT = [Res("pT0"), Res("pT1")]
    attn_o = S.sbuf("attn_o", [128, 512], BF16, A)
    r_attn_o = Res("attn_o")


    sqT = S.sbuf("sqT", [128, 8, 128], F32, A)
    r_sqTl = [Res("sqT")]
    gs, r_gs = gbuf("gsc", [64, 16, 8], F32)
    kt_bf, r_kt = gbuf("kt_bf", [64, 8, 128], BF16)
    kd_bf, r_kd = gbuf("kd_bf", [64, 8, 128], BF16)
    kn_bf, r_kn = gbuf("kn_bf", [64, 8, 128], BF16)
    v_bf, r_vbf = gbuf("v_bf", [64, 8, 128], BF16)
    knT, r_knT = gbuf("knT", [128, 8, 64], BF16)
    qT_bf, r_qTbf = gbuf("qT_bf", [128, 4, 128], BF16)
    gM, r_gM = gbuf("gM", [64, 8, 64], F32)
    Emat, r_E = gbuf("Emat", [64, 8, 64], F32)
    mb, r_mb = gM, r_gM
    Es, r_Es = gbuf("Es", [64, 8, 64], F32)
    Ei, r_Ei = gbuf("Ei", [64, 8, 64], F32)
    Bp = [gbuf("Bp%d" % i, [64, 8, 64], F32) for i in range(2)]
    Ap = [gbuf("Ap%d" % i, [64, 8, 64], F32) for i in range(2)]
    Zm, r_Z = Emat, r_E
    dgb, r_dgb = gM, r_gM
    qkT, r_qkT = gbuf("qkT", [64, 8, 64], BF16)
    NTb, r_NTb = gbuf("NTb", [64, 8, 64], BF16)
    ub, r_ub = gbuf("ub", [64, 8, 128], F32)
    wT_bf, r_wT = gbuf("wT_bf", [128, 8, 64], BF16)
    osq, r_osq = o2, r_o2

    S.op("dve", lambda e: e.memset(gs[:, :, :], 0.0), writes=[r_gs])
    S.op("dve", lambda e: e.memset(Sst[:], 0.0), writes=r_S)
    if xchg:
        S.op("dve", lambda e: e.tensor_copy(out=Sst[:, :, 128:256], in_=idf[:, :].unsqueeze(1).broadcast_to([128, 4, 128])),
             reads=[r_idf] + r_S, writes=r_S)
    S.op("act", lambda e: e.copy(out=S_bf[:, :, :], in_=Sst[:, :, :]), reads=r_S, writes=[r_Sbf])


    def load_x(t):
        xt, rx = xring.next()
        S.dma("sp", xt[:], xin[(t + npre) * 128:(t + npre + 1) * 128, :], rx, writes=[rx])
        return xt, rx

    def inproj_feat(hT, r_hT, xc, r_xc, tok_lo, ntok, out_off, groups=(0, 1, 2)):
        for grp in groups:
            bk, rb = nb()
            first = True
            for mm in range(4):
                m = grp * 4 + mm
                for kc in range(KC):
                    S.op("pe", lambda e, kc=kc, m=m, mm=mm: e.matmul(
                        bk[:, mm * 128:mm * 128 + ntok], lhsT=w_in[:, kc, 768 + m * 128:768 + (m + 1) * 128],
                        rhs=hT[:, kc, tok_lo:tok_lo + ntok], start=(kc == 0), stop=(kc == KC - 1)),
                        reads=[r_hT, r_win[kc]], writes=[rb], acc=not first)
                    first = False
            S.op("act", lambda e, grp=grp: e.copy(
                out=xc[:, grp * 4:(grp + 1) * 4, out_off:out_off + ntok],
                in_=bk[:, :].rearrange("p (m t) -> p m t", m=4)[:, :, 0:ntok]),
                reads=[rb], writes=[r_xc], acc=True)

    def kv_prep(patt, r_patt):
        kT, r_kT = kT_ring.next()
        vv, r_v = v_ring.next()
        bk, rb = nb()
        for kh in range(2):
            S.op("pe", lambda e, kh=kh: e.transpose(bk[0:64, kh * 128:(kh + 1) * 128],
                                                    patt[:, 512 + kh * 64:512 + (kh + 1) * 64], idf[:, :]),
                 reads=[r_patt, r_idf], writes=[rb], acc=(kh > 0))
        S.op("act", lambda e: e.copy(out=kT[:, :, :], in_=bk[0:64, 0:256].rearrange("p (k t) -> p k t", k=2)),
             reads=[rb], writes=[r_kT])
        S.op("dve", lambda e: e.tensor_copy(out=vv[:, :], in_=patt[:, 640:768]), reads=[r_patt], writes=[r_v])
        return (kT, r_kT, vv, r_v)

    def attention(patt, r_patt, prev_kv, cur_kv, mixT, r_mixT):
        kTp, r_kTp, vp, r_vp = prev_kv
        kTc, r_kTc, vc, r_vc = cur_kv
        for half in range(2):
            bk, rb = nb()
            for hh in range(4):
                h = half * 4 + hh
                S.op("pe", lambda e, h=h, hh=hh: e.transpose(bk[0:64, hh * 128:(hh + 1) * 128],
                                                             patt[:, h * 64:(h + 1) * 64], idf[:, :]),
                     reads=[r_patt, r_idf], writes=[rb], acc=(hh > 0))
            S.op("act", lambda e, half=half: e.activation(
                out=qTa[:, half * 4:(half + 1) * 4, :], in_=bk[0:64, :].rearrange("p (h t) -> p h t", h=4),
                func=AF.Copy, scale=0.125), reads=[rb], writes=[r_qTa], acc=(half > 0))
        yield
        for pr in range(4):
            bk, rb = nb()
            first = True
            for hh in range(2):
                h = pr * 2 + hh
                kh = h // 4
                S.op("pe", lambda e, h=h, hh=hh, kh=kh: e.matmul(bk[:, hh * 256:hh * 256 + 128], lhsT=qTa[:, h, :],
                                                                 rhs=kTp[:, kh, :], start=True, stop=True),
                     reads=[r_qTa, r_kTp], writes=[rb], acc=not first)
                first = False
                S.op("pe", lambda e, h=h, hh=hh, kh=kh: e.matmul(bk[:, hh * 256 + 128:hh * 256 + 256], lhsT=qTa[:, h, :],
                                                                 rhs=kTc[:, kh, :], start=True, stop=True),
                     reads=[r_qTa, r_kTc], writes=[rb], acc=True)
            S.op("dve", lambda e, pr=pr: e.tensor_tensor(
                out=sc[:, pr * 2:(pr + 1) * 2, :], in0=bk[:, :].rearrange("p (h s) -> p h s", h=2),
                in1=bias[:, pr * 2:(pr + 1) * 2, :], op=ALU.add), reads=[rb, r_bias], writes=[r_sc[pr]])
        yield
        st, rs = stat.next()
        S.op("dve", lambda e: e.tensor_reduce(out=st[:, 0:8], in_=sc[:, :, :], axis=AX.X, op=ALU.max),
             reads=r_sc, writes=[rs])
        yield
        S.op("dve", lambda e: e.scalar_tensor_tensor(out=st[:, 8:16], in0=st[:, 0:8], scalar=-1.0, in1=nsink[:, :],
                                                     op0=ALU.mult, op1=ALU.min), reads=[rs, r_nsink], writes=[rs])
        yield
        st2, rs2 = stat.next()
        for h in range(8):
            S.op("act", lambda e, h=h: e.activation(out=pbf[:, h, :], in_=sc[:, h, :], func=AF.Exp,
                                                    bias=st[:, 8 + h:9 + h], accum_out=st2[:, h:h + 1]),
                 reads=[r_sc[h // 2], rs], writes=[r_pbf[h], rs2], acc=(h > 0))
        yield
        S.op("dve", lambda e: e.tensor_tensor(out=st[:, 0:8], in0=st[:, 8:16], in1=sink[:, :], op=ALU.add),
             reads=[rs, r_sink], writes=[rs])
        yield
        S.op("act", lambda e: e.activation(out=st2[:, 8:16], in_=st[:, 0:8], func=AF.Exp), reads=[rs], writes=[rs2], acc=True)
        yield
        S.op("dve", lambda e: e.tensor_tensor(out=st2[:, 0:8], in0=st2[:, 0:8], in1=st2[:, 8:16], op=ALU.add),
             reads=[rs2], writes=[rs2])
        yield
        S.op("dve", lambda e: e.reciprocal(out=st2[:, 8:16], in_=st2[:, 0:8]), reads=[rs2], writes=[rs2])
        yield
        for half in range(2):
            bk, rb = nb()
            bkb = bk[:, :].bitcast(BF16)
            first = True
            for hh in range(4):
                h = half * 4 + hh
                for sh in range(2):
                    idx = hh * 2 + sh
                    S.op("pe", lambda e, h=h, sh=sh, idx=idx: e.transpose(
                        bkb[:, idx * 128:(idx + 1) * 128], pbf[:, h, sh * 128:(sh + 1) * 128], idb[:, :]),
                        reads=[r_pbf[h], r_idb], writes=[rb], acc=not first)
                    first = False
            eng = "act" if half == 0 else "dve"
            if eng == "act":
                S.op("act", lambda e, half=half: e.copy(out=pT[:, half * 8:(half + 1) * 8, :],
                                                        in_=bkb.rearrange("p (i q) -> p i q", i=8)),
                     reads=[rb], writes=[r_pT[half]])
            else:
                S.op("dve", lambda e, half=half: e.tensor_copy(out=pT[:, half * 8:(half + 1) * 8, :],
                                                               in_=bkb.rearrange("p (i q) -> p i q", i=8)),
                     reads=[rb], writes=[r_pT[half]])
        yield
        bk, rb = nb()
        first = True
        for h in range(8):
            kh = h // 4
            S.op("pe", lambda e, h=h, kh=kh: e.matmul(bk[:, h * 64:(h + 1) * 64], lhsT=pT[:, h * 2, :],
                                                      rhs=vp[:, kh * 64:(kh + 1) * 64], start=True, stop=False),
                 reads=[r_pT[h // 4], r_vp], writes=[rb], acc=not first)
            first = False
            S.op("pe", lambda e, h=h, kh=kh: e.matmul(bk[:, h * 64:(h + 1) * 64], lhsT=pT[:, h * 2 + 1, :],
                                                      rhs=vc[:, kh * 64:(kh + 1) * 64], start=False, stop=True),
                 reads=[r_pT[h // 4], r_vc], writes=[rb], acc=True)
        yield
        S.op("dve", lambda e: e.tensor_tensor(
            out=attn_o[:, :].rearrange("p (h d) -> p h d", h=8), in0=bk[:, :].rearrange("p (h d) -> p h d", h=8),
            in1=st2[:, 8:16].unsqueeze(2).broadcast_to([128, 8, 64]), op=ALU.mult),
            reads=[rb, rs2], writes=[r_attn_o])
        yield
        bk, rb = nb()
        bkb = bk[:, :].bitcast(BF16)
        for c4 in range(4):
            S.op("pe", lambda e, c4=c4: e.transpose(bkb[:, c4 * 128:(c4 + 1) * 128], attn_o[:, c4 * 128:(c4 + 1) * 128],
                                                    idb[:, :]), reads=[r_attn_o, r_idb], writes=[rb], acc=(c4 > 0))
        yield
        S.op("act", lambda e: e.copy(out=mixT[:, 0:4, :], in_=bkb[:, 0:512].rearrange("p (c t) -> p c t", c=4)),
             reads=[rb], writes=[r_mixT], acc=True)
        yield

    wT_bf_g, r_wT_g, qT_bf_g, r_qTbf_g, kd_bf_g, r_kd_g = wT_bf, r_wT, qT_bf, r_qTbf, kd_bf, r_kd
    ub_g, r_ub_g, qkT_g, r_qkT_g, gs_g, r_gs_g = ub, r_ub, qkT, r_qkT, gs, r_gs

    def gdn_prep(xc, r_xc, zab, r_zab, t, full):
        yield from gdn_prep1(xc, r_xc, zab, r_zab, t, full)
        yield from gdn_prep2(t, full)

    def par(gens, pools):
        gl = [(g, pools[i]) for i, g in enumerate(gens) if g is not None]
        outer = bstate["pool"]
        while gl:
            for item in list(gl):
                g, pl = item
                bstate["pool"] = pl
                try:
                    next(g)
                    bstate["pool"] = outer
                    yield
                except StopIteration:
                    gl.remove(item)
        bstate["pool"] = outer

    def gdn(xc, r_xc, zab, r_zab, mixT, r_mixT, t, full=True, aug=False, prepq=False):
        yield from gdn_prep(xc, r_xc, zab, r_zab, t, full or prepq)
        yield from gdn_scan(zab, r_zab, mixT, r_mixT, t, full, aug)

    def gsv(i):
        return gs[:, i, :]

    def bc_u(ap8, n):
        return ap8.unsqueeze(2).broadcast_to([64, ap8.shape[1], n])

    def gdn_prep1(xc, r_xc, zab, r_zab, t, full):
        m0 = 0 if full else 4
        for m in range(m0, 12):
            S.op("act", lambda e, m=m: e.activation(out=acc_t[:, m, :], in_=xc[:, m, 0:128], func=AF.Copy,
                                                    scale=convw[:, m, 0:1]), reads=[r_xc, r_convw], writes=[r_acc[m], r_qkvT], acc=(m > m0))
        yield
        for i in range(1, 4):
            for m in range(m0, 12):
                S.op("dve", lambda e, m=m, i=i: e.scalar_tensor_tensor(
                    out=acc_t[:, m, :], in0=xc[:, m, i:i + 128], scalar=convw[:, m, i:i + 1], in1=acc_t[:, m, :],
                    op0=ALU.mult, op1=ALU.add), reads=[r_xc, r_convw, r_acc[m]], writes=[r_acc[m]])
        yield
        S.op("act", lambda e: e.activation(out=qkvT[:, m0:12, :], in_=acc_t[:, m0:12, :], func=AF.Silu),
             reads=r_acc[m0:], writes=[r_qkvT] + r_acc[m0:])
        yield
        S.op("act", lambda e: e.activation(out=sqT[:, m0:8, :], in_=qkvT[:, m0:8, :], func=AF.Square),
             reads=[r_qkvT], writes=r_sqTl)
        yield
        bkq, rbq = nb()
        first = True
        for qk_ in range(0 if full else 1, 2):
            for c in range(2):
                for h in range(4):
                    col = qk_ * 8 + c * 4 + h
                    S.op("pe", lambda e, qk_=qk_, c=c, h=h, col=col: e.matmul(
                        bkq[0:64, col:col + 1], lhsT=sqT[:, qk_ * 4 + h, c * 64:(c + 1) * 64], rhs=ones[:, 0:1],
                        start=True, stop=True), reads=r_sqTl + [r_ones], writes=[rbq], acc=not first)
                    first = False
        yield
        q0 = 0 if full else 1
        S.op("act", lambda e: e.activation(out=gs[:, q0:2, :].rearrange("p a b -> p (a b)"), in_=bkq[0:64, q0 * 8:16],
                                           func=AF.Ln, bias=epsT[0:64, :], scale=1.0), reads=[rbq, r_eps], writes=[r_gs])
        yield
        S.op("act", lambda e: e.activation(out=gs[:, q0:2, :], in_=gs[:, q0:2, :], func=AF.Exp, scale=-0.5), reads=[r_gs], writes=[r_gs])
        yield
        a_ap = zab[:, :, 512:516]
        b_ap = zab[:, :, 516:520]

        def v8(i):
            return gs[:, i, :].rearrange("p (c h) -> p c h", c=2)
        S.op("dve", lambda e: e.tensor_tensor(out=v8(2), in0=a_ap, in1=dtb8[:, :].rearrange("p (c h) -> p c h", c=2),
                                              op=ALU.add), reads=[r_zab, r_dtb8], writes=[r_gs])
        yield
        S.op("act", lambda e: e.activation(out=gsv(3), in_=gsv(2), func=AF.Abs),
             reads=[r_gs], writes=[r_gs])
        yield
        S.op("act", lambda e: e.activation(out=gsv(3), in_=gsv(3), func=AF.Exp, scale=-1.0), reads=[r_gs], writes=[r_gs])
        yield
        S.op("act", lambda e: e.activation(out=gsv(3), in_=gsv(3), func=AF.Ln, bias=1.0), reads=[r_gs], writes=[r_gs])
        yield
        S.op("dve", lambda e: e.scalar_tensor_tensor(out=gsv(2), in0=gsv(2), scalar=0.0, in1=gsv(3),
                                                     op0=ALU.max, op1=ALU.add), reads=[r_gs], writes=[r_gs])
        yield
        S.op("dve", lambda e: e.tensor_tensor(out=gsv(2), in0=gsv(2), in1=negA8[:, :], op=ALU.mult),
             reads=[r_gs, r_negA8], writes=[r_gs])
        yield
        S.op("act", lambda e: e.activation(out=v8(3), in_=b_ap, func=AF.Exp, scale=-1.0), reads=[r_zab], writes=[r_gs])
        yield
        S.op("dve", lambda e: e.tensor_scalar(out=gsv(3), in0=gsv(3), scalar1=1.0, scalar2=None, op0=ALU.add),
             reads=[r_gs], writes=[r_gs])
        yield
        S.op("dve", lambda e: e.reciprocal(out=gsv(3), in_=gsv(3)), reads=[r_gs], writes=[r_gs])
        yield
        bkg, rbg = nb()
        S.op("pe", lambda e: e.matmul(bkg[0:64, 0:8], lhsT=m_le[:, :], rhs=gsv(2), start=True, stop=True),
             reads=[r_gs, r_mle], writes=[rbg])
        yield
        S.op("pe", lambda e: e.matmul(bkg[0:64, 8:16], lhsT=ones[0:64, 0:64], rhs=gsv(2), start=True, stop=True),
             reads=[r_gs, r_ones], writes=[rbg], acc=True)
        yield
        S.op("pe", lambda e: e.matmul(bkg[:, 16:24], lhsT=ones[0:64, :], rhs=gsv(2), start=True, stop=True),
             reads=[r_gs, r_ones], writes=[rbg], acc=True)
        yield
        S.op("act", lambda e: e.copy(out=gs[:, 4:6, :].rearrange("p a b -> p (a b)"), in_=bkg[0:64, 0:16]),
             reads=[rbg], writes=[r_gs])
        yield
        ld128, r_ld = stat.next()
        S.op("act", lambda e: e.activation(out=ld128[:, 0:8], in_=bkg[:, 16:24], func=AF.Exp), reads=[rbg], writes=[r_ld])
        yield
        S.op("act", lambda e: e.activation(out=gsv(6), in_=gsv(4), func=AF.Exp), reads=[r_gs], writes=[r_gs])
        yield
        S.op("dve", lambda e: e.tensor_tensor(out=gsv(7), in0=gsv(5), in1=gsv(4), op=ALU.subtract),
             reads=[r_gs], writes=[r_gs])
        yield
        S.op("act", lambda e: e.activation(out=gsv(7), in_=gsv(7), func=AF.Exp), reads=[r_gs], writes=[r_gs])
        yield
        S.op("dve", lambda e: e.tensor_tensor(out=gsv(8), in0=gsv(1), in1=gsv(6), op=ALU.mult), reads=[r_gs], writes=[r_gs])
        yield
        S.op("dve", lambda e: e.tensor_tensor(out=gsv(9), in0=gsv(1), in1=gsv(7), op=ALU.mult), reads=[r_gs], writes=[r_gs])
        yield
        if full:
            S.op("dve", lambda e: e.tensor_scalar(out=gsv(10), in0=gsv(0), scalar1=128.0 ** -0.5, scalar2=None, op0=ALU.mult),
                 reads=[r_gs], writes=[r_gs])
            S.op("dve", lambda e: e.tensor_tensor(out=gsv(11), in0=gsv(10), in1=gsv(6), op=ALU.mult), reads=[r_gs], writes=[r_gs])
        yield
        state["ld"] = (ld128, r_ld)

    def gdn_prep2(t, full):
        for c in range(2):
            bk, rb = nb()
            for h in range(4):
                S.op("pe", lambda e, c=c, h=h: e.transpose(bk[0:64, h * 128:(h + 1) * 128],
                                                           qkvT[:, 4 + h, c * 64:(c + 1) * 64], idf[:, :]),
                     reads=[r_qkvT, r_idf], writes=[rb], acc=(h > 0))
            bk3 = bk[0:64, :].rearrange("p (h d) -> p h d", h=4)
            us = slice(c * 4, (c + 1) * 4)
            S.op("dve", lambda e, bk3=bk3, us=us: e.tensor_tensor(out=kt_bf[:, us, :], in0=bk3, in1=bc_u(gs[:, 8, us], 128),
                                                                  op=ALU.mult), reads=[rb, r_gs], writes=[r_kt], acc=(c > 0))
            S.op("dve", lambda e, bk3=bk3, us=us: e.tensor_tensor(out=kd_bf[:, us, :], in0=bk3, in1=bc_u(gs[:, 9, us], 128),
                                                                  op=ALU.mult), reads=[rb, r_gs], writes=[r_kd], acc=(c > 0))
            S.op("dve", lambda e, bk3=bk3, us=us: e.tensor_tensor(out=kn_bf[:, us, :], in0=bk3, in1=bc_u(gs[:, 1, us], 128),
                                                                  op=ALU.mult), reads=[rb, r_gs], writes=[r_kn], acc=(c > 0))
        yield
        for c in range(2):
            bk, rb = nb()
            for h in range(4):
                S.op("pe", lambda e, c=c, h=h: e.transpose(bk[0:64, h * 128:(h + 1) * 128],
                                                           qkvT[:, 8 + h, c * 64:(c + 1) * 64], idf[:, :]),
                     reads=[r_qkvT, r_idf], writes=[rb], acc=(h > 0))
            S.op("act", lambda e, c=c, bk=bk: e.copy(out=v_bf[:, c * 4:(c + 1) * 4, :],
                                                     in_=bk[0:64, :].rearrange("p (h d) -> p h d", h=4)),
                 reads=[rb], writes=[r_vbf], acc=(c > 0))
        yield
        bk, rb = nb()
        bkb = bk[:, :].bitcast(BF16)
        for u in range(8):
            S.op("pe", lambda e, u=u: e.transpose(bkb[:, u * 64:(u + 1) * 64], kn_bf[:, u, :], idb[0:64, 0:64]),
                 reads=[r_kn, r_idb], writes=[rb], acc=(u > 0))
        yield
        S.op("act", lambda e: e.copy(out=knT[:, :, :], in_=bkb[:, 0:512].rearrange("p (u t) -> p u t", u=8)),
             reads=[rb], writes=[r_knT])
        yield
        if full:
            S.op("dve", lambda e: e.tensor_copy(out=qT_bf[:, :, :], in_=qkvT[:, 0:4, :]), reads=[r_qkvT], writes=[r_qTbf])
        yield
        S.op("dve", lambda e: e.tensor_tensor(out=gM[:, :, :], in0=m_gt[:, :].unsqueeze(1).broadcast_to([64, 8, 64]),
                                              in1=bc_u(gsv(2), 64), op=ALU.mult), reads=[r_mgt, r_gs], writes=[r_gM])
        yield
        bk, rb = nb()
        for u in range(8):
            S.op("pe", lambda e, u=u: e.matmul(bk[0:64, u * 64:(u + 1) * 64], lhsT=gM[:, u, :], rhs=m_le[:, :],
                                               start=True, stop=True), reads=[r_gM, r_mle], writes=[rb], acc=(u > 0))
        yield
        S.op("act", lambda e: e.activation(out=Emat[:, :, :], in_=bk[0:64, :].rearrange("p (u i) -> p u i", u=8),
                                           func=AF.Exp), reads=[rb], writes=[r_E])
        yield
        S.op("dve", lambda e: e.tensor_tensor(out=mb[:, :, :], in0=m_lt[:, :].unsqueeze(1).broadcast_to([64, 8, 64]),
                                              in1=bc_u(gsv(3), 64), op=ALU.mult), reads=[r_mlt, r_gs], writes=[r_mb])
        yield
        S.op("dve", lambda e: e.tensor_tensor(out=Es[:, :, :], in0=Emat[:, :, :], in1=mb[:, :, :], op=ALU.mult),
             reads=[r_E, r_mb], writes=[r_Es])
        yield
        if full:
            S.op("dve", lambda e: e.tensor_tensor(out=Ei[:, :, :], in0=Emat[:, :, :],
                                                  in1=m_le[:, :].unsqueeze(1).broadcast_to([64, 8, 64]), op=ALU.mult),
                 reads=[r_E, r_mle], writes=[r_Ei])
        yield
        bk, rb = nb()
        for u in range(8):
            S.op("pe", lambda e, u=u: e.matmul(bk[0:64, u * 64:(u + 1) * 64], lhsT=knT[:, u, :], rhs=knT[:, u, :],
                                               start=True, stop=True), reads=[r_knT], writes=[rb], acc=(u > 0))
        yield
        B0, rB0 = Bp[0]
        S.op("dve", lambda e: e.tensor_tensor(out=B0[:, :, :], in0=bk[0:64, :].rearrange("p (u i) -> p u i", u=8),
                                              in1=Es[:, :, :], op=ALU.mult), reads=[rb, r_Es], writes=[rB0])
        yield
        if full:
            bk, rb = nb()
            for u in range(8):
                c, h = u // 4, u % 4
                S.op("pe", lambda e, u=u, c=c, h=h, bk=bk: e.matmul(bk[0:64, u * 64:(u + 1) * 64], lhsT=knT[:, u, :],
                                                                    rhs=qT_bf[:, h, c * 64:(c + 1) * 64], start=True, stop=True),
                     reads=[r_knT, r_qTbf], writes=[rb], acc=(u > 0))
            S.op("dve", lambda e, bk=bk: e.tensor_tensor(out=qkT[:, :, :], in0=bk[0:64, :].rearrange("p (u i) -> p u i", u=8),
                                                         in1=Ei[:, :, :], op=ALU.mult), reads=[rb, r_Ei], writes=[r_qkT])
        yield
        bk, rb = nb()
        for u in range(8):
            S.op("pe", lambda e, u=u: e.transpose(bk[0:64, u * 64:(u + 1) * 64], B0[:, u, :], idf[0:64, 0:64]),
                 reads=[rB0, r_idf], writes=[rb], acc=(u > 0))
        yield
        A0, rA0 = Ap[0]
        S.op("act", lambda e: e.copy(out=A0[:, :, :], in_=bk[0:64, :].rearrange("p (u i) -> p u i", u=8)),
             reads=[rb], writes=[rA0])
        yield
        S.op("dve", lambda e: e.scalar_tensor_tensor(out=Zm[:, :, :], in0=A0[:, :, :], scalar=-1.0,
                                                     in1=idf[0:64, 0:64].unsqueeze(1).broadcast_to([64, 8, 64]),
                                                     op0=ALU.mult, op1=ALU.add), reads=[rA0, r_idf], writes=[r_Z])
        yield
        cur = 0
        for lvl in range(5):
            Bc, rBc = Bp[cur]
            Ac, rAc = Ap[cur]
            Bn, rBn = Bp[1 - cur]
            An, rAn = Ap[1 - cur]
            bkB, rbB = nb()
            for u in range(8):
                S.op("pe", lambda e, u=u: e.matmul(bkB[0:64, u * 64:(u + 1) * 64], lhsT=Ac[:, u, :], rhs=Bc[:, u, :],
                                                   start=True, stop=True), reads=[rAc, rBc], writes=[rbB], acc=(u > 0))
            if lvl < 4:
                bkA, rbA = nb()
                for u in range(8):
                    S.op("pe", lambda e, u=u: e.matmul(bkA[0:64, u * 64:(u + 1) * 64], lhsT=Bc[:, u, :], rhs=Ac[:, u, :],
                                                       start=True, stop=True), reads=[rAc, rBc], writes=[rbA], acc=(u > 0))
            S.op("act", lambda e: e.copy(out=Bn[:, :, :], in_=bkB[0:64, :].rearrange("p (u i) -> p u i", u=8)),
                 reads=[rbB], writes=[rBn])
            if lvl < 4:
                S.op("dve", lambda e: e.tensor_copy(out=An[:, :, :], in_=bkA[0:64, :].rearrange("p (u i) -> p u i", u=8)),
                     reads=[rbA], writes=[rAn])
            bkZ, rbZ = nb()
            for u in range(8):
                S.op("pe", lambda e, u=u: e.matmul(bkZ[0:64, u * 64:(u + 1) * 64], lhsT=Bn[:, u, :], rhs=Zm[:, u, :],
                                                   start=True, stop=True), reads=[rBn, r_Z], writes=[rbZ], acc=(u > 0))
            S.op("dve", lambda e: e.tensor_tensor(out=Zm[:, :, :], in0=bkZ[0:64, :].rearrange("p (u i) -> p u i", u=8),
                                                  in1=Zm[:, :, :], op=ALU.add), reads=[rbZ, r_Z], writes=[r_Z])
            cur = 1 - cur
        yield
        S.op("dve", lambda e: e.tensor_tensor(out=dgb[:, :, :], in0=idf[0:64, 0:64].unsqueeze(1).broadcast_to([64, 8, 64]),
                                              in1=bc_u(gsv(3), 64), op=ALU.mult), reads=[r_idf, r_gs], writes=[r_dgb])
        yield
        bk, rb = nb()
        for u in range(8):
            S.op("pe", lambda e, u=u: e.matmul(bk[0:64, u * 64:(u + 1) * 64], lhsT=Zm[:, u, :], rhs=dgb[:, u, :],
                                               start=True, stop=True), reads=[r_Z, r_dgb], writes=[rb], acc=(u > 0))
        yield
        S.op("act", lambda e: e.copy(out=NTb[:, :, :], in_=bk[0:64, :].rearrange("p (u i) -> p u i", u=8)),
             reads=[rb], writes=[r_NTb])
        yield
        for c in range(2):
            bk, rb = nb()
            for h in range(4):
                u = c * 4 + h
                S.op("pe", lambda e, u=u, h=h: e.matmul(bk[0:64, h * 128:(h + 1) * 128], lhsT=NTb[:, u, :], rhs=v_bf[:, u, :],
                                                        start=True, stop=True), reads=[r_NTb, r_vbf], writes=[rb], acc=(h > 0))
            S.op("act", lambda e, c=c, bk=bk: e.copy(out=ub[:, c * 4:(c + 1) * 4, :],
                                                     in_=bk[0:64, :].rearrange("p (h d) -> p h d", h=4)),
                 reads=[rb], writes=[r_ub], acc=(c > 0))
        yield
        bk, rb = nb()
        for u in range(8):
            S.op("pe", lambda e, u=u: e.matmul(bk[:, u * 64:(u + 1) * 64], lhsT=kt_bf[:, u, :], rhs=NTb[:, u, :],
                                               start=True, stop=True), reads=[r_kt, r_NTb], writes=[rb], acc=(u > 0))
        yield
        S.op("dve", lambda e: e.tensor_copy(out=wT_bf[:, :, :], in_=bk[:, :].rearrange("p (u i) -> p u i", u=8)),
             reads=[rb], writes=[r_wT])
        yield

    def gdn_scan(zab, r_zab, mixT, r_mixT, t, full, aug, BB=None):
        if BB is None:
            BB = dict(wT=(wT_bf_g, r_wT_g), qT=(qT_bf_g, r_qTbf_g), kd=(kd_bf_g, r_kd_g), ub=(ub_g, r_ub_g),
                      qkT=(qkT_g, r_qkT_g), gs=(gs_g, r_gs_g), ld=state["ld"])
        wT_bf, r_wT = BB["wT"]
        qT_bf, r_qTbf = BB["qT"]
        kd_bf, r_kd = BB["kd"]
        ub, r_ub = BB["ub"]
        qkT, r_qkT = BB["qkT"]
        gs, r_gs = BB["gs"]
        ld128, r_ld = BB["ld"]
        for c in range(2):
            us = slice(c * 4, (c + 1) * 4)
            if aug:
                bkas = [nb(), nb()]
                for h in range(4):
                    bk_, rb_ = bkas[h // 2]
                    S.op("pe", lambda e, h=h, c=c, bk_=bk_: e.matmul(bk_[0:64, (h % 2) * 256:(h % 2) * 256 + 256],
                                                                     lhsT=wT_bf[:, c * 4 + h, :], rhs=S_bf[:, h, 0:256],
                                                                     start=True, stop=True),
                         reads=[r_wT, r_Sbf], writes=[rb_], acc=(h % 2 > 0))
                yield
                for pr in range(2):
                    bk_, rb_ = bkas[pr]
                    b3 = bk_[0:64, :].rearrange("p (h w) -> p h w", h=2)
                    S.op("dve", lambda e, pr=pr, b3=b3, c=c: e.tensor_tensor(
                        out=u_bf[:, pr * 2:pr * 2 + 2, 0:128], in0=ub[:, c * 4 + pr * 2:c * 4 + pr * 2 + 2, :],
                        in1=b3[:, :, 0:128], op=ALU.subtract), reads=[r_ub, rb_], writes=[r_u], acc=(pr > 0))
                    S.op("dve", lambda e, pr=pr, b3=b3: e.tensor_scalar(
                        out=u_bf[:, pr * 2:pr * 2 + 2, 128:256], in0=b3[:, :, 128:256], scalar1=-1.0, scalar2=None,
                        op0=ALU.mult), reads=[rb_], writes=[r_u], acc=True)
                yield
                bkss = [nb(), nb()]
                for h in range(4):
                    bk_, rb_ = bkss[h // 2]
                    S.op("pe", lambda e, h=h, c=c, bk_=bk_: e.matmul(bk_[:, (h % 2) * 256:(h % 2) * 256 + 256],
                                                                     lhsT=kd_bf[:, c * 4 + h, :], rhs=u_bf[:, h, 0:256],
                                                                     start=True, stop=True),
                         reads=[r_kd, r_u], writes=[rb_], acc=(h % 2 > 0))
                yield
                for h in range(4):
                    bk_, rb_ = bkss[h // 2]
                    S.op("dve", lambda e, h=h, c=c, bk_=bk_: e.scalar_tensor_tensor(
                        out=Sst[:, h, 0:256], in0=Sst[:, h, 0:256], scalar=ld128[:, c * 4 + h:c * 4 + h + 1],
                        in1=bk_[:, (h % 2) * 256:(h % 2) * 256 + 256], op0=ALU.mult, op1=ALU.add),
                        reads=[r_S[h], r_ld, rb_], writes=[r_S[h]])
                yield
                S.op("act", lambda e: e.copy(out=S_bf[:, :, :], in_=Sst[:, :, :]), reads=r_S, writes=[r_Sbf])
                yield
                continue
            bka, rba = nb()
            for h in range(4):
                S.op("pe", lambda e, h=h, c=c: e.matmul(bka[0:64, h * 128:(h + 1) * 128], lhsT=wT_bf[:, c * 4 + h, :],
                                                        rhs=S_bf[:, h, 0:128], start=True, stop=True),
                     reads=[r_wT, r_Sbf], writes=[rba], acc=(h > 0))
            yield
            if full:
                bko, rbo = nb()
                for h in range(4):
                    S.op("pe", lambda e, h=h, c=c: e.matmul(bko[0:64, h * 128:(h + 1) * 128], lhsT=qT_bf[:, h, c * 64:(c + 1) * 64],
                                                            rhs=S_bf[:, h, 0:128], start=True, stop=True),
                         reads=[r_qTbf, r_Sbf], writes=[rbo], acc=(h > 0))
            yield
            S.op("dve", lambda e, us=us: e.tensor_tensor(out=u_bf[:, :, 0:128], in0=ub[:, us, :],
                                                         in1=bka[0:64, :].rearrange("p (h d) -> p h d", h=4), op=ALU.subtract),
                 reads=[r_ub, rba], writes=[r_u])
            yield
            if full:
                bk2, rb2 = nb()
                for h in range(4):
                    S.op("pe", lambda e, h=h, c=c: e.matmul(bk2[0:64, h * 128:(h + 1) * 128], lhsT=qkT[:, c * 4 + h, :],
                                                            rhs=u_bf[:, h, 0:128], start=True, stop=True),
                         reads=[r_qkT, r_u], writes=[rb2], acc=(h > 0))
            yield
            bks, rbs = nb()
            for h in range(4):
                S.op("pe", lambda e, h=h, c=c: e.matmul(bks[:, h * 128:(h + 1) * 128], lhsT=kd_bf[:, c * 4 + h, :],
                                                        rhs=u_bf[:, h, 0:128], start=True, stop=True),
                     reads=[r_kd, r_u], writes=[rbs], acc=(h > 0))
            yield
            for h in range(4):
                S.op("dve", lambda e, h=h, c=c: e.scalar_tensor_tensor(
                    out=Sst[:, h, 0:128], in0=Sst[:, h, 0:128], scalar=ld128[:, c * 4 + h:c * 4 + h + 1],
                    in1=bks[:, h * 128:(h + 1) * 128], op0=ALU.mult, op1=ALU.add),
                    reads=[r_S[h], r_ld, rbs], writes=[r_S[h]])
            yield
            S.op("act", lambda e: e.copy(out=S_bf[:, :, 0:128], in_=Sst[:, :, 0:128]), reads=r_S, writes=[r_Sbf])
            yield
            if not full:
                continue
            S.op("dve", lambda e, us=us: e.tensor_tensor(out=o1[:, :, :], in0=bko[0:64, :].rearrange("p (h d) -> p h d", h=4),
                                                         in1=bc_u(gs[:, 11, us], 128), op=ALU.mult),
                 reads=[rbo, r_gs], writes=[r_o1])
            yield
            S.op("dve", lambda e, us=us: e.tensor_tensor(out=o2[:, :, :], in0=bk2[0:64, :].rearrange("p (h d) -> p h d", h=4),
                                                         in1=bc_u(gs[:, 10, us], 128), op=ALU.mult),
                 reads=[rb2, r_gs], writes=[r_o2])
            yield
            S.op("dve", lambda e: e.tensor_tensor(out=o1[:, :, :], in0=o1[:, :, :], in1=o2[:, :, :], op=ALU.add),
                 reads=[r_o1, r_o2], writes=[r_o1])
            yield
            if "gdn_o" in dbg and t == 0 and c == 0:
                dbg_out("gdn_o", o1[:, :, :], r_o1, [64, 4, 128])
            S.op("dve", lambda e: e.tensor_tensor(out=osq[:, :, :], in0=o1[:, :, :], in1=o1[:, :, :], op=ALU.mult),
                 reads=[r_o1], writes=[r_osq])
            yield
            st, rs = stat.next()
            S.op("dve", lambda e: e.tensor_reduce(out=st[0:64, 0:4], in_=osq[:, :, :], axis=AX.X, op=ALU.add),
                 reads=[r_osq], writes=[rs])
            yield
            S.op("act", lambda e: e.activation(out=st[0:64, 4:8], in_=st[0:64, 0:4], func=AF.Ln, bias=epsT[0:64, :],
                                               scale=1.0 / 128), reads=[rs, r_eps], writes=[rs])
            yield
            S.op("act", lambda e: e.activation(out=st[0:64, 8:12], in_=st[0:64, 4:8], func=AF.Exp, scale=-0.5), reads=[rs], writes=[rs])
            yield
            S.op("act", lambda e, c=c: e.activation(out=sz[:, :, :].rearrange("p h d -> p (h d)"), in_=zab[:, c, 0:512],
                                                    func=AF.Silu), reads=[r_zab], writes=[r_sz])
            yield
            S.op("dve", lambda e: e.tensor_tensor(out=sz[:, :, :], in0=sz[:, :, :],
                                                  in1=g_gdn[:, :].unsqueeze(1).broadcast_to([64, 4, 128]), op=ALU.mult),
                 reads=[r_sz, r_ggdn], writes=[r_sz])
            yield
            S.op("dve", lambda e: e.tensor_tensor(out=o1[:, :, :], in0=o1[:, :, :], in1=bc_u(st[0:64, 8:12], 128),
                                                  op=ALU.mult), reads=[r_o1, rs], writes=[r_o1])
            yield
            S.op("dve", lambda e: e.tensor_tensor(out=y_bf[:, :, :], in0=o1[:, :, :], in1=sz[:, :, :], op=ALU.mult),
                 reads=[r_o1, r_sz], writes=[r_ybf])
            yield
            bk, rb = nb()
            bkb = bk[:, :].bitcast(BF16)
            for h in range(4):
                S.op("pe", lambda e, h=h: e.transpose(bkb[:, h * 64:(h + 1) * 64], y_bf[:, h, :], idb[0:64, 0:64]),
                     reads=[r_ybf, r_idb], writes=[rb], acc=(h > 0))
            yield
            S.op("act", lambda e, c=c, bkb=bkb: e.copy(out=mixT[:, 4:8, c * 64:(c + 1) * 64],
                                                       in_=bkb[:, 0:256].rearrange("p (h t) -> p h t", h=4)),
                 reads=[rb], writes=[r_mixT], acc=True)
            yield

    def drive(*gens, weights=None, pools=None):
        gl = [(g, (weights[i] if weights else 1), (pools[i] if pools else "all")) for i, g in enumerate(gens) if g is not None]
        while gl:
            for item in list(gl):
                g, w, pl = item
                bstate["pool"] = pl
                try:
                    for _ in range(w):
                        next(g)
                except StopIteration:
                    gl.remove(item)
        bstate["pool"] = "all"

    ctx = {}

    def pre_front(t, light=False, halo=False, do_gdn=True):
        xt, rx = load_x(t)
        hT, r_hT = hTring.next()
        norm_transpose(xt[:, :], rx, g_pre, r_gpre, hT[:, :, :], r_hT)
        yield
        xc, r_xc = xc_ring.next()
        if light:
            inproj_feat(hT, r_hT, xc, r_xc, 125, 3, 128, groups=(1, 2))
            state["prev_xc"] = (xc, r_xc)
            return
        if state["prev_xc"] is not None:
            pxc, r_pxc = state["prev_xc"]
            S.op("dve", lambda e: e.tensor_copy(out=xc[:, 4:12, 0:3], in_=pxc[:, 4:12, 128:131]), reads=[r_pxc], writes=[r_xc])
        else:
            S.op("dve", lambda e: e.memset(xc[:, :, 0:3], 0.0), writes=[r_xc])
        if do_gdn:
            inproj_feat(hT, r_hT, xc, r_xc, 0, 128, 3, groups=(1,))
            yield
            inproj_feat(hT, r_hT, xc, r_xc, 0, 128, 3, groups=(2,))
            yield
        else:
            inproj_feat(hT, r_hT, xc, r_xc, 125, 3, 128, groups=(1, 2))
        if halo:
            inproj_feat(hT, r_hT, xc, r_xc, 125, 3, 128, groups=(0,))
            patt, r_patt = patt_ring.next()
            bk, rb = inproj_tok(hT, r_hT, 512, 256, 0, 128)
            S.op("act", lambda e: e.copy(out=patt[:, 512:768], in_=bk[:, 0:256]), reads=[rb], writes=[r_patt])
            state["prev_kv"] = kv_prep(patt, r_patt)
            yield
        state["prev_xc"] = (xc, r_xc)
        if do_gdn:
            zab, r_zab = zab_ring.next()
            for c in range(2):
                bk, rb = inproj_tok(hT, r_hT, 2816, 8, c * 64, 64)
                S.op("dve", lambda e, c=c, bk=bk: e.tensor_copy(out=zab[:, c, 512:520], in_=bk[0:64, 0:8]), reads=[rb],
                     writes=[r_zab], acc=True)
            ctx[("p", t)] = (xc, r_xc, zab, r_zab)
        yield

    def pre_back(t, aug):
        xc, r_xc, zab, r_zab = ctx.pop(("p", t))
        yield from gdn(xc, r_xc, zab, r_zab, None, None, t, full=False, aug=aug)

    def wout_epi(t, xt, rx, mixT, r_mixT):
        bk0, rb0 = nb()
        bk1, rb1 = nb()
        for n, (bk, rb) in enumerate(((bk0, rb0), (bk1, rb1))):
            for kc in range(KC):
                S.op("pe", lambda e, kc=kc, n=n, bk=bk: e.matmul(bk[:, :], lhsT=mixT[:, kc, :],
                                                                 rhs=w_out[:, kc, n * 512:(n + 1) * 512],
                                                                 start=(kc == 0), stop=(kc == KC - 1)),
                     reads=[r_mixT, r_wout], writes=[rb], acc=(kc > 0))
            yield
        x1, r_x1 = x1_ring.next()
        epilogue(bk0, rb0, bk1, rb1, g_post, r_gpost, xt[:, :], rx, x1[:, :], r_x1)
        ryp = state.setdefault("ry_pool", {})
        if (t % 4) not in ryp:
            ryp[t % 4] = Res("y%d" % (t % 4))
        ryt = ryp[t % 4]
        S.dma("pool", y_d[t * 128:(t + 1) * 128, :], x1[:, :], r_x1, reads=[r_x1], writes=[ryt], acc=True)
        state.setdefault("ry", {})[t] = ryt
        yield

    def pconv_out(hT, r_hT):
        pc = acc_t[:, :, :].rearrange("p m t -> p (m t)")
        r_pc = Res("pc")
        for n3 in range(3):
            bk, rb = inproj_tok(hT, r_hT, 768 + n3 * 512, 512, 0, 128)
            S.op("act", lambda e, n3=n3, bk=bk: e.copy(out=pc[:, n3 * 512:(n3 + 1) * 512], in_=bk[:, :]),
                 reads=[rb], writes=[r_pc, r_qkvT] + r_acc, acc=(n3 > 0))
        S.dma("sp", pconv_d[:, :], pc[125:128, 0:1536], r_pc, reads=[r_pc], final=True)

    def front_a(t):
        xt, rx = load_x(t)
        hT, r_hT = hTring.next()
        norm_transpose(xt[:, :], rx, g_pre, r_gpre, hT[:, :, :], r_hT)
        ctx[("a", t)] = (xt, rx, hT, r_hT)
        yield

    def own_front(t):
        if ("a", t) not in ctx:
            yield from front_a(t)
        xt, rx, hT, r_hT = ctx.pop(("a", t))
        patt, r_patt = patt_ring.next()
        bk, rb = inproj_tok(hT, r_hT, 0, 512, 0, 128)
        S.op("act", lambda e: e.copy(out=patt[:, 0:512], in_=bk[:, :]), reads=[rb], writes=[r_patt])
        yield
        bk, rb = inproj_tok(hT, r_hT, 512, 256, 0, 128)
        S.op("dve", lambda e: e.tensor_copy(out=patt[:, 512:768], in_=bk[:, 0:256]), reads=[rb], writes=[r_patt], acc=True)
        yield
        zab, r_zab = zab_ring.next()
        for c in range(2):
            bk, rb = inproj_tok(hT, r_hT, 2304, 512, c * 64, 64)
            S.op("act", lambda e, c=c, bk=bk: e.copy(out=zab[:, c, 0:512], in_=bk[0:64, :]), reads=[rb], writes=[r_zab], acc=True)
            bk, rb = inproj_tok(hT, r_hT, 2816, 8, c * 64, 64)
            S.op("dve", lambda e, c=c, bk=bk: e.tensor_copy(out=zab[:, c, 512:520], in_=bk[0:64, 0:8]), reads=[rb],
                 writes=[r_zab], acc=True)
            yield
        xc, r_xc = xc_ring.next()
        pxc, r_pxc = state["prev_xc"]
        S.op("dve", lambda e: e.tensor_copy(out=xc[:, :, 0:3], in_=pxc[:, :, 128:131]), reads=[r_pxc], writes=[r_xc])
        for grp in range(3):
            inproj_feat(hT, r_hT, xc, r_xc, 0, 128, 3, groups=(grp,))
            yield
        state["prev_xc"] = (xc, r_xc)
        if t == 0:
            dbg_out("patt", patt[:, :], r_patt, [128, 768])
            dbg_out("zab", zab[:, :, :], r_zab, [64, 2, 520])
            dbg_out("xc", xc[:, :, :], r_xc, [128, 12, 131])
        mixT, r_mixT = mixT_ring.next()
        cur_kv = kv_prep(patt, r_patt)
        yield
        yield from attention(patt, r_patt, state["prev_kv"], cur_kv, mixT, r_mixT)
        state["prev_kv"] = cur_kv
        if t == 0:
            S.dma("sp", bias[:], cd["biasN"][:], r_bias, writes=[r_bias])
        if t == nt - 1:
            S.dma("sp", pk_d[:, :], patt[:, 512:640], r_patt, reads=[r_patt], final=True)
            S.dma("sp", pv_d[:, :], patt[:, 640:768], r_patt, reads=[r_patt], final=True)
        ctx[("o", t)] = (xt, rx, hT, r_hT, zab, r_zab, xc, r_xc, mixT, r_mixT)
        yield

    def own_back(t):
        xt, rx, hT, r_hT, zab, r_zab, xc, r_xc, mixT, r_mixT = ctx.pop(("o", t))
        yield from gdn(xc, r_xc, zab, r_zab, mixT, r_mixT, t)
        if t == 0 and "mixT" in dbg:
            d = S.dram("dbg_mixT", [128, KC, 128], BF16, "ExternalOutput")
            dbg_outs["mixT"] = d
            S.dma("sp", d[:], mixT[:, :, :], r_mixT, reads=[r_mixT], final=True)
        yield from wout_epi(t, xt, rx, mixT, r_mixT)
        if t == nt - 1:
            pconv_out(hT, r_hT)
            for h in range(4):
                S.dma("sp", pS_d[h, :, :], Sst[:, h, 0:128], r_S[h], reads=[r_S[h]], final=True)

    state["prev_kv"] = None
    state["A_closed"] = False
    if xchg:
        drive(pre_front(-1, halo=True, do_gdn=False))
        r_sp = {}
        sp_pool = {}

        def back1(t):
            xt, rx, hT, r_hT, zab, r_zab, xc, r_xc, mixT, r_mixT = ctx.pop(("o", t))
            prev_scan = state.pop("pending_scan", None)
            yield from par([gdn_prep1(xc, r_xc, zab, r_zab, t, True), prev_scan], ["P", "C"])
            yield from gdn_prep2(t, True)
            ld128, r_ld = state["ld"]
            state["pending_scan"] = gdn_scan(None, None, None, None, t, False, True,
                                             dict(wT=(wT_bf, r_wT), qT=(qT_bf, r_qTbf), kd=(kd_bf, r_kd), ub=(ub, r_ub),
                                                  qkT=(qkT, r_qkT), gs=(gs, r_gs), ld=(ld128, r_ld)))
            if (t % 4) not in sp_pool:
                sp_pool[t % 4] = Res("sp%d" % (t % 4))
            rsp = sp_pool[t % 4]
            r_sp[t] = rsp
            srcs_ = []
            for dst, src, rr in ((sp_wT[t], wT_bf[:, :, :], r_wT), (sp_qT[t], qT_bf[:, :, :], r_qTbf),
                                 (sp_kd[t], kd_bf[:, :, :], r_kd), (sp_ub[t], ub[:, :, :], r_ub),
                                 (sp_qkT[t], qkT[:, :, :], r_qkT), (sp_gs[t], gs[:, :, :], r_gs),
                                 (sp_ld[t], ld128[:, 0:8], r_ld), (sp_zab[t], zab[:, :, :], r_zab),
                                 (sp_mix[t], mixT[:, 0:4, :], r_mixT)):
                S.dma("pool", dst, src, rsp, reads=[rr], writes=[rsp], acc=True)
                srcs_.append(rr)
            for rr in srcs_:
                rr.r[id(rsp.dsem)] = (rsp.dsem, rsp.dcnt, "dma")
            if t == nt - 1:
                pconv_out(hT, r_hT)
            yield

        def seq(*gens):
            for g in gens:
                if g is not None:
                    yield from g

        drive(front_a(0))
        for t in range(nt + 1):
            fr = seq(own_front(t), front_a(t + 1) if t + 1 < nt else None) if t < nt else None
            drive(fr, back1(t - 1) if t > 0 else None, weights=P1W, pools=("F", "P"))
        drive(state.pop("pending_scan"))
        r_xsrc, r_xdst = Res("xsrc"), Res("xdst")
        S.dma("sp", xsrc_d[:, :], Sst[:, :, :].rearrange("p h w -> p (h w)"), r_xsrc, reads=r_S, writes=[r_xsrc])
        S.cc("AllGather", xsrc_d[:, :], xdst_d[:, :], r_xdst, [[0, 1, 2, 3], [4, 5, 6, 7]], reads=[r_xsrc], writes=[r_xdst])
        Gt = sc[:, 0:4, :].rearrange("p a b -> p (a b)")
        r_G = r_sc[0]
        r_G2 = r_sc[1]
        sel128, r_sel = const_tile("sel128", [128, 4], F32, sel_d[:].partition_broadcast(128), stack=A)
        cs, r_cs = sc[:, 4:6, :].rearrange("p a (h d) -> p (a h) d", h=2), r_sc[2]
        PhiT, r_PhiT = sc[:, 6:8, :].rearrange("p a (h d) -> p (a h) d", h=2), r_sc[3]
        S.op("dve", lambda e: e.memset(Sst[:, :, 0:128], 0.0), writes=r_S)
        for i in range(3):
            S.dma("sp", Gt, xdst_d[i * 128:(i + 1) * 128, :], r_G, reads=[r_xdst], writes=[r_G, r_G2])
            Gi = Gt.rearrange("p (h w) -> p h w", h=4)
            if i == 0:
                S.op("dve", lambda e, Gi=Gi: e.tensor_copy(out=cs, in_=Gi[:, :, 0:128]), reads=[r_G, r_G2], writes=[r_cs])
            else:
                bkT, rbT = nb()
                for h in range(4):
                    S.op("pe", lambda e, h=h, Gi=Gi: e.transpose(bkT[:, h * 128:(h + 1) * 128], Gi[:, h, 128:256], idf[:, :]),
                         reads=[r_G, r_G2, r_idf], writes=[rbT], acc=(h > 0))
                S.op("act", lambda e: e.copy(out=PhiT, in_=bkT[:, :].rearrange("p (h d) -> p h d", h=4)),
                     reads=[rbT], writes=[r_PhiT])
                bkM, rbM = nb()
                for h in range(4):
                    S.op("pe", lambda e, h=h: e.matmul(bkM[:, h * 128:(h + 1) * 128], lhsT=PhiT[:, h, :], rhs=cs[:, h, :],
                                                       start=True, stop=True), reads=[r_PhiT, r_cs], writes=[rbM], acc=(h > 0))
                S.op("dve", lambda e, Gi=Gi: e.tensor_tensor(out=cs, in0=bkM[:, :].rearrange("p (h d) -> p h d", h=4),
                                                            in1=Gi[:, :, 0:128], op=ALU.add), reads=[rbM, r_G, r_G2], writes=[r_cs])
            S.op("dve", lambda e, i=i: e.scalar_tensor_tensor(out=Sst[:, :, 0:128], in0=cs, scalar=sel128[:, i + 1:i + 2],
                                                              in1=Sst[:, :, 0:128], op0=ALU.mult, op1=ALU.add),
                 reads=[r_cs, r_sel] + r_S, writes=r_S)
        S.op("act", lambda e: e.copy(out=S_bf[:, :, 0:128], in_=Sst[:, :, 0:128]), reads=r_S, writes=[r_Sbf])
        S.barrier()
        A.close()
        state["A_closed"] = True
        A2 = contextlib.ExitStack()
        x1_ring = Ring(S, "x1t", 1, [128, D], F32, A2)
        wT2 = Ring(S, "wT2", 2, [128, 8, 64], BF16, A2)
        qT2 = Ring(S, "qT2", 2, [128, 4, 128], BF16, A2)
        kd2 = Ring(S, "kd2", 2, [64, 8, 128], BF16, A2)
        ub2 = Ring(S, "ub2", 2, [64, 8, 128], F32, A2)
        qkT2 = Ring(S, "qkT2", 2, [64, 8, 64], BF16, A2)
        gs2 = Ring(S, "gs2", 2, [64, 16, 8], F32, A2)
        ld2 = Ring(S, "ld2", 2, [128, 8], F32, A2)
        zab2 = Ring(S, "zab2", 2, [64, 2, 520], F32, A2)

        def front2(t):
            xt, rx = load_x(t)
            BB = {}
            for key, ring, src in (("wT", wT2, sp_wT), ("qT", qT2, sp_qT), ("kd", kd2, sp_kd), ("ub", ub2, sp_ub),
                                   ("qkT", qkT2, sp_qkT), ("gs", gs2, sp_gs), ("ld", ld2, sp_ld), ("zab", zab2, sp_zab)):
                tl, rr = ring.next()
                S.dma("sp", tl[:], src[t], rr, reads=[r_sp[t]], writes=[rr])
                BB[key] = (tl, rr)
            mixT, r_mixT = mixT_ring.next()
            S.dma("sp", mixT[:, 0:4, :], sp_mix[t], r_mixT, reads=[r_sp[t]], writes=[r_mixT])
            ctx[("2", t)] = (xt, rx, BB, mixT, r_mixT)
            yield

        def chain2(t, c):
            xt, rx, BB, mixT, r_mixT = ctx[("2", t)]
            wT_bf, r_wT = BB["wT"]
            qT_bf, r_qTbf = BB["qT"]
            kd_bf, r_kd = BB["kd"]
            ub, r_ub = BB["ub"]
            qkT, r_qkT = BB["qkT"]
            ld128, r_ld = BB["ld"]
            us = slice(c * 4, (c + 1) * 4)
            bka, rba = nb()
            for h in range(4):
                S.op("pe", lambda e, h=h: e.matmul(bka[0:64, h * 128:(h + 1) * 128], lhsT=wT_bf[:, c * 4 + h, :],
                                                   rhs=S_bf[:, h, 0:128], start=True, stop=True),
                     reads=[r_wT, r_Sbf], writes=[rba], acc=(h > 0))
            yield
            bko, rbo = nb("C2")
            for h in range(4):
                S.op("pe", lambda e, h=h: e.matmul(bko[0:64, h * 128:(h + 1) * 128], lhsT=qT_bf[:, h, c * 64:(c + 1) * 64],
                                                   rhs=S_bf[:, h, 0:128], start=True, stop=True),
                     reads=[r_qTbf, r_Sbf], writes=[rbo], acc=(h > 0))
            yield
            S.op("dve", lambda e: e.tensor_tensor(out=u_bf[:, :, 0:128], in0=ub[:, us, :],
                                                  in1=bka[0:64, :].rearrange("p (h d) -> p h d", h=4), op=ALU.subtract),
                 reads=[r_ub, rba], writes=[r_u])
            yield
            bks, rbs = nb()
            for h in range(4):
                S.op("pe", lambda e, h=h: e.matmul(bks[:, h * 128:(h + 1) * 128], lhsT=kd_bf[:, c * 4 + h, :],
                                                   rhs=u_bf[:, h, 0:128], start=True, stop=True),
                     reads=[r_kd, r_u], writes=[rbs], acc=(h > 0))
            yield
            bk2, rb2 = nb("C2")
            for h in range(4):
                S.op("pe", lambda e, h=h: e.matmul(bk2[0:64, h * 128:(h + 1) * 128], lhsT=qkT[:, c * 4 + h, :],
                                                   rhs=u_bf[:, h, 0:128], start=True, stop=True),
                     reads=[r_qkT, r_u], writes=[rb2], acc=(h > 0))
            yield
            for h in range(4):
                S.op("dve", lambda e, h=h: e.scalar_tensor_tensor(
                    out=Sst[:, h, 0:128], in0=Sst[:, h, 0:128], scalar=ld128[:, c * 4 + h:c * 4 + h + 1],
                    in1=bks[:, h * 128:(h + 1) * 128], op0=ALU.mult, op1=ALU.add),
                    reads=[r_S[h], r_ld, rbs], writes=[r_S[h]])
                yield
            S.op("act", lambda e: e.copy(out=S_bf[:, :, 0:128], in_=Sst[:, :, 0:128]), reads=r_S, writes=[r_Sbf])
            ctx[("c", t, c)] = (bko, rbo, bk2, rb2)
            yield

        def outs2(t, c):
            xt, rx, BB, mixT, r_mixT = ctx[("2", t)]
            gs, r_gs = BB["gs"]
            zab, r_zab = BB["zab"]
            bko, rbo, bk2, rb2 = ctx.pop(("c", t, c))
            us = slice(c * 4, (c + 1) * 4)
            S.op("dve", lambda e: e.tensor_tensor(out=o1[:, :, :], in0=bko[0:64, :].rearrange("p (h d) -> p h d", h=4),
                                                  in1=bc_u(gs[:, 11, us], 128), op=ALU.mult),
                 reads=[rbo, r_gs], writes=[r_o1])
            yield
            S.op("dve", lambda e: e.tensor_tensor(out=o2[:, :, :], in0=bk2[0:64, :].rearrange("p (h d) -> p h d", h=4),
                                                  in1=bc_u(gs[:, 10, us], 128), op=ALU.mult),
                 reads=[rb2, r_gs], writes=[r_o2])
            yield
            S.op("act", lambda e: e.activation(out=sz[:, :, :].rearrange("p h d -> p (h d)"), in_=zab[:, c, 0:512],
                                               func=AF.Silu), reads=[r_zab], writes=[r_sz])
            yield
            S.op("dve", lambda e: e.tensor_tensor(out=o1[:, :, :], in0=o1[:, :, :], in1=o2[:, :, :], op=ALU.add),
                 reads=[r_o1, r_o2], writes=[r_o1])
            yield
            S.op("dve", lambda e: e.tensor_tensor(out=osq[:, :, :], in0=o1[:, :, :], in1=o1[:, :, :], op=ALU.mult),
                 reads=[r_o1], writes=[r_osq])
            yield
            st, rs = stat.next()
            S.op("dve", lambda e: e.tensor_reduce(out=st[0:64, 0:4], in_=osq[:, :, :], axis=AX.X, op=ALU.add),
                 reads=[r_osq], writes=[rs])
            yield
            S.op("act", lambda e: e.activation(out=st[0:64, 4:8], in_=st[0:64, 0:4], func=AF.Ln, bias=epsT[0:64, :],
                                               scale=1.0 / 128), reads=[rs, r_eps], writes=[rs])
            yield
            S.op("act", lambda e: e.activation(out=st[0:64, 8:12], in_=st[0:64, 4:8], func=AF.Exp, scale=-0.5), reads=[rs], writes=[rs])
            yield
            S.op("dve", lambda e: e.tensor_tensor(out=sz[:, :, :], in0=sz[:, :, :],
                                                  in1=g_gdn[:, :].unsqueeze(1).broadcast_to([64, 4, 128]), op=ALU.mult),
                 reads=[r_sz, r_ggdn], writes=[r_sz])
            yield
            S.op("dve", lambda e: e.tensor_tensor(out=o1[:, :, :], in0=o1[:, :, :], in1=bc_u(st[0:64, 8:12], 128),
                                                  op=ALU.mult), reads=[r_o1, rs], writes=[r_o1])
            yield
            S.op("dve", lambda e: e.tensor_tensor(out=y_bf[:, :, :], in0=o1[:, :, :], in1=sz[:, :, :], op=ALU.mult),
                 reads=[r_o1, r_sz], writes=[r_ybf])
            yield
            bk, rb = nb()
            bkb = bk[:, :].bitcast(BF16)
            for h in range(4):
                S.op("pe", lambda e, h=h: e.transpose(bkb[:, h * 64:(h + 1) * 64], y_bf[:, h, :], idb[0:64, 0:64]),
                     reads=[r_ybf, r_idb], writes=[rb], acc=(h > 0))
            yield
            S.op("act", lambda e: e.copy(out=mixT[:, 4:8, c * 64:(c + 1) * 64],
                                         in_=bkb[:, 0:256].rearrange("p (h t) -> p h t", h=4)),
                 reads=[rb], writes=[r_mixT], acc=True)
            yield
            if c == 1:
                ctx.pop(("2", t))
                yield from wout_epi(t, xt, rx, mixT, r_mixT)

        steps = [(t, c) for t in range(nt) for c in range(2)]
        drive(front2(0))
        for k in range(len(steps) + 1):
            gens = []
            pls = []
            wls = []
            if k < len(steps):
                t_, c_ = steps[k]
                gens.append(chain2(t_, c_))
                pls.append("C1")
                wls.append(1)
                if c_ == 1 and t_ + 1 < nt:
                    gens.append(front2(t_ + 1))
                    pls.append("all")
                    wls.append(1)
            if k > 0:
                gens.append(outs2(*steps[k - 1]))
                pls.append("O")
                wls.append(2)
            drive(*gens, pools=pls, weights=wls)
        for h in range(4):
            S.dma("sp", pS_d[h, :, :], Sst[:, h, 0:128], r_S[h], reads=[r_S[h]], final=True)
        S.barrier()
        A2.close()
    else:
        for t in range(-npre, 0):
            drive(pre_front(t, halo=(t == -1)))
            drive(pre_back(t, False))
        for t in range(nt + 1):
            drive(own_front(t) if t < nt else None, own_back(t - 1) if t > 0 else None)

    S.barrier()
    if not state["A_closed"]:
        A.close()
    AC.close()

    S.barrier()
    W.close()

    if do_ffn:
        Bs = contextlib.ExitStack()
        w_up = S.sbuf("w_up_bf", [128, KC, 4096], BF16, Bs)
        r_wup2 = [Res("w_up0"), Res("w_up1")]
        for hf in range(2):
            for kc in range(KC):
                S.dma("pool", w_up[:, kc, hf * 2048:(hf + 1) * 2048], w_up_d[:, kc, hf * 2048:(hf + 1) * 2048], r_wup2[hf],
                      writes=[r_wup2[hf]], acc=True)
        w_dn = S.sbuf("w_dn_bf", [128, 32, D], BF16, Bs)
        r_wdn4 = [Res("w_dn%d" % i) for i in range(4)]
        for fc in range(32):
            S.dma("pool", w_dn[:, fc, :], w_dn_d[:, fc, :], r_wdn4[fc // 8], writes=[r_wdn4[fc // 8]], acc=True)
        g_fpre, r_gfpre = const_tile("g_fpre", [128, D], F32, g_fpre_d[:].partition_broadcast(128), stack=Bs)
        g_fpost, r_gfpost = const_tile("g_fpost", [128, D], F32, g_fpost_d[:].partition_broadcast(128), stack=Bs)
        G = ffn_group
        NTOK = G * 128
        x1g_ring = Ring(S, "x1g", 2, [128, G, D], F32, Bs)
        h2T_ring = Ring(S, "h2T", 2, [128, KC, NTOK], BF16, Bs)
        u2T = S.sbuf("u2T", [128, 32, NTOK], BF16, Bs)
        r_u2T = [Res("u2T%d" % i) for i in range(32)]
        rl_ring = Ring(S, "rl", 3, [128, NTOK], F32, Bs)
        yo_ring = Ring(S, "yo", 2, [128, D], F32, Bs)
        for g in range(nt // G):
            x1g, r_x1g = x1g_ring.next()
            h2T, r_h2T = h2T_ring.next()
            for ti in range(G):
                t = g * G + ti
                S.dma("sp", x1g[:, ti, :], y_d[t * 128:(t + 1) * 128, :], r_x1g, reads=[state["ry"][t]], writes=[r_x1g],
                      acc=(ti > 0))
            for ti in range(G):
                state["ry"][g * G + ti].r[id(r_x1g.dsem)] = (r_x1g.dsem, r_x1g.dcnt, "dma")
            for ti in range(G):
                norm_transpose(x1g[:, ti, :], r_x1g, g_fpre, r_gfpre, h2T[:, :, ti * 128:(ti + 1) * 128], r_h2T)
            for fc in range(32):
                bk, rb = nb()
                for kc in range(KC):
                    S.op("pe", lambda e, kc=kc, fc=fc, bk=bk: e.matmul(bk[:, 0:NTOK], lhsT=w_up[:, kc, fc * 128:(fc + 1) * 128],
                                                                       rhs=h2T[:, kc, :], start=(kc == 0), stop=(kc == KC - 1)),
                         reads=[r_h2T, r_wup2[fc // 16]], writes=[rb], acc=(kc > 0))
                rl, r_rl = rl_ring.next()
                S.op("act", lambda e, bk=bk, rl=rl: e.activation(out=rl[:, :], in_=bk[:, 0:NTOK], func=AF.Relu),
                     reads=[rb], writes=[r_rl])
                S.op("dve", lambda e, fc=fc, rl=rl: e.tensor_tensor(out=u2T[:, fc, :], in0=rl[:, :], in1=rl[:, :], op=ALU.mult),
                     reads=[r_rl], writes=[r_u2T[fc]])
            for ti in range(G):
                t = g * G + ti
                bk0, rb0 = nb()
                bk1, rb1 = nb()
                for n, (bk, rb) in enumerate(((bk0, rb0), (bk1, rb1))):
                    for fc in range(32):
                        S.op("pe", lambda e, fc=fc, n=n, bk=bk, ti=ti: e.matmul(
                            bk[:, :], lhsT=u2T[:, fc, ti * 128:(ti + 1) * 128], rhs=w_dn[:, fc, n * 512:(n + 1) * 512],
                            start=(fc == 0), stop=(fc == 31)), reads=[r_u2T[fc], r_wdn4[fc // 8]], writes=[rb], acc=(fc > 0))
                yo, r_yo = yo_ring.next()
                epilogue(bk0, rb0, bk1, rb1, g_fpost, r_gfpost, x1g[:, ti, :], r_x1g, yo[:, :], r_yo)
                S.dma("pool", y_d[t * 128:(t + 1) * 128, :], yo[:, :], r_yo, reads=[r_yo], writes=[state["ry"][t]], final=True)
        if do_sample and state.get("r_ys") is not None:
            x1s2, r_x1s2 = S.sbuf("x1s2", [NS, D], F32, Bs), Res("x1s2")
            S.dma("sp", x1s2[:, :], ys_d[:, :], r_x1s2, reads=[state["r_ys"]], writes=[r_x1s2])
            h2Ts, r_h2Ts = S.sbuf("h2Ts", [128, KC, NS], BF16, Bs), Res("h2Ts")
            norm_transpose(x1s2[:, :], r_x1s2, g_fpre, r_gfpre, h2Ts[:, :, :], r_h2Ts, n=NS)
            for fc in range(32):
                bk, rb = nb()
                for kc in range(KC):
                    S.op("pe", lambda e, kc=kc, fc=fc, bk=bk: e.matmul(bk[:, 0:NS], lhsT=w_up[:, kc, fc * 128:(fc + 1) * 128],
                                                                       rhs=h2Ts[:, kc, :], start=(kc == 0), stop=(kc == KC - 1)),
                         reads=[r_h2Ts, r_wup2[fc // 16]], writes=[rb], acc=(kc > 0))
                rl, r_rl = rl_ring.next()
                S.op("act", lambda e, bk=bk, rl=rl: e.activation(out=rl[:, 0:NS], in_=bk[:, 0:NS], func=AF.Relu),
                     reads=[rb], writes=[r_rl])
                S.op("dve", lambda e, fc=fc, rl=rl: e.tensor_tensor(out=u2T[:, fc, 0:NS], in0=rl[:, 0:NS], in1=rl[:, 0:NS],
                                                                    op=ALU.mult), reads=[r_rl], writes=[r_u2T[fc]])
            bk0, rb0 = nb()
            bk1, rb1 = nb()
            for n, (bk, rb) in enumerate(((bk0, rb0), (bk1, rb1))):
                for fc in range(32):
                    S.op("pe", lambda e, fc=fc, n=n, bk=bk: e.matmul(
                        bk[0:NS, :], lhsT=u2T[:, fc, 0:NS], rhs=w_dn[:, fc, n * 512:(n + 1) * 512],
                        start=(fc == 0), stop=(fc == 31)), reads=[r_u2T[fc], r_wdn4[fc // 8]], writes=[rb], acc=(fc > 0))
            yso, r_yso = S.sbuf("yso", [NS, D], F32, Bs), Res("yso")
            epilogue(bk0, rb0, bk1, rb1, g_fpost, r_gfpost, x1s2[:, :], r_x1s2, yso[:, :], r_yso, n=NS)
            S.dma("sp", ys_d[:, :], yso[:, :], r_yso, reads=[r_yso], writes=[state["r_ys"]], final=True)
        S.barrier()
        Bs.close()
    S.finish()
    S.close()
    info = dict(nops=S.nops, nwaits=S.nwaits, ndma=S.ndma)
    return nc, dbg_outs, info


def _core_inputs(inp, consts, b, j, nt=NT, npre=NPRE):
    seg = nt * 128
    x = inp["x_prompt"][b]
    lo = j * seg
    xin = np.zeros(((npre + nt) * 128, D), np.float32)
    npz = min(lo, npre * 128)
    if npz > 0:
        xin[npre * 128 - npz:npre * 128] = x[lo - npz:lo]
    xin[npre * 128:] = x[lo:lo + seg]
    m = {"xin": xin}
    m.update(consts)
    if j != 0:
        m["bias0"] = consts["biasN"]
    core = b * 4 + j
    sel = np.zeros((1, 4), np.float32)
    sel[0, j] = 1.0
    m["sel"] = sel
    s0, s1 = core * 16, (core + 1) * 16
    m["xs"] = np.ascontiguousarray(inp["x_sample"][s0:s1, 0, :])
    m["cst"] = np.ascontiguousarray(inp["state_conv"][0, s0:s1])
    m["ck"] = np.ascontiguousarray(inp["cache_win_k"][0, s0:s1]).reshape(16, 128, 128)
    m["cv"] = np.ascontiguousarray(inp["cache_win_v"][0, s0:s1]).reshape(16, 128, 128)
    m["sst"] = np.ascontiguousarray(inp["state_gdn"][0, s0:s1])
    m["convwt"] = np.ascontiguousarray(inp["conv_w"][0]).reshape(1, 4 * 1536)
    m["sinksh"] = np.ascontiguousarray(np.tile(inp["attn_sinks"][0], 16).reshape(128, 1))
    return m


def _shared_inputs(inp):
    w_in = np.ascontiguousarray(inp["w_in"][0].reshape(KC, 128, INW).transpose(1, 0, 2))
    w_out = np.ascontiguousarray(inp["w_out"][0].reshape(KC, 128, D).transpose(1, 0, 2))
    w_up = np.ascontiguousarray(inp["w_up"][0].reshape(KC, 128, 4096).transpose(1, 0, 2))
    w_dn = np.ascontiguousarray(inp["w_down"][0].reshape(32, 128, D).transpose(1, 0, 2))
    convw = np.ascontiguousarray(inp["conv_w"][0].T.reshape(12, 128, 4).transpose(1, 0, 2))
    sh = {"w_in": w_in, "w_out": w_out, "w_up": w_up, "w_down": w_dn, "convw": convw,
          "g_pre": inp["norm_mix_pre"], "g_post": inp["norm_mix_post"], "g_fpre": inp["norm_ffn_pre"],
          "g_fpost": inp["norm_ffn_post"], "g_gdn": inp["gdn_norm"], "sinks": inp["attn_sinks"],
          "alog": inp["gdn_a_log"], "dtb": inp["gdn_dt_bias"]}
    return {k: np.ascontiguousarray(v, dtype=np.float32) for k, v in sh.items()}


def kernel(**inputs):
    inp = {k: np.asarray(v) for k, v in inputs.items()}
    consts = _consts()
    shared = _shared_inputs(inp)
    nc, _, _ = build_program()
    in_maps = []
    for core in range(8):
        b, j = core // 4, core % 4
        m = _core_inputs(inp, consts, b, j)
        m.update(shared)
        in_maps.append(m)
    res = run_bass_kernel_spmd(nc, in_maps, core_ids=list(range(8)))
    R = res.results
    y_prompt = np.zeros((2, 8192, D), np.float32)
    for core in range(8):
        b, j = core // 4, core % 4
        y_prompt[b, j * SEG:(j + 1) * SEG] = R[core]["y"]
    p_conv = np.stack([R[3]["p_conv"], R[7]["p_conv"]])[None]
    p_k = np.stack([R[3]["p_k"], R[7]["p_k"]]).reshape(1, 2, 128, 2, 64)
    p_v = np.stack([R[3]["p_v"], R[7]["p_v"]]).reshape(1, 2, 128, 2, 64)
    p_S = np.stack([R[3]["p_S"], R[7]["p_S"]])[None]
    y_sample = np.concatenate([R[c]["ys"] for c in range(8)], 0).reshape(128, 1, D)
    s_conv = np.concatenate([R[c]["s_conv"] for c in range(8)], 0)[None]
    s_k = np.concatenate([R[c]["s_k"] for c in range(8)], 0).reshape(1, 128, 128, 2, 64)
    s_v = np.concatenate([R[c]["s_v"] for c in range(8)], 0).reshape(1, 128, 128, 2, 64)
    s_S = np.concatenate([R[c]["s_S"] for c in range(8)], 0)[None]
    return (y_prompt, y_sample, p_conv, p_k, p_v, p_S, s_conv, s_k, s_v, s_S)
```

```python
import contextlib
import numpy as np
import ml_dtypes
import concourse.bass as bass
import concourse.mybir as mybir
from concourse.bass_utils import run_bass_kernel_spmd

F32 = mybir.dt.float32
BF16 = mybir.dt.bfloat16
AF = mybir.ActivationFunctionType
ALU = mybir.AluOpType
AX = mybir.AxisListType

D = 1024
KC = 8
NT = 16
SEG = NT * 128
INW = 2824
EPS = 1e-6
NEG = -30000.0
P1W = (1, 2)
XCHG = True
NPRE = 1


class Res:
    __slots__ = ("name", "w", "r", "dsem", "dcnt")

    def __init__(self, name):
        self.name = name
        self.w = []
        self.r = {}
        self.dsem = None
        self.dcnt = 0


class Eng:
    def __init__(self, name, h, sem):
        self.name = name
        self.h = h
        self.sem = sem
        self.cnt = 0
        self.waited = {}


class Sched:
    def __init__(self, nc):
        self.nc = nc
        self.stack = contextlib.ExitStack()
        self.eng = {}
        for name, h in (("pe", nc.tensor), ("dve", nc.vector), ("act", nc.scalar),
                        ("pool", nc.gpsimd), ("sp", nc.sync)):
            sem = self.stack.enter_context(nc.semaphore("s_" + name))
            self.eng[name] = Eng(name, h, sem)
        self.finals = {}
        self.dtoks = {}
        self.nops = {k: 0 for k in self.eng}
        self.nwaits = 0
        self.ndma = 0

    def sbuf(self, name, shape, dtype, stack=None):
        return (stack or self.stack).enter_context(self.nc.sbuf_tensor("sb_" + name, list(shape), dtype))

    def psum(self, name, shape, dtype):
        return self.stack.enter_context(self.nc.psum_tensor("ps_" + name, list(shape), dtype))

    def dram(self, name, shape, dtype, kind, **kw):
        return self.nc.dram_tensor(name, list(shape), dtype, kind=kind, **kw).ap()

    def _wait(self, E, tok):
        sem, val, src = tok
        k = id(sem)
        if E.waited.get(k, 0) >= val:
            return
        E.waited[k] = val
        E.h.wait_ge(sem, val)
        self.nwaits += 1

    def _deps(self, E, reads, writes, acc):
        eng = E.name
        for r in reads:
            for tok in r.w:
                if tok[2] == eng and eng == "pe":
                    continue
                self._wait(E, tok)
        for w in writes:
            for tok in list(w.r.values()):
                if tok[2] == eng and eng == "pe":
                    continue
                self._wait(E, tok)
            if not acc:
                for tok in w.w:
                    if tok[2] == eng and eng == "pe":
                        continue
                    self._wait(E, tok)

    def _commit(self, tok, reads, writes, acc):
        k = id(tok[0])
        for r in reads:
            old = r.r.get(k)
            if old is None or old[1] < tok[1]:
                r.r[k] = tok
        for w in writes:
            if acc:
                w.w = [t for t in w.w if id(t[0]) != k] + [tok]
            else:
                w.w = [tok]
                w.r = {}

    def op(self, eng, fn, reads=(), writes=(), acc=False):
        E = self.eng[eng]
        self._deps(E, reads, writes, acc)
        ins = fn(E.h)
        E.cnt += 1
        ins.then_inc(E.sem, 1)
        tok = (E.sem, E.cnt, eng)
        self._commit(tok, reads, writes, acc)
        self.nops[eng] += 1
        return ins

    def dma(self, eng, out, in_, dres, reads=(), writes=(), acc=False, final=False, **kw):
        E = self.eng[eng]
        self._deps(E, reads, writes, acc)
        if dres.dsem is None:
            dres.dsem = self.stack.enter_context(self.nc.semaphore("d_" + dres.name))
        ins = E.h.dma_start(out=out, in_=in_, **kw)
        dres.dcnt += 16
        ins.then_inc(dres.dsem, 16)
        tok = (dres.dsem, dres.dcnt, "dma")
        self._commit(tok, reads, writes, acc)
        self.dtoks[id(dres.dsem)] = tok
        if final:
            self.finals[id(dres.dsem)] = tok
        self.ndma += 1
        return ins

    def cc(self, kind, in_ap, out_ap, dres, groups, reads=(), writes=()):
        E = self.eng["pool"]
        self._deps(E, reads, writes, False)
        if dres.dsem is None:
            dres.dsem = self.stack.enter_context(self.nc.semaphore("d_" + dres.name))
        ins = E.h.collective_compute(kind, ALU.bypass, replica_groups=groups, ins=[in_ap], outs=[out_ap])
        dres.dcnt += 1
        ins.then_inc(dres.dsem, 1)
        tok = (dres.dsem, dres.dcnt, "dma")
        self._commit(tok, reads, writes, False)
        self.dtoks[id(dres.dsem)] = tok
        return ins

    def barrier(self):
        toks = [(e.sem, e.cnt, e.name) for e in self.eng.values() if e.cnt > 0]
        toks += list(self.dtoks.values())
        for E in self.eng.values():
            for tok in toks:
                if tok[2] == E.name:
                    continue
                self._wait(E, tok)

    def finish(self):
        E = self.eng["sp"]
        for tok in self.dtoks.values():
            self._wait(E, tok)
        for e in self.eng.values():
            if e.cnt > 0 and e.name != "sp":
                self._wait(E, (e.sem, e.cnt, e.name))

    def close(self):
        self.stack.close()


class Ring:
    def __init__(self, S, name, n, shape, dtype, stack=None):
        self.t = [S.sbuf("%s%d" % (name, i), shape, dtype, stack) for i in range(n)]
        self.r = [Res("%s%d" % (name, i)) for i in range(n)]
        self.i = 0
        self.n = n

    def next(self):
        i = self.i
        self.i = (i + 1) % self.n
        return self.t[i], self.r[i]


def _consts():
    c = {}
    c["idf"] = np.eye(128, dtype=np.float32)
    c["idb"] = np.eye(128).astype(ml_dtypes.bfloat16)
    c["ones"] = np.ones((128, 128), np.float32)
    k = np.arange(64)[:, None]
    i = np.arange(64)[None, :]
    c["m_le"] = (k <= i).astype(np.float32)
    c["m_gt"] = (k > i).astype(np.float32)
    c["m_lt"] = (k < i).astype(np.float32)
    q = np.arange(128)[:, None]
    s = np.arange(256)[None, :]
    dist = 128 + q - s
    valid = (dist >= 0) & (dist <= 128)
    slopes = np.exp2(-8.0 * np.arange(1, 9, dtype=np.float32) / 8).astype(np.float32)
    b = np.where(valid[:, None, :], -slopes[None, :, None] * dist[:, None, :].astype(np.float32), NEG)
    c["biasN"] = b.astype(np.float32)
    b0 = b.copy()
    b0[:, :, :128] = NEG
    c["bias0"] = b0.astype(np.float32)
    c["i16b"] = np.eye(16, dtype=np.float32).reshape(1, 256)
    pos = np.arange(129)[None, :]
    hs = (np.arange(128) % 8)
    bs = -slopes[hs][:, None] * (128 - pos).astype(np.float32)
    c["biass"] = bs.astype(np.float32)
    selm = np.zeros((32, 128), np.float32)
    for sidx in range(16):
        for h in range(8):
            selm[(h // 4) * 16 + sidx, sidx * 8 + h] = 1.0
    c["selm"] = selm
    return c


SAMPLE_CONSTS = ("i16b", "biass")
CONST_SHAPES = {"selm": ([32, 128], F32), "idf": ([128, 128], F32), "idb": ([128, 128], BF16), "ones": ([128, 128], F32),
                "m_le": ([64, 64], F32), "m_gt": ([64, 64], F32), "m_lt": ([64, 64], F32),
                "biasN": ([128, 8, 256], F32), "bias0": ([128, 8, 256], F32)}


def build_program(dbg=(), nt=NT, do_ffn=True, ffn_group=2, npre=NPRE, do_sample=True, sstage=9, xchg=XCHG):
    nc = bass.Bass("TRN2", target_bir_lowering=False)
    S = Sched(nc)
    dbg_outs = {}

    xin = S.dram("xin", [(nt + npre) * 128, D], F32, "ExternalInput")
    w_in_d = S.dram("w_in", [128, KC, INW], F32, "ExternalInput")
    w_out_d = S.dram("w_out", [128, KC, D], F32, "ExternalInput")
    w_up_d = S.dram("w_up", [128, KC, 4096], F32, "ExternalInput")
    w_dn_d = S.dram("w_down", [128, 32, D], F32, "ExternalInput")
    convw_d = S.dram("convw", [128, 12, 4], F32, "ExternalInput")
    g_pre_d = S.dram("g_pre", [1, D], F32, "ExternalInput")
    g_post_d = S.dram("g_post", [1, D], F32, "ExternalInput")
    g_fpre_d = S.dram("g_fpre", [1, D], F32, "ExternalInput")
    g_fpost_d = S.dram("g_fpost", [1, D], F32, "ExternalInput")
    g_gdn_d = S.dram("g_gdn", [1, 128], F32, "ExternalInput")
    sinks_d = S.dram("sinks", [1, 8], F32, "ExternalInput")
    alog_d = S.dram("alog", [1, 4], F32, "ExternalInput")
    dtb_d = S.dram("dtb", [1, 4], F32, "ExternalInput")
    cd = {k: S.dram(k, shp, dt, "ExternalInput") for k, (shp, dt) in CONST_SHAPES.items()}

    NS = 16
    xs_d = S.dram("xs", [NS, D], F32, "ExternalInput")
    cst_d = S.dram("cst", [NS, 3, 1536], F32, "ExternalInput")
    ck_d = S.dram("ck", [NS, 128, 128], F32, "ExternalInput")
    cv_d = S.dram("cv", [NS, 128, 128], F32, "ExternalInput")
    sst_d = S.dram("sst", [NS, 4, 128, 128], F32, "ExternalInput")
    convwt_d = S.dram("convwt", [1, 4 * 1536], F32, "ExternalInput")
    i16b_d = S.dram("i16b", [1, 256], F32, "ExternalInput")
    biass_d = S.dram("biass", [128, 129], F32, "ExternalInput")
    sinksh_d = S.dram("sinksh", [128, 1], F32, "ExternalInput")
    ys_d = S.dram("ys", [NS, D], F32, "ExternalOutput")
    sconv_d = S.dram("s_conv", [NS, 3, 1536], F32, "ExternalOutput")
    sk_d = S.dram("s_k", [NS, 128, 128], F32, "ExternalOutput")
    sv_d = S.dram("s_v", [NS, 128, 128], F32, "ExternalOutput")
    sS_d = S.dram("s_S", [NS, 4, 128, 128], F32, "ExternalOutput")
    scr_q = S.dram("scr_q", [NS, 512], F32, "Internal")
    scr_kv = S.dram("scr_kv", [NS, 256], F32, "Internal")
    scr_ao = S.dram("scr_ao", [128, 64], F32, "Internal")
    sel_d = S.dram("sel", [1, 4], F32, "ExternalInput")
    sp_wT = S.dram("sp_wT", [nt, 128, 8, 64], BF16, "Internal")
    sp_qT = S.dram("sp_qT", [nt, 128, 4, 128], BF16, "Internal")
    sp_kd = S.dram("sp_kd", [nt, 64, 8, 128], BF16, "Internal")
    sp_ub = S.dram("sp_ub", [nt, 64, 8, 128], F32, "Internal")
    sp_qkT = S.dram("sp_qkT", [nt, 64, 8, 64], BF16, "Internal")
    sp_gs = S.dram("sp_gs", [nt, 64, 16, 8], F32, "Internal")
    sp_ld = S.dram("sp_ld", [nt, 128, 8], F32, "Internal")
    sp_zab = S.dram("sp_zab", [nt, 64, 2, 520], F32, "Internal")
    sp_mix = S.dram("sp_mix", [nt, 128, 4, 128], BF16, "Internal")
    xsrc_d = S.dram("xsrc", [128, 1024], F32, "Internal")
    xdst_d = S.dram("xdst", [4 * 128, 1024], F32, "Internal", addr_space="Local")
    y_d = S.dram("y", [nt * 128, D], F32, "ExternalOutput")
    pconv_d = S.dram("p_conv", [3, 1536], F32, "ExternalOutput")
    pk_d = S.dram("p_k", [128, 128], F32, "ExternalOutput")
    pv_d = S.dram("p_v", [128, 128], F32, "ExternalOutput")
    pS_d = S.dram("p_S", [4, 128, 128], F32, "ExternalOutput")

    def dbg_out(name, ap_sb, res, shape):
        if name not in dbg:
            return
        d = S.dram("dbg_" + name, list(shape), F32, "ExternalOutput")
        dbg_outs[name] = d
        S.dma("sp", d[:] if True else d, ap_sb, res, reads=[res], final=True)

    banks = [S.psum("bank%d" % i, [128, 512], F32) for i in range(8)]
    bres = [Res("bank%d" % i) for i in range(8)]
    bstate = {"i": 0}

    POOLS = {"all": list(range(8)), "F": [0, 1, 2], "P": [3, 4, 5], "C": [6, 7],
             "C1": [0, 1], "C2": [2, 3, 4, 5], "O": [6, 7]}
    bstate.update({"pool": "all", "cnt": {k: 0 for k in POOLS}})

    def nb(pool=None):
        p = pool or bstate["pool"]
        lst = POOLS[p]
        k = bstate["cnt"][p]
        bstate["cnt"][p] = k + 1
        i = lst[k % len(lst)]
        return banks[i], bres[i]

    def const_tile(name, shape, dt, src_ap, eng="sp", stack=None):
        t = S.sbuf(name, shape, dt, stack)
        r = Res(name)
        S.dma(eng, t[:], src_ap, r, writes=[r])
        return t, r

    idf, r_idf = const_tile("idf", [128, 128], F32, cd["idf"][:])
    idb, r_idb = const_tile("idb", [128, 128], BF16, cd["idb"][:])
    ones, r_ones = const_tile("ones", [128, 128], F32, cd["ones"][:])
    epsT = S.sbuf("epsT", [128, 1], F32)
    r_eps = Res("eps")
    S.op("dve", lambda e: e.memset(epsT[:], EPS), writes=[r_eps])

    stat = Ring(S, "stat", 12, [128, 16], F32)
    junk = Ring(S, "junk", 2, [128, 1024], BF16)

    def rstd_from_ss(ss_ap, r_ss, n, scale):
        st, rs = stat.next()
        S.op("act", lambda e: e.activation(out=st[0:n, 0:1], in_=ss_ap, func=AF.Ln, bias=epsT[0:n, :], scale=scale),
             reads=[r_ss, r_eps], writes=[rs])
        S.op("act", lambda e: e.activation(out=st[0:n, 1:2], in_=st[0:n, 0:1], func=AF.Exp, scale=-0.5), reads=[rs], writes=[rs])
        return st[0:n, 1:2], rs

    def norm_transpose(x_ap, r_x, gain, r_gain, hT_out_ap, r_hT, n=128):
        jk, rj = junk.next()
        st, rs = stat.next()
        S.op("act", lambda e: e.activation(out=jk[0:n, :], in_=x_ap, func=AF.Square, accum_out=st[0:n, 0:1]),
             reads=[r_x], writes=[rj, rs])
        rstd, rr = rstd_from_ss(st[0:n, 0:1], rs, n, 1.0 / D)
        hb, rh = junk.next()
        S.op("dve", lambda e: e.scalar_tensor_tensor(out=hb[0:n, :], in0=x_ap, scalar=rstd, in1=gain[0:n, :],
                                                     op0=ALU.mult, op1=ALU.mult),
             reads=[r_x, rr, r_gain], writes=[rh])
        bk, rb = nb()
        bkb = bk[:, :].bitcast(BF16)
        for kc in range(KC):
            S.op("pe", lambda e, kc=kc: e.transpose(bkb[:, kc * 128:kc * 128 + n], hb[0:n, kc * 128:(kc + 1) * 128],
                                                    idb[0:n, 0:n]),
                 reads=[rh, r_idb], writes=[rb], acc=(kc > 0))
        S.op("act", lambda e: e.copy(out=hT_out_ap, in_=bkb.rearrange("p (k t) -> p k t", k=KC)[:, :, 0:n]),
             reads=[rb], writes=[r_hT])

    def epilogue(bk0, rb0, bk1, rb1, gain, r_gain, resid_ap, r_resid, out_ap, r_out, n=128):
        jk, rj = junk.next()
        st, rs = stat.next()
        S.op("act", lambda e: e.activation(out=jk[0:n, 0:512], in_=bk0[0:n, :], func=AF.Square, accum_out=st[0:n, 0:1]),
             reads=[rb0], writes=[rj, rs])
        S.op("act", lambda e: e.activation(out=jk[0:n, 512:1024], in_=bk1[0:n, :], func=AF.Square, accum_out=st[0:n, 1:2]),
             reads=[rb1], writes=[rj, rs], acc=True)
        S.op("dve", lambda e: e.tensor_tensor(out=st[0:n, 2:3], in0=st[0:n, 0:1], in1=st[0:n, 1:2], op=ALU.add),
             reads=[rs], writes=[rs])
        rstd, rr = rstd_from_ss(st[0:n, 2:3], rs, n, 1.0 / D)
        S.op("dve", lambda e: e.tensor_tensor(out=out_ap[:, 0:512], in0=bk0[0:n, :], in1=gain[0:n, 0:512], op=ALU.mult),
             reads=[rb0, r_gain], writes=[r_out])
        S.op("dve", lambda e: e.tensor_tensor(out=out_ap[:, 512:1024], in0=bk1[0:n, :], in1=gain[0:n, 512:1024], op=ALU.mult),
             reads=[rb1, r_gain], writes=[r_out], acc=True)
        S.op("dve", lambda e: e.scalar_tensor_tensor(out=out_ap, in0=out_ap, scalar=rstd, in1=resid_ap,
                                                     op0=ALU.mult, op1=ALU.add),
             reads=[r_out, rr, r_resid], writes=[r_out])

    W = contextlib.ExitStack()
    AC = contextlib.ExitStack()
    A = contextlib.ExitStack()
    w_in = S.sbuf("w_in_bf", [128, KC, INW], BF16, W)
    r_win = [Res("w_in%d" % kc) for kc in range(KC)]
    for kc in range(KC):
        S.dma("pool", w_in[:, kc, :], w_in_d[:, kc, :], r_win[kc], writes=[r_win[kc]])
    w_out = S.sbuf("w_out_bf", [128, KC, D], BF16, W)
    r_wout = Res("w_out")
    for kc in range(KC):
        S.dma("pool", w_out[:, kc, :], w_out_d[:, kc, :], r_wout, writes=[r_wout], acc=True)

    g_pre, r_gpre = const_tile("g_pre", [128, D], F32, g_pre_d[:].partition_broadcast(128), stack=W)
    g_post, r_gpost = const_tile("g_post", [128, D], F32, g_post_d[:].partition_broadcast(128), stack=W)
    convw, r_convw = const_tile("convw", [128, 12, 4], F32, convw_d[:], stack=W)
    m_le, r_mle = const_tile("m_le", [64, 64], F32, cd["m_le"][:], stack=W)
    m_gt, r_mgt = const_tile("m_gt", [64, 64], F32, cd["m_gt"][:], stack=W)
    m_lt, r_mlt = const_tile("m_lt", [64, 64], F32, cd["m_lt"][:], stack=W)
    bias = S.sbuf("bias", [128, 8, 256], F32, W)
    r_bias = Res("bias")
    S.dma("sp", bias[:], cd["bias0"][:], r_bias, writes=[r_bias])
    sink, r_sink = const_tile("sink", [128, 8], F32, sinks_d[:].partition_broadcast(128), stack=W)
    nsink = S.sbuf("nsink", [128, 8], F32, W)
    r_nsink = Res("nsink")
    S.op("dve", lambda e: e.tensor_scalar(out=nsink[:], in0=sink[:], scalar1=-1.0, scalar2=None, op0=ALU.mult),
         reads=[r_sink], writes=[r_nsink])
    g_gdn, r_ggdn = const_tile("g_gdn", [64, 128], F32, g_gdn_d[:].partition_broadcast(64), stack=W)
    dtb8 = S.sbuf("dtb8", [64, 8], F32, W)
    r_dtb8 = Res("dtb8")
    S.dma("sp", dtb8[:, 0:4], dtb_d[:].partition_broadcast(64), r_dtb8, writes=[r_dtb8], acc=True)
    S.dma("sp", dtb8[:, 4:8], dtb_d[:].partition_broadcast(64), r_dtb8, writes=[r_dtb8], acc=True)
    negA8 = S.sbuf("negA8", [64, 8], F32, W)
    r_negA8 = Res("negA8")
    S.dma("sp", negA8[:, 0:4], alog_d[:].partition_broadcast(64), r_negA8, writes=[r_negA8], acc=True)
    S.dma("sp", negA8[:, 4:8], alog_d[:].partition_broadcast(64), r_negA8, writes=[r_negA8], acc=True)
    S.op("act", lambda e: e.activation(out=negA8[:], in_=negA8[:], func=AF.Exp), reads=[r_negA8], writes=[r_negA8])
    S.op("dve", lambda e: e.tensor_scalar(out=negA8[:], in0=negA8[:], scalar1=-1.0, scalar2=None, op0=ALU.mult),
         reads=[r_negA8], writes=[r_negA8])

    state = {"prev_xc": None}

    def inproj_tok(hT, r_hT, col_lo, ncols, tok_lo, ntok):
        bk, rb = nb()
        for kc in range(KC):
            S.op("pe", lambda e, kc=kc: e.matmul(bk[0:ntok, 0:ncols], lhsT=hT[:, kc, tok_lo:tok_lo + ntok],
                                                 rhs=w_in[:, kc, col_lo:col_lo + ncols], start=(kc == 0), stop=(kc == KC - 1)),
                 reads=[r_hT, r_win[kc]], writes=[rb], acc=(kc > 0))
        return bk, rb

    def sample_phase():
        r_misc = Res("misc")
        for sidx in range(NS):
            S.dma("sp", sconv_d[sidx, 0:2, :], cst_d[sidx, 1:3, :], r_misc, final=True)
            S.dma("sp", sk_d[sidx, 0:127, :], ck_d[sidx, 1:128, :], r_misc, final=True)
            S.dma("sp", sv_d[sidx, 0:127, :], cv_d[sidx, 1:128, :], r_misc, final=True)
        SA = contextlib.ExitStack()
        xs, r_xs = const_tile("xs", [NS, D], F32, xs_d[:], stack=SA)
        hTs = S.sbuf("hTs", [128, KC, NS], BF16, SA)
        r_hTs = Res("hTs")
        norm_transpose(xs[:, :], r_xs, g_pre, r_gpre, hTs[:, :, :], r_hTs, n=NS)
        Ps = S.sbuf("Ps", [NS, INW], F32, SA)
        r_Ps = Res("Ps")
        col = 0
        ci = 0
        while col < INW:
            ncol = min(512, INW - col)
            bk, rb = inproj_tok(hTs, r_hTs, col, ncol, 0, NS)
            eng = "act" if ci % 2 == 0 else "dve"
            if eng == "act":
                S.op("act", lambda e, bk=bk, col=col, ncol=ncol: e.copy(out=Ps[:, col:col + ncol], in_=bk[0:NS, 0:ncol]),
                     reads=[rb], writes=[r_Ps], acc=True)
            else:
                S.op("dve", lambda e, bk=bk, col=col, ncol=ncol: e.tensor_copy(out=Ps[:, col:col + ncol], in_=bk[0:NS, 0:ncol]),
                     reads=[rb], writes=[r_Ps], acc=True)
            col += ncol
            ci += 1
        S.dma("pool", sconv_d[:, 2, :], Ps[:, 768:2304], r_Ps, reads=[r_Ps], final=True)
        S.dma("pool", sk_d[:, 127, :], Ps[:, 512:640], r_Ps, reads=[r_Ps], final=True)
        S.dma("pool", sv_d[:, 127, :], Ps[:, 640:768], r_Ps, reads=[r_Ps], final=True)
        r_scr = Res("scr_qkv")
        S.dma("sp", scr_q[:, :], Ps[:, 0:512], r_scr, reads=[r_Ps], writes=[r_scr], acc=True)
        S.dma("sp", scr_kv[:, :], Ps[:, 512:768], r_scr, reads=[r_Ps], writes=[r_scr], acc=True)
        mixs = S.sbuf("mixs", [NS, D], BF16, SA)
        r_mixs = Res("mixs")
        i16b_t, r_i16b = const_tile("i16b", [128, 256], F32, i16b_d[:].partition_broadcast(128), stack=SA)
        i16b = i16b_t[:, :].rearrange("p (a b) -> p a b", a=16)
        cvs = S.sbuf("cvs", [NS, 1536], F32, SA)
        r_cvs = Res("cvs")
        sg, r_sg = S.sbuf("sg", [NS, 16, 4], F32, SA), Res("sg")
        rn8, r_rn8 = S.sbuf("rn8", [NS, 8], F32, SA), Res("rn8")

        if sstage < 1:
            S.barrier()
            SA.close()
            return
        S1 = contextlib.ExitStack()
        cst, r_cst = const_tile("cst", [NS, 3, 1536], F32, cst_d[:], stack=S1)
        cwb_t, r_cwb = const_tile("cwb", [NS, 4 * 1536], F32, convwt_d[:].partition_broadcast(NS), stack=S1)
        cwb = cwb_t[:, :].rearrange("p (i c) -> p i c", i=4)
        ctmp, r_ctmp = S.sbuf("ctmp", [NS, 1536], F32, S1), Res("ctmp")
        S.op("dve", lambda e: e.tensor_tensor(out=cvs[:, :], in0=cst[:, 0, :], in1=cwb[:, 0, :], op=ALU.mult),
             reads=[r_cst, r_cwb], writes=[r_cvs])
        for i in range(1, 4):
            src = cst[:, i, :] if i < 3 else Ps[:, 768:2304]
            S.op("dve", lambda e, i=i, src=src: e.tensor_tensor(out=ctmp[:, :], in0=src, in1=cwb[:, i, :], op=ALU.mult),
                 reads=[r_cst, r_cwb, r_Ps], writes=[r_ctmp])
            S.op("dve", lambda e: e.tensor_tensor(out=cvs[:, :], in0=cvs[:, :], in1=ctmp[:, :], op=ALU.add),
                 reads=[r_cvs, r_ctmp], writes=[r_cvs])
        S.op("act", lambda e: e.activation(out=cvs[:, :], in_=cvs[:, :], func=AF.Silu), reads=[r_cvs], writes=[r_cvs])
        S.op("dve", lambda e: e.tensor_tensor(out=ctmp[:, 0:1024], in0=cvs[:, 0:1024], in1=cvs[:, 0:1024], op=ALU.mult),
             reads=[r_cvs], writes=[r_ctmp])
        S.op("dve", lambda e: e.tensor_reduce(out=rn8[:, :], in_=ctmp[:, 0:1024].rearrange("p (h d) -> p h d", h=8),
                                              axis=AX.X, op=ALU.add), reads=[r_ctmp], writes=[r_rn8])
        S.op("act", lambda e: e.activation(out=rn8[:, :], in_=rn8[:, :], func=AF.Ln, bias=epsT[0:NS, :], scale=1.0),
             reads=[r_rn8, r_eps], writes=[r_rn8])
        S.op("act", lambda e: e.activation(out=rn8[:, :], in_=rn8[:, :], func=AF.Exp, scale=-0.5), reads=[r_rn8], writes=[r_rn8])
        S.op("dve", lambda e: e.tensor_scalar(out=rn8[:, 0:4], in0=rn8[:, 0:4], scalar1=128.0 ** -0.5, scalar2=None,
                                              op0=ALU.mult), reads=[r_rn8], writes=[r_rn8])
        S.op("dve", lambda e: e.tensor_tensor(out=cvs[:, 0:1024].rearrange("p (h d) -> p h d", h=8),
                                              in0=cvs[:, 0:1024].rearrange("p (h d) -> p h d", h=8),
                                              in1=rn8[:, :].unsqueeze(2).broadcast_to([NS, 8, 128]), op=ALU.mult),
             reads=[r_cvs, r_rn8], writes=[r_cvs])

        def sgv(i):
            return sg[:, i, :]
        S.op("dve", lambda e: e.tensor_tensor(out=sgv(0), in0=Ps[:, 2816:2820], in1=dtb8[0:NS, 0:4], op=ALU.add),
             reads=[r_Ps, r_dtb8], writes=[r_sg])
        S.op("act", lambda e: e.activation(out=sgv(4), in_=sgv(0), func=AF.Abs), reads=[r_sg], writes=[r_sg])
        S.op("act", lambda e: e.activation(out=sgv(4), in_=sgv(4), func=AF.Exp, scale=-1.0), reads=[r_sg], writes=[r_sg])
        S.op("act", lambda e: e.activation(out=sgv(4), in_=sgv(4), func=AF.Ln, bias=1.0), reads=[r_sg], writes=[r_sg])
        S.op("dve", lambda e: e.scalar_tensor_tensor(out=sgv(0), in0=sgv(0), scalar=0.0, in1=sgv(4), op0=ALU.max, op1=ALU.add),
             reads=[r_sg], writes=[r_sg])
        S.op("dve", lambda e: e.tensor_tensor(out=sgv(0), in0=sgv(0), in1=negA8[0:NS, 0:4], op=ALU.mult),
             reads=[r_sg, r_negA8], writes=[r_sg])
        S.op("act", lambda e: e.activation(out=sgv(2), in_=sgv(0), func=AF.Exp), reads=[r_sg], writes=[r_sg])
        S.op("act", lambda e: e.activation(out=sgv(1), in_=Ps[:, 2820:2824], func=AF.Exp, scale=-1.0), reads=[r_Ps], writes=[r_sg])
        S.op("dve", lambda e: e.tensor_scalar(out=sgv(1), in0=sgv(1), scalar1=1.0, scalar2=None, op0=ALU.add),
             reads=[r_sg], writes=[r_sg])
        S.op("dve", lambda e: e.reciprocal(out=sgv(1), in_=sgv(1)), reads=[r_sg], writes=[r_sg])
        S.op("dve", lambda e: e.tensor_tensor(out=ctmp[:, 0:512], in0=cvs[:, 0:512], in1=cvs[:, 512:1024], op=ALU.mult),
             reads=[r_cvs], writes=[r_ctmp])
        S.op("dve", lambda e: e.tensor_reduce(out=sgv(3), in_=ctmp[:, 0:512].rearrange("p (h d) -> p h d", h=4),
                                              axis=AX.X, op=ALU.add), reads=[r_ctmp], writes=[r_sg])
        S.barrier()
        S1.close()

        if sstage < 2:
            SA.close()
            return
        S2 = contextlib.ExitStack()
        Ssb = S.sbuf("Ssb", [128, NS * 4, 128], F32, S2)
        r_Ssb4 = [Res("Ssb%d" % i) for i in range(4)]
        r_Ssb = [r_Ssb4[i % 4] for i in range(NS)]
        for sidx in range(NS):
            S.dma("sp", Ssb[:, sidx * 4:(sidx + 1) * 4, :], sst_d[sidx, :, :, :].rearrange("h k v -> k h v"),
                  r_Ssb[sidx], writes=[r_Ssb[sidx]], acc=True)
        kqT, r_kqT = S.sbuf("kqT", [128, 8, NS], F32, S2), Res("kqT")
        bk, rb = nb()
        for hh in range(8):
            S.op("pe", lambda e, hh=hh, bk=bk: e.transpose(bk[:, hh * NS:(hh + 1) * NS], cvs[:, hh * 128:(hh + 1) * 128],
                                                           idf[0:NS, 0:NS]), reads=[r_cvs, r_idf], writes=[rb], acc=(hh > 0))
        S.op("act", lambda e, bk=bk: e.copy(out=kqT[:, :, :], in_=bk[:, 0:8 * NS].rearrange("p (h s) -> p h s", h=8)),
             reads=[rb], writes=[r_kqT])
        kqTm, r_kqTm = S.sbuf("kqTm", [128, 8, NS, NS], F32, S2), Res("kqTm")
        S.op("dve", lambda e: e.tensor_tensor(out=kqTm[:, :, :, :],
                                              in0=kqT[:, :, :].unsqueeze(2).broadcast_to([128, 8, NS, NS]),
                                              in1=i16b.unsqueeze(1).broadcast_to([128, 8, NS, NS]), op=ALU.mult),
             reads=[r_kqT, r_i16b], writes=[r_kqTm])
        bkk, rbk = nb()
        bkq2, rbq2 = nb()
        for h in range(4):
            for sidx in range(NS):
                S.op("pe", lambda e, h=h, sidx=sidx: e.matmul(bkk[0:NS, h * 128:(h + 1) * 128], lhsT=kqTm[:, 4 + h, sidx, :],
                                                              rhs=Ssb[:, sidx * 4 + h, :], start=(sidx == 0), stop=(sidx == NS - 1)),
                     reads=[r_kqTm, r_Ssb[sidx]], writes=[rbk], acc=not (h == 0 and sidx == 0))
        for h in range(4):
            for sidx in range(NS):
                S.op("pe", lambda e, h=h, sidx=sidx: e.matmul(bkq2[0:NS, h * 128:(h + 1) * 128], lhsT=kqTm[:, h, sidx, :],
                                                              rhs=Ssb[:, sidx * 4 + h, :], start=(sidx == 0), stop=(sidx == NS - 1)),
                     reads=[r_kqTm, r_Ssb[sidx]], writes=[rbq2], acc=not (h == 0 and sidx == 0))
        us, r_us = S.sbuf("us", [NS, 4, 128], F32, S2), Res("us")
        os_, r_os = S.sbuf("os", [NS, 4, 128], F32, S2), Res("os")
        ot, r_ot = S.sbuf("ot", [NS, 4, 128], F32, S2), Res("ot")

        def bcs(i):
            return sg[:, i, :].unsqueeze(2).broadcast_to([NS, 4, 128])
        v3 = cvs[:, 1024:1536].rearrange("p (h d) -> p h d", h=4)
        S.op("dve", lambda e: e.tensor_tensor(out=us[:, :, :], in0=bkk[0:NS, :].rearrange("p (h d) -> p h d", h=4),
                                              in1=bcs(2), op=ALU.mult), reads=[rbk, r_sg], writes=[r_us])
        S.op("dve", lambda e: e.tensor_tensor(out=us[:, :, :], in0=v3, in1=us[:, :, :], op=ALU.subtract),
             reads=[r_cvs, r_us], writes=[r_us])
        S.op("dve", lambda e: e.tensor_tensor(out=us[:, :, :], in0=us[:, :, :], in1=bcs(1), op=ALU.mult),
             reads=[r_us, r_sg], writes=[r_us])
        S.op("dve", lambda e: e.tensor_tensor(out=os_[:, :, :], in0=bkq2[0:NS, :].rearrange("p (h d) -> p h d", h=4),
                                              in1=bcs(2), op=ALU.mult), reads=[rbq2, r_sg], writes=[r_os])
        S.op("dve", lambda e: e.tensor_tensor(out=ot[:, :, :], in0=us[:, :, :], in1=bcs(3), op=ALU.mult),
             reads=[r_us, r_sg], writes=[r_ot])
        S.op("dve", lambda e: e.tensor_tensor(out=os_[:, :, :], in0=os_[:, :, :], in1=ot[:, :, :], op=ALU.add),
             reads=[r_os, r_ot], writes=[r_os])
        S.op("dve", lambda e: e.tensor_tensor(out=ot[:, :, :], in0=os_[:, :, :], in1=os_[:, :, :], op=ALU.mult),
             reads=[r_os], writes=[r_ot])
        S.op("dve", lambda e: e.tensor_reduce(out=sgv(5), in_=ot[:, :, :], axis=AX.X, op=ALU.add), reads=[r_ot], writes=[r_sg])
        S.op("act", lambda e: e.activation(out=sgv(5), in_=sgv(5), func=AF.Ln, bias=epsT[0:NS, :], scale=1.0 / 128),
             reads=[r_sg, r_eps], writes=[r_sg])
        S.op("act", lambda e: e.activation(out=sgv(5), in_=sgv(5), func=AF.Exp, scale=-0.5), reads=[r_sg], writes=[r_sg])
        S.op("act", lambda e: e.activation(out=ot[:, :, :].rearrange("p h d -> p (h d)"), in_=Ps[:, 2304:2816], func=AF.Silu),
             reads=[r_Ps], writes=[r_ot])
        S.op("dve", lambda e: e.tensor_tensor(out=ot[:, :, :], in0=ot[:, :, :],
                                              in1=g_gdn[0:NS, :].unsqueeze(1).broadcast_to([NS, 4, 128]), op=ALU.mult),
             reads=[r_ot, r_ggdn], writes=[r_ot])
        S.op("dve", lambda e: e.tensor_tensor(out=os_[:, :, :], in0=os_[:, :, :], in1=bcs(5), op=ALU.mult),
             reads=[r_os, r_sg], writes=[r_os])
        S.op("dve", lambda e: e.tensor_tensor(out=mixs[:, 512:1024].rearrange("p (h d) -> p h d", h=4), in0=os_[:, :, :],
                                              in1=ot[:, :, :], op=ALU.mult), reads=[r_os, r_ot], writes=[r_mixs], acc=True)
        um, r_um = S.sbuf("um", [NS, NS, 512], F32, S2), Res("um")
        S.op("dve", lambda e: e.tensor_tensor(out=um[:, :, :],
                                              in0=us[:, :, :].rearrange("p h d -> p (h d)").unsqueeze(1).broadcast_to([NS, NS, 512]),
                                              in1=idf[0:NS, 0:NS].unsqueeze(2).broadcast_to([NS, NS, 512]), op=ALU.mult),
             reads=[r_us, r_idf], writes=[r_um])
        dm, r_dm = S.sbuf("dm", [NS, NS, 4], F32, S2), Res("dm")
        S.op("dve", lambda e: e.tensor_tensor(out=dm[:, :, :], in0=sg[:, 2, :].unsqueeze(1).broadcast_to([NS, NS, 4]),
                                              in1=idf[0:NS, 0:NS].unsqueeze(2).broadcast_to([NS, NS, 4]), op=ALU.mult),
             reads=[r_sg, r_idf], writes=[r_dm])
        bkd, rbd = nb()
        S.op("pe", lambda e: e.matmul(bkd[:, 0:NS * 4], lhsT=ones[0:NS, :], rhs=dm[:, :, :].rearrange("p s h -> p (s h)"),
                                      start=True, stop=True), reads=[r_dm, r_ones], writes=[rbd])
        dec128, r_dec = S.sbuf("dec128", [128, NS * 4], F32, S2), Res("dec128")
        S.op("act", lambda e: e.copy(out=dec128[:, :], in_=bkd[:, 0:NS * 4]), reads=[rbd], writes=[r_dec])
        for sidx in range(NS):
            bk, rb = nb()
            for h in range(4):
                S.op("pe", lambda e, h=h, sidx=sidx, bk=bk: e.matmul(bk[:, h * 128:(h + 1) * 128],
                                                                      lhsT=cvs[:, 512 + h * 128:512 + (h + 1) * 128],
                                                                      rhs=um[:, sidx, h * 128:(h + 1) * 128], start=True, stop=True),
                     reads=[r_cvs, r_um], writes=[rb], acc=(h > 0))
            for h in range(4):
                sh = sidx * 4 + h
                S.op("dve", lambda e, h=h, sh=sh, bk=bk: e.scalar_tensor_tensor(
                    out=Ssb[:, sh, :], in0=Ssb[:, sh, :], scalar=dec128[:, sh:sh + 1], in1=bk[:, h * 128:(h + 1) * 128],
                    op0=ALU.mult, op1=ALU.add), reads=[r_Ssb[sidx], r_dec, rb], writes=[r_Ssb[sidx]])
            S.dma("pool", sS_d[sidx, :, :, :].rearrange("h k v -> k h v"), Ssb[:, sidx * 4:(sidx + 1) * 4, :], r_Ssb[sidx],
                  reads=[r_Ssb[sidx]], final=True)
        S.barrier()
        S2.close()

        if sstage < 3:
            SA.close()
            return
        S3 = contextlib.ExitStack()
        Kc = S.sbuf("Kc", [128, 128, 64], F32, S3)
        Vc = Kc
        r_Kc = Res("Kc")
        r_Vc = r_Kc
        Kc32 = S.sbuf("Kc32", [32, 128, 64], F32, S3)
        Vc32 = Kc32
        r_Kc32 = Res("Kc32")
        r_Vc32 = r_Kc32
        for kh in range(2):
            S.dma("sp", Kc32[kh * 16:(kh + 1) * 16, :, :], ck_d[:, :, kh * 64:(kh + 1) * 64], r_Kc32, writes=[r_Kc32], acc=True)
        selm, r_selm = const_tile("selm", [32, 128], F32, cd["selm"][:], stack=S3)
        qsh, r_qsh = S.sbuf("qsh", [128, 64], F32, S3), Res("qsh")
        knew, r_knew = S.sbuf("knew", [128, 64], F32, S3), Res("knew")
        vnew, r_vnew = S.sbuf("vnew", [128, 64], F32, S3), Res("vnew")
        S.dma("sp", qsh[:, :], scr_q[:, :].rearrange("s (h d) -> (s h) d", h=8), r_qsh, reads=[r_scr], writes=[r_qsh])
        for sidx in range(NS):
            for kh in range(2):
                p0 = sidx * 8 + kh * 4
                S.dma("sp", knew[p0:p0 + 4, :], scr_kv[sidx, kh * 64:(kh + 1) * 64].partition_broadcast(4),
                      r_knew, reads=[r_scr], writes=[r_knew], acc=True)
                S.dma("sp", vnew[p0:p0 + 4, :], scr_kv[sidx, 128 + kh * 64:128 + (kh + 1) * 64].partition_broadcast(4),
                      r_vnew, reads=[r_scr], writes=[r_vnew], acc=True)
        biass, r_biass = const_tile("biass", [128, 129], F32, biass_d[:], stack=S3)
        sinksh, r_sinksh = const_tile("sinksh", [128, 1], F32, sinksh_d[:], stack=S3)
        scs, r_scs = S.sbuf("scs", [128, 132], F32, S3), Res("scs")
        ps_, r_ps = S.sbuf("ps", [128, 132], F32, S3), Res("ps")
        tq, r_tq = S.sbuf("tq", [128, 64], F32, S3), Res("tq")
        aos, r_aos = S.sbuf("aos", [128, 64], F32, S3), Res("aos")
        for jb in range(16):
            bk, rb = nb()
            S.op("pe", lambda e, jb=jb, bk=bk: e.matmul(bk[:, :], lhsT=selm[:, :],
                                                        rhs=Kc32[:, jb * 8:(jb + 1) * 8, :].rearrange("p a d -> p (a d)"),
                                                        start=True, stop=True), reads=[r_selm, r_Kc32], writes=[rb])
            S.op("dve", lambda e, jb=jb, bk=bk: e.tensor_tensor(
                out=Kc[:, jb * 8:(jb + 1) * 8, :], in0=bk[:, :].rearrange("p (a d) -> p a d", a=8),
                in1=qsh[:, :].unsqueeze(1).broadcast_to([128, 8, 64]), op=ALU.mult),
                reads=[rb, r_qsh], writes=[r_Kc], acc=(jb > 0))
        for kh in range(2):
            S.dma("sp", Vc32[kh * 16:(kh + 1) * 16, :, :], cv_d[:, :, kh * 64:(kh + 1) * 64], r_Vc32, writes=[r_Vc32],
                  acc=(kh > 0))
        S.op("dve", lambda e: e.tensor_reduce(out=scs[:, 0:128], in_=Kc[:, :, :], axis=AX.X, op=ALU.add),
             reads=[r_Kc], writes=[r_scs])
        S.op("dve", lambda e: e.tensor_tensor(out=tq[:, :], in0=qsh[:, :], in1=knew[:, :], op=ALU.mult),
             reads=[r_qsh, r_knew], writes=[r_tq])
        S.op("dve", lambda e: e.tensor_reduce(out=scs[:, 128:129], in_=tq[:, :], axis=AX.X, op=ALU.add),
             reads=[r_tq], writes=[r_scs], acc=True)
        S.op("dve", lambda e: e.scalar_tensor_tensor(out=scs[:, 0:129], in0=scs[:, 0:129], scalar=0.125, in1=biass[:, :],
                                                     op0=ALU.mult, op1=ALU.add), reads=[r_scs, r_biass], writes=[r_scs])
        st, rs = stat.next()
        S.op("dve", lambda e: e.tensor_reduce(out=st[:, 0:1], in_=scs[:, 0:129], axis=AX.X, op=ALU.max), reads=[r_scs], writes=[rs])
        S.op("dve", lambda e: e.tensor_tensor(out=st[:, 0:1], in0=st[:, 0:1], in1=sinksh[:, :], op=ALU.max),
             reads=[rs, r_sinksh], writes=[rs])
        S.op("dve", lambda e: e.tensor_scalar(out=st[:, 1:2], in0=st[:, 0:1], scalar1=-1.0, scalar2=None, op0=ALU.mult),
             reads=[rs], writes=[rs])
        S.op("act", lambda e: e.activation(out=ps_[:, 0:129], in_=scs[:, 0:129], func=AF.Exp, bias=st[:, 1:2],
                                           accum_out=st[:, 2:3]), reads=[r_scs, rs], writes=[r_ps, rs])
        S.op("act", lambda e: e.activation(out=st[:, 3:4], in_=sinksh[:, :], func=AF.Exp, bias=st[:, 1:2]),
             reads=[r_sinksh, rs], writes=[rs])
        S.op("dve", lambda e: e.tensor_tensor(out=st[:, 4:5], in0=st[:, 2:3], in1=st[:, 3:4], op=ALU.add), reads=[rs], writes=[rs])
        S.op("dve", lambda e: e.reciprocal(out=st[:, 5:6], in_=st[:, 4:5]), reads=[rs], writes=[rs])
        for jb in range(16):
            bk, rb = nb()
            S.op("pe", lambda e, jb=jb, bk=bk: e.matmul(bk[:, :], lhsT=selm[:, :],
                                                        rhs=Vc32[:, jb * 8:(jb + 1) * 8, :].rearrange("p a d -> p (a d)"),
                                                        start=True, stop=True), reads=[r_selm, r_Vc32], writes=[rb])
            S.op("dve", lambda e, jb=jb, bk=bk: e.tensor_tensor(
                out=Vc[:, jb * 8:(jb + 1) * 8, :], in0=bk[:, :].rearrange("p (a d) -> p a d", a=8),
                in1=ps_[:, jb * 8:(jb + 1) * 8].unsqueeze(2).broadcast_to([128, 8, 64]), op=ALU.mult),
                reads=[rb, r_ps], writes=[r_Vc], acc=(jb > 0))
        S.op("dve", lambda e: e.tensor_reduce(out=aos[:, :], in_=Vc[:, :, :].rearrange("p s d -> p d s"), axis=AX.X, op=ALU.add),
             reads=[r_Vc], writes=[r_aos])
        S.op("dve", lambda e: e.scalar_tensor_tensor(out=aos[:, :], in0=vnew[:, :], scalar=ps_[:, 128:129], in1=aos[:, :],
                                                     op0=ALU.mult, op1=ALU.add), reads=[r_vnew, r_ps, r_aos], writes=[r_aos])
        S.op("dve", lambda e: e.tensor_scalar(out=aos[:, :], in0=aos[:, :], scalar1=st[:, 5:6], scalar2=None, op0=ALU.mult),
             reads=[r_aos, rs], writes=[r_aos])
        r_sao = Res("scr_ao")
        S.dma("sp", scr_ao[:, :], aos[:, :], r_sao, reads=[r_aos], writes=[r_sao])
        aot, r_aot = S.sbuf("aot", [NS, 512], F32, S3), Res("aot")
        S.dma("sp", aot[:, :], scr_ao[:, :].rearrange("(s h) d -> s (h d)", h=8), r_aot, reads=[r_sao], writes=[r_aot])
        S.op("dve", lambda e: e.tensor_copy(out=mixs[:, 0:512], in_=aot[:, :]), reads=[r_aot], writes=[r_mixs], acc=True)
        mixTs, r_mixTs = S.sbuf("mixTs", [128, KC, NS], BF16, S3), Res("mixTs")
        bk, rb = nb()
        bkb = bk[:, :].bitcast(BF16)
        for kc in range(KC):
            S.op("pe", lambda e, kc=kc: e.transpose(bkb[:, kc * NS:(kc + 1) * NS], mixs[:, kc * 128:(kc + 1) * 128],
                                                    idb[0:NS, 0:NS]), reads=[r_mixs, r_idb], writes=[rb], acc=(kc > 0))
        S.op("act", lambda e: e.copy(out=mixTs[:, :, :], in_=bkb[:, 0:KC * NS].rearrange("p (k s) -> p k s", k=KC)),
             reads=[rb], writes=[r_mixTs])
        bk0, rb0 = nb()
        bk1, rb1 = nb()
        for n, (bk, rb) in enumerate(((bk0, rb0), (bk1, rb1))):
            for kc in range(KC):
                S.op("pe", lambda e, kc=kc, n=n, bk=bk: e.matmul(bk[0:NS, :], lhsT=mixTs[:, kc, :],
                                                                 rhs=w_out[:, kc, n * 512:(n + 1) * 512],
                                                                 start=(kc == 0), stop=(kc == KC - 1)),
                     reads=[r_mixTs, r_wout], writes=[rb], acc=(kc > 0))
        x1s, r_x1s = S.sbuf("x1s", [NS, D], F32, S3), Res("x1s")
        epilogue(bk0, rb0, bk1, rb1, g_post, r_gpost, xs[:, :], r_xs, x1s[:, :], r_x1s, n=NS)
        r_ys = Res("ys")
        S.dma("sp", ys_d[:, :], x1s[:, :], r_x1s, reads=[r_x1s], writes=[r_ys])
        state["r_ys"] = r_ys
        S.barrier()
        S3.close()
        SA.close()

    if do_sample:
        sample_phase()
    S.barrier()

    xring = Ring(S, "xt", 2, [128, D], F32, AC)
    def gbuf(name, shape, dt, st=None):
        return S.sbuf(name, shape, dt, st or A), Res(name)

    mixT_ring = Ring(S, "mixT", 2, [128, KC, 128], BF16, AC)
    x1_ring = None if xchg else Ring(S, "x1t", 1, [128, D], F32, AC)
    Sst = S.sbuf("Sst", [128, 4, 256], F32, AC)
    S_bf, r_Sbf = gbuf("S_bf", [128, 4, 256], BF16, AC)
    u_bf, r_u = gbuf("u_bf", [64, 4, 256], BF16, AC)
    o1, r_o1 = gbuf("o1", [64, 4, 128], F32, AC)
    o2, r_o2 = gbuf("o2", [64, 4, 128], F32, AC)
    sz, r_sz = gbuf("sz", [64, 4, 128], F32, AC)
    y_bf, r_ybf = gbuf("y_bf", [64, 4, 128], BF16, AC)
    r_S = [Res("S%d" % h) for h in range(4)]
    hTring = Ring(S, "hT", 3 if xchg else 2, [128, KC, 128], BF16, A)
    patt_ring = Ring(S, "patt", 1, [128, 768], F32, A)
    zab_ring = Ring(S, "zab", 2, [64, 2, 520], F32, A)
    xc_ring = Ring(S, "xc", 2, [128, 12, 131], F32, A)
    kT_ring = Ring(S, "kTr", 3, [64, 2, 128], BF16, A)
    v_ring = Ring(S, "vr", 3, [128, 128], BF16, A)

    acc_t = S.sbuf("convacc", [128, 12, 128], F32, A)
    r_acc = [Res("acc%d" % m) for m in range(12)]
    qkvT = acc_t
    r_qkvT = Res("qkvT")

    qTa = S.sbuf("qTa", [64, 8, 128], BF16, A)
    r_qTa = Res("qTa")
    sc = S.sbuf("sc", [128, 8, 256], F32, A)
    r_sc = [Res("sc%d" % i) for i in range(4)]
    pbf = S.sbuf("pbf", [128, 8, 256], BF16, A)
    r_pbf = [Res("pbf%d" % i) for i in range(8)]
    pT = S.sbuf("pT", [128, 16, 128], BF16, A)
    r_pT = [Res("pT0"), Res("pT1")]
    attn_o = S.sbuf("attn_o", [128, 512], BF16, A)
    r_attn_o = Res("attn_o")


    sqT = S.sbuf("sqT", [128, 8, 128], F32, A)
    r_sqTl = [Res("sqT")]
    gs, r_gs = gbuf("gsc", [64, 16, 8], F32)
    kt_bf, r_kt = gbuf("kt_bf", [64, 8, 128], BF16)
    kd_bf, r_kd = gbuf("kd_bf", [64, 8, 128], BF16)
    kn_bf, r_kn = gbuf("kn_bf", [64, 8, 128], BF16)
    v_bf, r_vbf = gbuf("v_bf", [64, 8, 128], BF16)
    knT, r_knT = gbuf("knT", [128, 8, 64], BF16)
    qT_bf, r_qTbf = gbuf("qT_bf", [128, 4, 128], BF16)
    gM, r_gM = gbuf("gM", [64, 8, 64], F32)
    Emat, r_E = gbuf("Emat", [64, 8, 64], F32)
    mb, r_mb = gM, r_gM
    Es, r_Es = gbuf("Es", [64, 8, 64], F32)
    Ei, r_Ei = gbuf("Ei", [64, 8, 64], F32)
    Bp = [gbuf("Bp%d" % i, [64, 8, 64], F32) for i in range(2)]
    Ap = [gbuf("Ap%d" % i, [64, 8, 64], F32) for i in range(2)]
    Zm, r_Z = Emat, r_E
    dgb, r_dgb = gM, r_gM
    qkT, r_qkT = gbuf("qkT", [64, 8, 64], BF16)
    NTb, r_NTb = gbuf("NTb", [64, 8, 64], BF16)
    ub, r_ub = gbuf("ub", [64, 8, 128], F32)
    wT_bf, r_wT = gbuf("wT_bf", [128, 8, 64], BF16)
    osq, r_osq = o2, r_o2

    S.op("dve", lambda e: e.memset(gs[:, :, :], 0.0), writes=[r_gs])
    S.op("dve", lambda e: e.memset(Sst[:], 0.0), writes=r_S)
    if xchg:
        S.op("dve", lambda e: e.tensor_copy(out=Sst[:, :, 128:256], in_=idf[:, :].unsqueeze(1).broadcast_to([128, 4, 128])),
             reads=[r_idf] + r_S, writes=r_S)
    S.op("act", lambda e: e.copy(out=S_bf[:, :, :], in_=Sst[:, :, :]), reads=r_S, writes=[r_Sbf])


    def load_x(t):
        xt, rx = xring.next()
        S.dma("sp", xt[:], xin[(t + npre) * 128:(t + npre + 1) * 128, :], rx, writes=[rx])
        return xt, rx

    def inproj_feat(hT, r_hT, xc, r_xc, tok_lo, ntok, out_off, groups=(0, 1, 2)):
        for grp in groups:
            bk, rb = nb()
            first = True
            for mm in range(4):
                m = grp * 4 + mm
                for kc in range(KC):
                    S.op("pe", lambda e, kc=kc, m=m, mm=mm: e.matmul(
                        bk[:, mm * 128:mm * 128 + ntok], lhsT=w_in[:, kc, 768 + m * 128:768 + (m + 1) * 128],
                        rhs=hT[:, kc, tok_lo:tok_lo + ntok], start=(kc == 0), stop=(kc == KC - 1)),
                        reads=[r_hT, r_win[kc]], writes=[rb], acc=not first)
                    first = False
            S.op("act", lambda e, grp=grp: e.copy(
                out=xc[:, grp * 4:(grp + 1) * 4, out_off:out_off + ntok],
                in_=bk[:, :].rearrange("p (m t) -> p m t", m=4)[:, :, 0:ntok]),
                reads=[rb], writes=[r_xc], acc=True)

    def kv_prep(patt, r_patt):
        kT, r_kT = kT_ring.next()
        vv, r_v = v_ring.next()
        bk, rb = nb()
        for kh in range(2):
            S.op("pe", lambda e, kh=kh: e.transpose(bk[0:64, kh * 128:(kh + 1) * 128],
                                                    patt[:, 512 + kh * 64:512 + (kh + 1) * 64], idf[:, :]),
                 reads=[r_patt, r_idf], writes=[rb], acc=(kh > 0))
        S.op("act", lambda e: e.copy(out=kT[:, :, :], in_=bk[0:64, 0:256].rearrange("p (k t) -> p k t", k=2)),
             reads=[rb], writes=[r_kT])
        S.op("dve", lambda e: e.tensor_copy(out=vv[:, :], in_=patt[:, 640:768]), reads=[r_patt], writes=[r_v])
        return (kT, r_kT, vv, r_v)

    def attention(patt, r_patt, prev_kv, cur_kv, mixT, r_mixT):
        kTp, r_kTp, vp, r_vp = prev_kv
        kTc, r_kTc, vc, r_vc = cur_kv
        for half in range(2):
            bk, rb = nb()
            for hh in range(4):
                h = half * 4 + hh
                S.op("pe", lambda e, h=h, hh=hh: e.transpose(bk[0:64, hh * 128:(hh + 1) * 128],
                                                             patt[:, h * 64:(h + 1) * 64], idf[:, :]),
                     reads=[r_patt, r_idf], writes=[rb], acc=(hh > 0))
            S.op("act", lambda e, half=half: e.activation(
                out=qTa[:, half * 4:(half + 1) * 4, :], in_=bk[0:64, :].rearrange("p (h t) -> p h t", h=4),
                func=AF.Copy, scale=0.125), reads=[rb], writes=[r_qTa], acc=(half > 0))
        yield
        for pr in range(4):
            bk, rb = nb()
            first = True
            for hh in range(2):
                h = pr * 2 + hh
                kh = h // 4
                S.op("pe", lambda e, h=h, hh=hh, kh=kh: e.matmul(bk[:, hh * 256:hh * 256 + 128], lhsT=qTa[:, h, :],
                                                                 rhs=kTp[:, kh, :], start=True, stop=True),
                     reads=[r_qTa, r_kTp], writes=[rb], acc=not first)
                first = False
                S.op("pe", lambda e, h=h, hh=hh, kh=kh: e.matmul(bk[:, hh * 256 + 128:hh * 256 + 256], lhsT=qTa[:, h, :],
                                                                 rhs=kTc[:, kh, :], start=True, stop=True),
                     reads=[r_qTa, r_kTc], writes=[rb], acc=True)
            S.op("dve", lambda e, pr=pr: e.tensor_tensor(
                out=sc[:, pr * 2:(pr + 1) * 2, :], in0=bk[:, :].rearrange("p (h s) -> p h s", h=2),
                in1=bias[:, pr * 2:(pr + 1) * 2, :], op=ALU.add), reads=[rb, r_bias], writes=[r_sc[pr]])
        yield
        st, rs = stat.next()
        S.op("dve", lambda e: e.tensor_reduce(out=st[:, 0:8], in_=sc[:, :, :], axis=AX.X, op=ALU.max),
             reads=r_sc, writes=[rs])
        yield
        S.op("dve", lambda e: e.scalar_tensor_tensor(out=st[:, 8:16], in0=st[:, 0:8], scalar=-1.0, in1=nsink[:, :],
                                                     op0=ALU.mult, op1=ALU.min), reads=[rs, r_nsink], writes=[rs])
        yield
        st2, rs2 = stat.next()
        for h in range(8):
            S.op("act", lambda e, h=h: e.activation(out=pbf[:, h, :], in_=sc[:, h, :], func=AF.Exp,
                                                    bias=st[:, 8 + h:9 + h], accum_out=st2[:, h:h + 1]),
                 reads=[r_sc[h // 2], rs], writes=[r_pbf[h], rs2], acc=(h > 0))
        yield
        S.op("dve", lambda e: e.tensor_tensor(out=st[:, 0:8], in0=st[:, 8:16], in1=sink[:, :], op=ALU.add),
             reads=[rs, r_sink], writes=[rs])
        yield
        S.op("act", lambda e: e.activation(out=st2[:, 8:16], in_=st[:, 0:8], func=AF.Exp), reads=[rs], writes=[rs2], acc=True)
        yield
        S.op("dve", lambda e: e.tensor_tensor(out=st2[:, 0:8], in0=st2[:, 0:8], in1=st2[:, 8:16], op=ALU.add),
             reads=[rs2], writes=[rs2])
        yield
        S.op("dve", lambda e: e.reciprocal(out=st2[:, 8:16], in_=st2[:, 0:8]), reads=[rs2], writes=[rs2])
        yield
        for half in range(2):
            bk, rb = nb()
            bkb = bk[:, :].bitcast(BF16)
            first = True
            for hh in range(4):
                h = half * 4 + hh
                for sh in range(2):
                    idx = hh * 2 + sh
                    S.op("pe", lambda e, h=h, sh=sh, idx=idx: e.transpose(
                        bkb[:, idx * 128:(idx + 1) * 128], pbf[:, h, sh * 128:(sh + 1) * 128], idb[:, :]),
                        reads=[r_pbf[h], r_idb], writes=[rb], acc=not first)
                    first = False
            eng = "act" if half == 0 else "dve"
            if eng == "act":
                S.op("act", lambda e, half=half: e.copy(out=pT[:, half * 8:(half + 1) * 8, :],
                                                        in_=bkb.rearrange("p (i q) -> p i q", i=8)),
                     reads=[rb], writes=[r_pT[half]])
            else:
                S.op("dve", lambda e, half=half: e.tensor_copy(out=pT[:, half * 8:(half + 1) * 8, :],
                                                               in_=bkb.rearrange("p (i q) -> p i q", i=8)),
                     reads=[rb], writes=[r_pT[half]])
        yield
        bk, rb = nb()
        first = True
        for h in range(8):
            kh = h // 4
            S.op("pe", lambda e, h=h, kh=kh: e.matmul(bk[:, h * 64:(h + 1) * 64], lhsT=pT[:, h * 2, :],
                                                      rhs=vp[:, kh * 64:(kh + 1) * 64], start=True, stop=False),
                 reads=[r_pT[h // 4], r_vp], writes=[rb], acc=not first)
            first = False
            S.op("pe", lambda e, h=h, kh=kh: e.matmul(bk[:, h * 64:(h + 1) * 64], lhsT=pT[:, h * 2 + 1, :],
                                                      rhs=vc[:, kh * 64:(kh + 1) * 64], start=False, stop=True),
                 reads=[r_pT[h // 4], r_vc], writes=[rb], acc=True)
        yield
        S.op("dve", lambda e: e.tensor_tensor(
            out=attn_o[:, :].rearrange("p (h d) -> p h d", h=8), in0=bk[:, :].rearrange("p (h d) -> p h d", h=8),
            in1=st2[:, 8:16].unsqueeze(2).broadcast_to([128, 8, 64]), op=ALU.mult),
            reads=[rb, rs2], writes=[r_attn_o])
        yield
        bk, rb = nb()
        bkb = bk[:, :].bitcast(BF16)
        for c4 in range(4):
            S.op("pe", lambda e, c4=c4: e.transpose(bkb[:, c4 * 128:(c4 + 1) * 128], attn_o[:, c4 * 128:(c4 + 1) * 128],
                                                    idb[:, :]), reads=[r_attn_o, r_idb], writes=[rb], acc=(c4 > 0))
        yield
        S.op("act", lambda e: e.copy(out=mixT[:, 0:4, :], in_=bkb[:, 0:512].rearrange("p (c t) -> p c t", c=4)),
             reads=[rb], writes=[r_mixT], acc=True)
        yield

    wT_bf_g, r_wT_g, qT_bf_g, r_qTbf_g, kd_bf_g, r_kd_g = wT_bf, r_wT, qT_bf, r_qTbf, kd_bf, r_kd
    ub_g, r_ub_g, qkT_g, r_qkT_g, gs_g, r_gs_g = ub, r_ub, qkT, r_qkT, gs, r_gs

    def gdn_prep(xc, r_xc, zab, r_zab, t, full):
        yield from gdn_prep1(xc, r_xc, zab, r_zab, t, full)
        yield from gdn_prep2(t, full)

    def par(gens, pools):
        gl = [(g, pools[i]) for i, g in enumerate(gens) if g is not None]
        outer = bstate["pool"]
        while gl:
            for item in list(gl):
                g, pl = item
                bstate["pool"] = pl
                try:
                    next(g)
                    bstate["pool"] = outer
                    yield
                except StopIteration:
                    gl.remove(item)
        bstate["pool"] = outer

    def gdn(xc, r_xc, zab, r_zab, mixT, r_mixT, t, full=True, aug=False, prepq=False):
        yield from gdn_prep(xc, r_xc, zab, r_zab, t, full or prepq)
        yield from gdn_scan(zab, r_zab, mixT, r_mixT, t, full, aug)

    def gsv(i):
        return gs[:, i, :]

    def bc_u(ap8, n):
        return ap8.unsqueeze(2).broadcast_to([64, ap8.shape[1], n])

    def gdn_prep1(xc, r_xc, zab, r_zab, t, full):
        m0 = 0 if full else 4
        for m in range(m0, 12):
            S.op("act", lambda e, m=m: e.activation(out=acc_t[:, m, :], in_=xc[:, m, 0:128], func=AF.Copy,
                                                    scale=convw[:, m, 0:1]), reads=[r_xc, r_convw], writes=[r_acc[m], r_qkvT], acc=(m > m0))
        yield
        for i in range(1, 4):
            for m in range(m0, 12):
                S.op("dve", lambda e, m=m, i=i: e.scalar_tensor_tensor(
                    out=acc_t[:, m, :], in0=xc[:, m, i:i + 128], scalar=convw[:, m, i:i + 1], in1=acc_t[:, m, :],
                    op0=ALU.mult, op1=ALU.add), reads=[r_xc, r_convw, r_acc[m]], writes=[r_acc[m]])
        yield
        S.op("act", lambda e: e.activation(out=qkvT[:, m0:12, :], in_=acc_t[:, m0:12, :], func=AF.Silu),
             reads=r_acc[m0:], writes=[r_qkvT] + r_acc[m0:])
        yield
        S.op("act", lambda e: e.activation(out=sqT[:, m0:8, :], in_=qkvT[:, m0:8, :], func=AF.Square),
             reads=[r_qkvT], writes=r_sqTl)
        yield
        bkq, rbq = nb()
        first = True
        for qk_ in range(0 if full else 1, 2):
            for c in range(2):
                for h in range(4):
                    col = qk_ * 8 + c * 4 + h
                    S.op("pe", lambda e, qk_=qk_, c=c, h=h, col=col: e.matmul(
                        bkq[0:64, col:col + 1], lhsT=sqT[:, qk_ * 4 + h, c * 64:(c + 1) * 64], rhs=ones[:, 0:1],
                        start=True, stop=True), reads=r_sqTl + [r_ones], writes=[rbq], acc=not first)
                    first = False
        yield
        q0 = 0 if full else 1
        S.op("act", lambda e: e.activation(out=gs[:, q0:2, :].rearrange("p a b -> p (a b)"), in_=bkq[0:64, q0 * 8:16],
                                           func=AF.Ln, bias=epsT[0:64, :], scale=1.0), reads=[rbq, r_eps], writes=[r_gs])
        yield
        S.op("act", lambda e: e.activation(out=gs[:, q0:2, :], in_=gs[:, q0:2, :], func=AF.Exp, scale=-0.5), reads=[r_gs], writes=[r_gs])
        yield
        a_ap = zab[:, :, 512:516]
        b_ap = zab[:, :, 516:520]

        def v8(i):
            return gs[:, i, :].rearrange("p (c h) -> p c h", c=2)
        S.op("dve", lambda e: e.tensor_tensor(out=v8(2), in0=a_ap, in1=dtb8[:, :].rearrange("p (c h) -> p c h", c=2),
                                              op=ALU.add), reads=[r_zab, r_dtb8], writes=[r_gs])
        yield
        S.op("act", lambda e: e.activation(out=gsv(3), in_=gsv(2), func=AF.Abs),
             reads=[r_gs], writes=[r_gs])
        yield
        S.op("act", lambda e: e.activation(out=gsv(3), in_=gsv(3), func=AF.Exp, scale=-1.0), reads=[r_gs], writes=[r_gs])
        yield
        S.op("act", lambda e: e.activation(out=gsv(3), in_=gsv(3), func=AF.Ln, bias=1.0), reads=[r_gs], writes=[r_gs])
        yield
        S.op("dve", lambda e: e.scalar_tensor_tensor(out=gsv(2), in0=gsv(2), scalar=0.0, in1=gsv(3),
                                                     op0=ALU.max, op1=ALU.add), reads=[r_gs], writes=[r_gs])
        yield
        S.op("dve", lambda e: e.tensor_tensor(out=gsv(2), in0=gsv(2), in1=negA8[:, :], op=ALU.mult),
             reads=[r_gs, r_negA8], writes=[r_gs])
        yield
        S.op("act", lambda e: e.activation(out=v8(3), in_=b_ap, func=AF.Exp, scale=-1.0), reads=[r_zab], writes=[r_gs])
        yield
        S.op("dve", lambda e: e.tensor_scalar(out=gsv(3), in0=gsv(3), scalar1=1.0, scalar2=None, op0=ALU.add),
             reads=[r_gs], writes=[r_gs])
        yield
        S.op("dve", lambda e: e.reciprocal(out=gsv(3), in_=gsv(3)), reads=[r_gs], writes=[r_gs])
        yield
        bkg, rbg = nb()
        S.op("pe", lambda e: e.matmul(bkg[0:64, 0:8], lhsT=m_le[:, :], rhs=gsv(2), start=True, stop=True),
             reads=[r_gs, r_mle], writes=[rbg])
        yield
        S.op("pe", lambda e: e.matmul(bkg[0:64, 8:16], lhsT=ones[0:64, 0:64], rhs=gsv(2), start=True, stop=True),
             reads=[r_gs, r_ones], writes=[rbg], acc=True)
        yield
        S.op("pe", lambda e: e.matmul(bkg[:, 16:24], lhsT=ones[0:64, :], rhs=gsv(2), start=True, stop=True),
             reads=[r_gs, r_ones], writes=[rbg], acc=True)
        yield
        S.op("act", lambda e: e.copy(out=gs[:, 4:6, :].rearrange("p a b -> p (a b)"), in_=bkg[0:64, 0:16]),
             reads=[rbg], writes=[r_gs])
        yield
        ld128, r_ld = stat.next()
        S.op("act", lambda e: e.activation(out=ld128[:, 0:8], in_=bkg[:, 16:24], func=AF.Exp), reads=[rbg], writes=[r_ld])
        yield
        S.op("act", lambda e: e.activation(out=gsv(6), in_=gsv(4), func=AF.Exp), reads=[r_gs], writes=[r_gs])
        yield
        S.op("dve", lambda e: e.tensor_tensor(out=gsv(7), in0=gsv(5), in1=gsv(4), op=ALU.subtract),
             reads=[r_gs], writes=[r_gs])
        yield
        S.op("act", lambda e: e.activation(out=gsv(7), in_=gsv(7), func=AF.Exp), reads=[r_gs], writes=[r_gs])
        yield
        S.op("dve", lambda e: e.tensor_tensor(out=gsv(8), in0=gsv(1), in1=gsv(6), op=ALU.mult), reads=[r_gs], writes=[r_gs])
        yield
        S.op("dve", lambda e: e.tensor_tensor(out=gsv(9), in0=gsv(1), in1=gsv(7), op=ALU.mult), reads=[r_gs], writes=[r_gs])
        yield
        if full:
            S.op("dve", lambda e: e.tensor_scalar(out=gsv(10), in0=gsv(0), scalar1=128.0 ** -0.5, scalar2=None, op0=ALU.mult),
                 reads=[r_gs], writes=[r_gs])
            S.op("dve", lambda e: e.tensor_tensor(out=gsv(11), in0=gsv(10), in1=gsv(6), op=ALU.mult), reads=[r_gs], writes=[r_gs])
        yield
        state["ld"] = (ld128, r_ld)

    def gdn_prep2(t, full):
        for c in range(2):
            bk, rb = nb()
            for h in range(4):
                S.op("pe", lambda e, c=c, h=h: e.transpose(bk[0:64, h * 128:(h + 1) * 128],
                                                           qkvT[:, 4 + h, c * 64:(c + 1) * 64], idf[:, :]),
                     reads=[r_qkvT, r_idf], writes=[rb], acc=(h > 0))
            bk3 = bk[0:64, :].rearrange("p (h d) -> p h d", h=4)
            us = slice(c * 4, (c + 1) * 4)
            S.op("dve", lambda e, bk3=bk3, us=us: e.tensor_tensor(out=kt_bf[:, us, :], in0=bk3, in1=bc_u(gs[:, 8, us], 128),
                                                                  op=ALU.mult), reads=[rb, r_gs], writes=[r_kt], acc=(c > 0))
            S.op("dve", lambda e, bk3=bk3, us=us: e.tensor_tensor(out=kd_bf[:, us, :], in0=bk3, in1=bc_u(gs[:, 9, us], 128),
                                                                  op=ALU.mult), reads=[rb, r_gs], writes=[r_kd], acc=(c > 0))
            S.op("dve", lambda e, bk3=bk3, us=us: e.tensor_tensor(out=kn_bf[:, us, :], in0=bk3, in1=bc_u(gs[:, 1, us], 128),
                                                                  op=ALU.mult), reads=[rb, r_gs], writes=[r_kn], acc=(c > 0))
        yield
        for c in range(2):
            bk, rb = nb()
            for h in range(4):
                S.op("pe", lambda e, c=c, h=h: e.transpose(bk[0:64, h * 128:(h + 1) * 128],
                                                           qkvT[:, 8 + h, c * 64:(c + 1) * 64], idf[:, :]),
                     reads=[r_qkvT, r_idf], writes=[rb], acc=(h > 0))
            S.op("act", lambda e, c=c, bk=bk: e.copy(out=v_bf[:, c * 4:(c + 1) * 4, :],
                                                     in_=bk[0:64, :].rearrange("p (h d) -> p h d", h=4)),
                 reads=[rb], writes=[r_vbf], acc=(c > 0))
        yield
        bk, rb = nb()
        bkb = bk[:, :].bitcast(BF16)
        for u in range(8):
            S.op("pe", lambda e, u=u: e.transpose(bkb[:, u * 64:(u + 1) * 64], kn_bf[:, u, :], idb[0:64, 0:64]),
                 reads=[r_kn, r_idb], writes=[rb], acc=(u > 0))
        yield
        S.op("act", lambda e: e.copy(out=knT[:, :, :], in_=bkb[:, 0:512].rearrange("p (u t) -> p u t", u=8)),
             reads=[rb], writes=[r_knT])
        yield
        if full:
            S.op("dve", lambda e: e.tensor_copy(out=qT_bf[:, :, :], in_=qkvT[:, 0:4, :]), reads=[r_qkvT], writes=[r_qTbf])
        yield
        S.op("dve", lambda e: e.tensor_tensor(out=gM[:, :, :], in0=m_gt[:, :].unsqueeze(1).broadcast_to([64, 8, 64]),
                                              in1=bc_u(gsv(2), 64), op=ALU.mult), reads=[r_mgt, r_gs], writes=[r_gM])
        yield
        bk, rb = nb()
        for u in range(8):
            S.op("pe", lambda e, u=u: e.matmul(bk[0:64, u * 64:(u + 1) * 64], lhsT=gM[:, u, :], rhs=m_le[:, :],
                                               start=True, stop=True), reads=[r_gM, r_mle], writes=[rb], acc=(u > 0))
        yield
        S.op("act", lambda e: e.activation(out=Emat[:, :, :], in_=bk[0:64, :].rearrange("p (u i) -> p u i", u=8),
                                           func=AF.Exp), reads=[rb], writes=[r_E])
        yield
        S.op("dve", lambda e: e.tensor_tensor(out=mb[:, :, :], in0=m_lt[:, :].unsqueeze(1).broadcast_to([64, 8, 64]),
                                              in1=bc_u(gsv(3), 64), op=ALU.mult), reads=[r_mlt, r_gs], writes=[r_mb])
        yield
        S.op("dve", lambda e: e.tensor_tensor(out=Es[:, :, :], in0=Emat[:, :, :], in1=mb[:, :, :], op=ALU.mult),
             reads=[r_E, r_mb], writes=[r_Es])
        yield
        if full:
            S.op("dve", lambda e: e.tensor_tensor(out=Ei[:, :, :], in0=Emat[:, :, :],
                                                  in1=m_le[:, :].unsqueeze(1).broadcast_to([64, 8, 64]), op=ALU.mult),
                 reads=[r_E, r_mle], writes=[r_Ei])
        yield
        bk, rb = nb()
        for u in range(8):
            S.op("pe", lambda e, u=u: e.matmul(bk[0:64, u * 64:(u + 1) * 64], lhsT=knT[:, u, :], rhs=knT[:, u, :],
                                               start=True, stop=True), reads=[r_knT], writes=[rb], acc=(u > 0))
        yield
        B0, rB0 = Bp[0]
        S.op("dve", lambda e: e.tensor_tensor(out=B0[:, :, :], in0=bk[0:64, :].rearrange("p (u i) -> p u i", u=8),
                                              in1=Es[:, :, :], op=ALU.mult), reads=[rb, r_Es], writes=[rB0])
        yield
        if full:
            bk, rb = nb()
            for u in range(8):
                c, h = u // 4, u % 4
                S.op("pe", lambda e, u=u, c=c, h=h, bk=bk: e.matmul(bk[0:64, u * 64:(u + 1) * 64], lhsT=knT[:, u, :],
                                                                    rhs=qT_bf[:, h, c * 64:(c + 1) * 64], start=True, stop=True),
                     reads=[r_knT, r_qTbf], writes=[rb], acc=(u > 0))
            S.op("dve", lambda e, bk=bk: e.tensor_tensor(out=qkT[:, :, :], in0=bk[0:64, :].rearrange("p (u i) -> p u i", u=8),
                                                         in1=Ei[:, :, :], op=ALU.mult), reads=[rb, r_Ei], writes=[r_qkT])
        yield
        bk, rb = nb()
        for u in range(8):
            S.op("pe", lambda e, u=u: e.transpose(bk[0:64, u * 64:(u + 1) * 64], B0[:, u, :], idf[0:64, 0:64]),
                 reads=[rB0, r_idf], writes=[rb], acc=(u > 0))
        yield
        A0, rA0 = Ap[0]
        S.op("act", lambda e: e.copy(out=A0[:, :, :], in_=bk[0:64, :].rearrange("p (u i) -> p u i", u=8)),
             reads=[rb], writes=[rA0])
        yield
        S.op("dve", lambda e: e.scalar_tensor_tensor(out=Zm[:, :, :], in0=A0[:, :, :], scalar=-1.0,
                                                     in1=idf[0:64, 0:64].unsqueeze(1).broadcast_to([64, 8, 64]),
                                                     op0=ALU.mult, op1=ALU.add), reads=[rA0, r_idf], writes=[r_Z])
        yield
        cur = 0
        for lvl in range(5):
            Bc, rBc = Bp[cur]
            Ac, rAc = Ap[cur]
            Bn, rBn = Bp[1 - cur]
            An, rAn = Ap[1 - cur]
            bkB, rbB = nb()
            for u in range(8):
                S.op("pe", lambda e, u=u: e.matmul(bkB[0:64, u * 64:(u + 1) * 64], lhsT=Ac[:, u, :], rhs=Bc[:, u, :],
                                                   start=True, stop=True), reads=[rAc, rBc], writes=[rbB], acc=(u > 0))
            if lvl < 4:
                bkA, rbA = nb()
                for u in range(8):
                    S.op("pe", lambda e, u=u: e.matmul(bkA[0:64, u * 64:(u + 1) * 64], lhsT=Bc[:, u, :], rhs=Ac[:, u, :],
                                                       start=True, stop=True), reads=[rAc, rBc], writes=[rbA], acc=(u > 0))
            S.op("act", lambda e: e.copy(out=Bn[:, :, :], in_=bkB[0:64, :].rearrange("p (u i) -> p u i", u=8)),
                 reads=[rbB], writes=[rBn])
            if lvl < 4:
                S.op("dve", lambda e: e.tensor_copy(out=An[:, :, :], in_=bkA[0:64, :].rearrange("p (u i) -> p u i", u=8)),
                     reads=[rbA], writes=[rAn])
            bkZ, rbZ = nb()
            for u in range(8):
                S.op("pe", lambda e, u=u: e.matmul(bkZ[0:64, u * 64:(u + 1) * 64], lhsT=Bn[:, u, :], rhs=Zm[:, u, :],
                                                   start=True, stop=True), reads=[rBn, r_Z], writes=[rbZ], acc=(u > 0))
            S.op("dve", lambda e: e.tensor_tensor(out=Zm[:, :, :], in0=bkZ[0:64, :].rearrange("p (u i) -> p u i", u=8),
                                                  in1=Zm[:, :, :], op=ALU.add), reads=[rbZ, r_Z], writes=[r_Z])
            cur = 1 - cur
        yield
        S.op("dve", lambda e: e.tensor_tensor(out=dgb[:, :, :], in0=idf[0:64, 0:64].unsqueeze(1).broadcast_to([64, 8, 64]),
                                              in1=bc_u(gsv(3), 64), op=ALU.mult), reads=[r_idf, r_gs], writes=[r_dgb])
        yield
        bk, rb = nb()
        for u in range(8):
            S.op("pe", lambda e, u=u: e.matmul(bk[0:64, u * 64:(u + 1) * 64], lhsT=Zm[:, u, :], rhs=dgb[:, u, :],
                                               start=True, stop=True), reads=[r_Z, r_dgb], writes=[rb], acc=(u > 0))
        yield
        S.op("act", lambda e: e.copy(out=NTb[:, :, :], in_=bk[0:64, :].rearrange("p (u i) -> p u i", u=8)),
             reads=[rb], writes=[r_NTb])
        yield
        for c in range(2):
            bk, rb = nb()
            for h in range(4):
                u = c * 4 + h
                S.op("pe", lambda e, u=u, h=h: e.matmul(bk[0:64, h * 128:(h + 1) * 128], lhsT=NTb[:, u, :], rhs=v_bf[:, u, :],
                                                        start=True, stop=True), reads=[r_NTb, r_vbf], writes=[rb], acc=(h > 0))
            S.op("act", lambda e, c=c, bk=bk: e.copy(out=ub[:, c * 4:(c + 1) * 4, :],
                                                     in_=bk[0:64, :].rearrange("p (h d) -> p h d", h=4)),
                 reads=[rb], writes=[r_ub], acc=(c > 0))
        yield
        bk, rb = nb()
        for u in range(8):
            S.op("pe", lambda e, u=u: e.matmul(bk[:, u * 64:(u + 1) * 64], lhsT=kt_bf[:, u, :], rhs=NTb[:, u, :],
                                               start=True, stop=True), reads=[r_kt, r_NTb], writes=[rb], acc=(u > 0))
        yield
        S.op("dve", lambda e: e.tensor_copy(out=wT_bf[:, :, :], in_=bk[:, :].rearrange("p (u i) -> p u i", u=8)),
             reads=[rb], writes=[r_wT])
        yield

    def gdn_scan(zab, r_zab, mixT, r_mixT, t, full, aug, BB=None):
        if BB is None:
            BB = dict(wT=(wT_bf_g, r_wT_g), qT=(qT_bf_g, r_qTbf_g), kd=(kd_bf_g, r_kd_g), ub=(ub_g, r_ub_g),
                      qkT=(qkT_g, r_qkT_g), gs=(gs_g, r_gs_g), ld=state["ld"])
        wT_bf, r_wT = BB["wT"]
        qT_bf, r_qTbf = BB["qT"]
        kd_bf, r_kd = BB["kd"]
        ub, r_ub = BB["ub"]
        qkT, r_qkT = BB["qkT"]
        gs, r_gs = BB["gs"]
        ld128, r_ld = BB["ld"]
        for c in range(2):
            us = slice(c * 4, (c + 1) * 4)
            if aug:
                bkas = [nb(), nb()]
                for h in range(4):
                    bk_, rb_ = bkas[h // 2]
                    S.op("pe", lambda e, h=h, c=c, bk_=bk_: e.matmul(bk_[0:64, (h % 2) * 256:(h % 2) * 256 + 256],
                                                                     lhsT=wT_bf[:, c * 4 + h, :], rhs=S_bf[:, h, 0:256],
                                                                     start=True, stop=True),
                         reads=[r_wT, r_Sbf], writes=[rb_], acc=(h % 2 > 0))
                yield
                for pr in range(2):
                    bk_, rb_ = bkas[pr]
                    b3 = bk_[0:64, :].rearrange("p (h w) -> p h w", h=2)
                    S.op("dve", lambda e, pr=pr, b3=b3, c=c: e.tensor_tensor(
                        out=u_bf[:, pr * 2:pr * 2 + 2, 0:128], in0=ub[:, c * 4 + pr * 2:c * 4 + pr * 2 + 2, :],
                        in1=b3[:, :, 0:128], op=ALU.subtract), reads=[r_ub, rb_], writes=[r_u], acc=(pr > 0))
                    S.op("dve", lambda e, pr=pr, b3=b3: e.tensor_scalar(
                        out=u_bf[:, pr * 2:pr * 2 + 2, 128:256], in0=b3[:, :, 128:256], scalar1=-1.0, scalar2=None,
                        op0=ALU.mult), reads=[rb_], writes=[r_u], acc=True)
                yield
                bkss = [nb(), nb()]
                for h in range(4):
                    bk_, rb_ = bkss[h // 2]
                    S.op("pe", lambda e, h=h, c=c, bk_=bk_: e.matmul(bk_[:, (h % 2) * 256:(h % 2) * 256 + 256],
                                                                     lhsT=kd_bf[:, c * 4 + h, :], rhs=u_bf[:, h, 0:256],
                                                                     start=True, stop=True),
                         reads=[r_kd, r_u], writes=[rb_], acc=(h % 2 > 0))
                yield
                for h in range(4):
                    bk_, rb_ = bkss[h // 2]
                    S.op("dve", lambda e, h=h, c=c, bk_=bk_: e.scalar_tensor_tensor(
                        out=Sst[:, h, 0:256], in0=Sst[:, h, 0:256], scalar=ld128[:, c * 4 + h:c * 4 + h + 1],
                        in1=bk_[:, (h % 2) * 256:(h % 2) * 256 + 256], op0=ALU.mult, op1=ALU.add),
                        reads=[r_S[h], r_ld, rb_], writes=[r_S[h]])
                yield
                S.op("act", lambda e: e.copy(out=S_bf[:, :, :], in_=Sst[:, :, :]), reads=r_S, writes=[r_Sbf])
                yield
                continue
            bka, rba = nb()
            for h in range(4):
                S.op("pe", lambda e, h=h, c=c: e.matmul(bka[0:64, h * 128:(h + 1) * 128], lhsT=wT_bf[:, c * 4 + h, :],
                                                        rhs=S_bf[:, h, 0:128], start=True, stop=True),
                     reads=[r_wT, r_Sbf], writes=[rba], acc=(h > 0))
            yield
            if full:
                bko, rbo = nb()
                for h in range(4):
                    S.op("pe", lambda e, h=h, c=c: e.matmul(bko[0:64, h * 128:(h + 1) * 128], lhsT=qT_bf[:, h, c * 64:(c + 1) * 64],
                                                            rhs=S_bf[:, h, 0:128], start=True, stop=True),
                         reads=[r_qTbf, r_Sbf], writes=[rbo], acc=(h > 0))
            yield
            S.op("dve", lambda e, us=us: e.tensor_tensor(out=u_bf[:, :, 0:128], in0=ub[:, us, :],
                                                         in1=bka[0:64, :].rearrange("p (h d) -> p h d", h=4), op=ALU.subtract),
                 reads=[r_ub, rba], writes=[r_u])
            yield
            if full:
                bk2, rb2 = nb()
                for h in range(4):
                    S.op("pe", lambda e, h=h, c=c: e.matmul(bk2[0:64, h * 128:(h + 1) * 128], lhsT=qkT[:, c * 4 + h, :],
                                                            rhs=u_bf[:, h, 0:128], start=True, stop=True),
                         reads=[r_qkT, r_u], writes=[rb2], acc=(h > 0))
            yield
            bks, rbs = nb()
            for h in range(4):
                S.op("pe", lambda e, h=h, c=c: e.matmul(bks[:, h * 128:(h + 1) * 128], lhsT=kd_bf[:, c * 4 + h, :],
                                                        rhs=u_bf[:, h, 0:128], start=True, stop=True),
                     reads=[r_kd, r_u], writes=[rbs], acc=(h > 0))
            yield
            for h in range(4):
                S.op("dve", lambda e, h=h, c=c: e.scalar_tensor_tensor(
                    out=Sst[:, h, 0:128], in0=Sst[:, h, 0:128], scalar=ld128[:, c * 4 + h:c * 4 + h + 1],
                    in1=bks[:, h * 128:(h + 1) * 128], op0=ALU.mult, op1=ALU.add),
                    reads=[r_S[h], r_ld, rbs], writes=[r_S[h]])
            yield
            S.op("act", lambda e: e.copy(out=S_bf[:, :, 0:128], in_=Sst[:, :, 0:128]), reads=r_S, writes=[r_Sbf])
            yield
            if not full:
                continue
            S.op("dve", lambda e, us=us: e.tensor_tensor(out=o1[:, :, :], in0=bko[0:64, :].rearrange("p (h d) -> p h d", h=4),
                                                         in1=bc_u(gs[:, 11, us], 128), op=ALU.mult),
                 reads=[rbo, r_gs], writes=[r_o1])
            yield
            S.op("dve", lambda e, us=us: e.tensor_tensor(out=o2[:, :, :], in0=bk2[0:64, :].rearrange("p (h d) -> p h d", h=4),
                                                         in1=bc_u(gs[:, 10, us], 128), op=ALU.mult),
                 reads=[rb2, r_gs], writes=[r_o2])
            yield
            S.op("dve", lambda e: e.tensor_tensor(out=o1[:, :, :], in0=o1[:, :, :], in1=o2[:, :, :], op=ALU.add),
                 reads=[r_o1, r_o2], writes=[r_o1])
            yield
            if "gdn_o" in dbg and t == 0 and c == 0:
                dbg_out("gdn_o", o1[:, :, :], r_o1, [64, 4, 128])
            S.op("dve", lambda e: e.tensor_tensor(out=osq[:, :, :], in0=o1[:, :, :], in1=o1[:, :, :], op=ALU.mult),
                 reads=[r_o1], writes=[r_osq])
            yield
            st, rs = stat.next()
            S.op("dve", lambda e: e.tensor_reduce(out=st[0:64, 0:4], in_=osq[:, :, :], axis=AX.X, op=ALU.add),
                 reads=[r_osq], writes=[rs])
            yield
            S.op("act", lambda e: e.activation(out=st[0:64, 4:8], in_=st[0:64, 0:4], func=AF.Ln, bias=epsT[0:64, :],
                                               scale=1.0 / 128), reads=[rs, r_eps], writes=[rs])
            yield
            S.op("act", lambda e: e.activation(out=st[0:64, 8:12], in_=st[0:64, 4:8], func=AF.Exp, scale=-0.5), reads=[rs], writes=[rs])
            yield
            S.op("act", lambda e, c=c: e.activation(out=sz[:, :, :].rearrange("p h d -> p (h d)"), in_=zab[:, c, 0:512],
                                                    func=AF.Silu), reads=[r_zab], writes=[r_sz])
            yield
            S.op("dve", lambda e: e.tensor_tensor(out=sz[:, :, :], in0=sz[:, :, :],
                                                  in1=g_gdn[:, :].unsqueeze(1).broadcast_to([64, 4, 128]), op=ALU.mult),
                 reads=[r_sz, r_ggdn], writes=[r_sz])
            yield
            S.op("dve", lambda e: e.tensor_tensor(out=o1[:, :, :], in0=o1[:, :, :], in1=bc_u(st[0:64, 8:12], 128),
                                                  op=ALU.mult), reads=[r_o1, rs], writes=[r_o1])
            yield
            S.op("dve", lambda e: e.tensor_tensor(out=y_bf[:, :, :], in0=o1[:, :, :], in1=sz[:, :, :], op=ALU.mult),
                 reads=[r_o1, r_sz], writes=[r_ybf])
            yield
            bk, rb = nb()
            bkb = bk[:, :].bitcast(BF16)
            for h in range(4):
                S.op("pe", lambda e, h=h: e.transpose(bkb[:, h * 64:(h + 1) * 64], y_bf[:, h, :], idb[0:64, 0:64]),
                     reads=[r_ybf, r_idb], writes=[rb], acc=(h > 0))
            yield
            S.op("act", lambda e, c=c, bkb=bkb: e.copy(out=mixT[:, 4:8, c * 64:(c + 1) * 64],
                                                       in_=bkb[:, 0:256].rearrange("p (h t) -> p h t", h=4)),
                 reads=[rb], writes=[r_mixT], acc=True)
            yield

    def drive(*gens, weights=None, pools=None):
        gl = [(g, (weights[i] if weights else 1), (pools[i] if pools else "all")) for i, g in enumerate(gens) if g is not None]
        while gl:
            for item in list(gl):
                g, w, pl = item
                bstate["pool"] = pl
                try:
                    for _ in range(w):
                        next(g)
                except StopIteration:
                    gl.remove(item)
        bstate["pool"] = "all"

    ctx = {}

    def pre_front(t, light=False, halo=False, do_gdn=True):
        xt, rx = load_x(t)
        hT, r_hT = hTring.next()
        norm_transpose(xt[:, :], rx, g_pre, r_gpre, hT[:, :, :], r_hT)
        yield
        xc, r_xc = xc_ring.next()
        if light:
            inproj_feat(hT, r_hT, xc, r_xc, 125, 3, 128, groups=(1, 2))
            state["prev_xc"] = (xc, r_xc)
            return
        if state["prev_xc"] is not None:
            pxc, r_pxc = state["prev_xc"]
            S.op("dve", lambda e: e.tensor_copy(out=xc[:, 4:12, 0:3], in_=pxc[:, 4:12, 128:131]), reads=[r_pxc], writes=[r_xc])
        else:
            S.op("dve", lambda e: e.memset(xc[:, :, 0:3], 0.0), writes=[r_xc])
        if do_gdn:
            inproj_feat(hT, r_hT, xc, r_xc, 0, 128, 3, groups=(1,))
            yield
            inproj_feat(hT, r_hT, xc, r_xc, 0, 128, 3, groups=(2,))
            yield
        else:
            inproj_feat(hT, r_hT, xc, r_xc, 125, 3, 128, groups=(1, 2))
        if halo:
            inproj_feat(hT, r_hT, xc, r_xc, 125, 3, 128, groups=(0,))
            patt, r_patt = patt_ring.next()
            bk, rb = inproj_tok(hT, r_hT, 512, 256, 0, 128)
            S.op("act", lambda e: e.copy(out=patt[:, 512:768], in_=bk[:, 0:256]), reads=[rb], writes=[r_patt])
            state["prev_kv"] = kv_prep(patt, r_patt)
            yield
        state["prev_xc"] = (xc, r_xc)
        if do_gdn:
            zab, r_zab = zab_ring.next()
            for c in range(2):
                bk, rb = inproj_tok(hT, r_hT, 2816, 8, c * 64, 64)
                S.op("dve", lambda e, c=c, bk=bk: e.tensor_copy(out=zab[:, c, 512:520], in_=bk[0:64, 0:8]), reads=[rb],
                     writes=[r_zab], acc=True)
            ctx[("p", t)] = (xc, r_xc, zab, r_zab)
        yield

    def pre_back(t, aug):
        xc, r_xc, zab, r_zab = ctx.pop(("p", t))
        yield from gdn(xc, r_xc, zab, r_zab, None, None, t, full=False, aug=aug)

    def wout_epi(t, xt, rx, mixT, r_mixT):
        bk0, rb0 = nb()
        bk1, rb1 = nb()
        for n, (bk, rb) in enumerate(((bk0, rb0), (bk1, rb1))):
            for kc in range(KC):
                S.op("pe", lambda e, kc=kc, n=n, bk=bk: e.matmul(bk[:, :], lhsT=mixT[:, kc, :],
                                                                 rhs=w_out[:, kc, n * 512:(n + 1) * 512],
                                                                 start=(kc == 0), stop=(kc == KC - 1)),
                     reads=[r_mixT, r_wout], writes=[rb], acc=(kc > 0))
            yield
        x1, r_x1 = x1_ring.next()
        epilogue(bk0, rb0, bk1, rb1, g_post, r_gpost, xt[:, :], rx, x1[:, :], r_x1)
        ryp = state.setdefault("ry_pool", {})
        if (t % 4) not in ryp:
            ryp[t % 4] = Res("y%d" % (t % 4))
        ryt = ryp[t % 4]
        S.dma("pool", y_d[t * 128:(t + 1) * 128, :], x1[:, :], r_x1, reads=[r_x1], writes=[ryt], acc=True)
        state.setdefault("ry", {})[t] = ryt
        yield

    def pconv_out(hT, r_hT):
        pc = acc_t[:, :, :].rearrange("p m t -> p (m t)")
        r_pc = Res("pc")
        for n3 in range(3):
            bk, rb = inproj_tok(hT, r_hT, 768 + n3 * 512, 512, 0, 128)
            S.op("act", lambda e, n3=n3, bk=bk: e.copy(out=pc[:, n3 * 512:(n3 + 1) * 512], in_=bk[:, :]),
                 reads=[rb], writes=[r_pc, r_qkvT] + r_acc, acc=(n3 > 0))
        S.dma("sp", pconv_d[:, :], pc[125:128, 0:1536], r_pc, reads=[r_pc], final=True)

    def front_a(t):
        xt, rx = load_x(t)
        hT, r_hT = hTring.next()
        norm_transpose(xt[:, :], rx, g_pre, r_gpre, hT[:, :, :], r_hT)
        ctx[("a", t)] = (xt, rx, hT, r_hT)
        yield

    def own_front(t):
        if ("a", t) not in ctx:
            yield from front_a(t)
        xt, rx, hT, r_hT = ctx.pop(("a", t))
        patt, r_patt = patt_ring.next()
        bk, rb = inproj_tok(hT, r_hT, 0, 512, 0, 128)
        S.op("act", lambda e: e.copy(out=patt[:, 0:512], in_=bk[:, :]), reads=[rb], writes=[r_patt])
        yield
        bk, rb = inproj_tok(hT, r_hT, 512, 256, 0, 128)
        S.op("dve", lambda e: e.tensor_copy(out=patt[:, 512:768], in_=bk[:, 0:256]), reads=[rb], writes=[r_patt], acc=True)
        yield
        zab, r_zab = zab_ring.next()
        for c in range(2):
            bk, rb = inproj_tok(hT, r_hT, 2304, 512, c * 64, 64)
            S.op("act", lambda e, c=c, bk=bk: e.copy(out=zab[:, c, 0:512], in_=bk[0:64, :]), reads=[rb], writes=[r_zab], acc=True)
            bk, rb = inproj_tok(hT, r_hT, 2816, 8, c * 64, 64)
            S.op("dve", lambda e, c=c, bk=bk: e.tensor_copy(out=zab[:, c, 512:520], in_=bk[0:64, 0:8]), reads=[rb],
                 writes=[r_zab], acc=True)
            yield
        xc, r_xc = xc_ring.next()
        pxc, r_pxc = state["prev_xc"]
        S.op("dve", lambda e: e.tensor_copy(out=xc[:, :, 0:3], in_=pxc[:, :, 128:131]), reads=[r_pxc], writes=[r_xc])
        for grp in range(3):
            inproj_feat(hT, r_hT, xc, r_xc, 0, 128, 3, groups=(grp,))
            yield
        state["prev_xc"] = (xc, r_xc)
        if t == 0:
            dbg_out("patt", patt[:, :], r_patt, [128, 768])
            dbg_out("zab", zab[:, :, :], r_zab, [64, 2, 520])
            dbg_out("xc", xc[:, :, :], r_xc, [128, 12, 131])
        mixT, r_mixT = mixT_ring.next()
        cur_kv = kv_prep(patt, r_patt)
        yield
        yield from attention(patt, r_patt, state["prev_kv"], cur_kv, mixT, r_mixT)
        state["prev_kv"] = cur_kv
        if t == 0:
            S.dma("sp", bias[:], cd["biasN"][:], r_bias, writes=[r_bias])
        if t == nt - 1:
            S.dma("sp", pk_d[:, :], patt[:, 512:640], r_patt, reads=[r_patt], final=True)
            S.dma("sp", pv_d[:, :], patt[:, 640:768], r_patt, reads=[r_patt], final=True)
        ctx[("o", t)] = (xt, rx, hT, r_hT, zab, r_zab, xc, r_xc, mixT, r_mixT)
        yield

    def own_back(t):
        xt, rx, hT, r_hT, zab, r_zab, xc, r_xc, mixT, r_mixT = ctx.pop(("o", t))
        yield from gdn(xc, r_xc, zab, r_zab, mixT, r_mixT, t)
        if t == 0 and "mixT" in dbg:
            d = S.dram("dbg_mixT", [128, KC, 128], BF16, "ExternalOutput")
            dbg_outs["mixT"] = d
            S.dma("sp", d[:], mixT[:, :, :], r_mixT, reads=[r_mixT], final=True)
        yield from wout_epi(t, xt, rx, mixT, r_mixT)
        if t == nt - 1:
            pconv_out(hT, r_hT)
            for h in range(4):
                S.dma("sp", pS_d[h, :, :], Sst[:, h, 0:128], r_S[h], reads=[r_S[h]], final=True)

    state["prev_kv"] = None
    state["A_closed"] = False
    if xchg:
        drive(pre_front(-1, halo=True, do_gdn=False))
        r_sp = {}
        sp_pool = {}

        def back1(t):
            xt, rx, hT, r_hT, zab, r_zab, xc, r_xc, mixT, r_mixT = ctx.pop(("o", t))
            prev_scan = state.pop("pending_scan", None)
            yield from par([gdn_prep1(xc, r_xc, zab, r_zab, t, True), prev_scan], ["P", "C"])
            yield from gdn_prep2(t, True)
            ld128, r_ld = state["ld"]
            state["pending_scan"] = gdn_scan(None, None, None, None, t, False, True,
                                             dict(wT=(wT_bf, r_wT), qT=(qT_bf, r_qTbf), kd=(kd_bf, r_kd), ub=(ub, r_ub),
                                                  qkT=(qkT, r_qkT), gs=(gs, r_gs), ld=(ld128, r_ld)))
            if (t % 4) not in sp_pool:
                sp_pool[t % 4] = Res("sp%d" % (t % 4))
            rsp = sp_pool[t % 4]
            r_sp[t] = rsp
            srcs_ = []
            for dst, src, rr in ((sp_wT[t], wT_bf[:, :, :], r_wT), (sp_qT[t], qT_bf[:, :, :], r_qTbf),
                                 (sp_kd[t], kd_bf[:, :, :], r_kd), (sp_ub[t], ub[:, :, :], r_ub),
                                 (sp_qkT[t], qkT[:, :, :], r_qkT), (sp_gs[t], gs[:, :, :], r_gs),
                                 (sp_ld[t], ld128[:, 0:8], r_ld), (sp_zab[t], zab[:, :, :], r_zab),
                                 (sp_mix[t], mixT[:, 0:4, :], r_mixT)):
                S.dma("pool", dst, src, rsp, reads=[rr], writes=[rsp], acc=True)
                srcs_.append(rr)
            for rr in srcs_:
                rr.r[id(rsp.dsem)] = (rsp.dsem, rsp.dcnt, "dma")
            if t == nt - 1:
                pconv_out(hT, r_hT)
            yield

        def seq(*gens):
            for g in gens:
                if g is not None:
                    yield from g

        drive(front_a(0))
        for t in range(nt + 1):
            fr = seq(own_front(t), front_a(t + 1) if t + 1 < nt else None) if t < nt else None
            drive(fr, back1(t - 1) if t > 0 else None, weights=P1W, pools=("F", "P"))
        drive(state.pop("pending_scan"))
        r_xsrc, r_xdst = Res("xsrc"), Res("xdst")
        S.dma("sp", xsrc_d[:, :], Sst[:, :, :].rearrange("p h w -> p (h w)"), r_xsrc, reads=r_S, writes=[r_xsrc])
        S.cc("AllGather", xsrc_d[:, :], xdst_d[:, :], r_xdst, [[0, 1, 2, 3], [4, 5, 6, 7]], reads=[r_xsrc], writes=[r_xdst])
        Gt = sc[:, 0:4, :].rearrange("p a b -> p (a b)")
        r_G = r_sc[0]
        r_G2 = r_sc[1]
        sel128, r_sel = const_tile("sel128", [128, 4], F32, sel_d[:].partition_broadcast(128), stack=A)
        cs, r_cs = sc[:, 4:6, :].rearrange("p a (h d) -> p (a h) d", h=2), r_sc[2]
        PhiT, r_PhiT = sc[:, 6:8, :].rearrange("p a (h d) -> p (a h) d", h=2), r_sc[3]
        S.op("dve", lambda e: e.memset(Sst[:, :, 0:128], 0.0), writes=r_S)
        for i in range(3):
            S.dma("sp", Gt, xdst_d[i * 128:(i + 1) * 128, :], r_G, reads=[r_xdst], writes=[r_G, r_G2])
            Gi = Gt.rearrange("p (h w) -> p h w", h=4)
            if i == 0:
                S.op("dve", lambda e, Gi=Gi: e.tensor_copy(out=cs, in_=Gi[:, :, 0:128]), reads=[r_G, r_G2], writes=[r_cs])
            else:
                bkT, rbT = nb()
                for h in range(4):
                    S.op("pe", lambda e, h=h, Gi=Gi: e.transpose(bkT[:, h * 128:(h + 1) * 128], Gi[:, h, 128:256], idf[:, :]),
                         reads=[r_G, r_G2, r_idf], writes=[rbT], acc=(h > 0))
                S.op("act", lambda e: e.copy(out=PhiT, in_=bkT[:, :].rearrange("p (h d) -> p h d", h=4)),
                     reads=[rbT], writes=[r_PhiT])
                bkM, rbM = nb()
                for h in range(4):
                    S.op("pe", lambda e, h=h: e.matmul(bkM[:, h * 128:(h + 1) * 128], lhsT=PhiT[:, h, :], rhs=cs[:, h, :],
                                                       start=True, stop=True), reads=[r_PhiT, r_cs], writes=[rbM], acc=(h > 0))
                S.op("dve", lambda e, Gi=Gi: e.tensor_tensor(out=cs, in0=bkM[:, :].rearrange("p (h d) -> p h d", h=4),
                                                            in1=Gi[:, :, 0:128], op=ALU.add), reads=[rbM, r_G, r_G2], writes=[r_cs])
            S.op("dve", lambda e, i=i: e.scalar_tensor_tensor(out=Sst[:, :, 0:128], in0=cs, scalar=sel128[:, i + 1:i + 2],
                                                              in1=Sst[:, :, 0:128], op0=ALU.mult, op1=ALU.add),
                 reads=[r_cs, r_sel] + r_S, writes=r_S)
        S.op("act", lambda e: e.copy(out=S_bf[:, :, 0:128], in_=Sst[:, :, 0:128]), reads=r_S, writes=[r_Sbf])
        S.barrier()
        A.close()
        state["A_closed"] = True
        A2 = contextlib.ExitStack()
        x1_ring = Ring(S, "x1t", 1, [128, D], F32, A2)
        wT2 = Ring(S, "wT2", 2, [128, 8, 64], BF16, A2)
        qT2 = Ring(S, "qT2", 2, [128, 4, 128], BF16, A2)
        kd2 = Ring(S, "kd2", 2, [64, 8, 128], BF16, A2)
        ub2 = Ring(S, "ub2", 2, [64, 8, 128], F32, A2)
        qkT2 = Ring(S, "qkT2", 2, [64, 8, 64], BF16, A2)
        gs2 = Ring(S, "gs2", 2, [64, 16, 8], F32, A2)
        ld2 = Ring(S, "ld2", 2, [128, 8], F32, A2)
        zab2 = Ring(S, "zab2", 2, [64, 2, 520], F32, A2)

        def front2(t):
            xt, rx = load_x(t)
            BB = {}
            for key, ring, src in (("wT", wT2, sp_wT), ("qT", qT2, sp_qT), ("kd", kd2, sp_kd), ("ub", ub2, sp_ub),
                                   ("qkT", qkT2, sp_qkT), ("gs", gs2, sp_gs), ("ld", ld2, sp_ld), ("zab", zab2, sp_zab)):
                tl, rr = ring.next()
                S.dma("sp", tl[:], src[t], rr, reads=[r_sp[t]], writes=[rr])
                BB[key] = (tl, rr)
            mixT, r_mixT = mixT_ring.next()
            S.dma("sp", mixT[:, 0:4, :], sp_mix[t], r_mixT, reads=[r_sp[t]], writes=[r_mixT])
            ctx[("2", t)] = (xt, rx, BB, mixT, r_mixT)
            yield

        def chain2(t, c):
            xt, rx, BB, mixT, r_mixT = ctx[("2", t)]
            wT_bf, r_wT = BB["wT"]
            qT_bf, r_qTbf = BB["qT"]
            kd_bf, r_kd = BB["kd"]
            ub, r_ub = BB["ub"]
            qkT, r_qkT = BB["qkT"]
            ld128, r_ld = BB["ld"]
            us = slice(c * 4, (c + 1) * 4)
            bka, rba = nb()
            for h in range(4):
                S.op("pe", lambda e, h=h: e.matmul(bka[0:64, h * 128:(h + 1) * 128], lhsT=wT_bf[:, c * 4 + h, :],
                                                   rhs=S_bf[:, h, 0:128], start=True, stop=True),
                     reads=[r_wT, r_Sbf], writes=[rba], acc=(h > 0))
            yield
            bko, rbo = nb("C2")
            for h in range(4):
                S.op("pe", lambda e, h=h: e.matmul(bko[0:64, h * 128:(h + 1) * 128], lhsT=qT_bf[:, h, c * 64:(c + 1) * 64],
                                                   rhs=S_bf[:, h, 0:128], start=True, stop=True),
                     reads=[r_qTbf, r_Sbf], writes=[rbo], acc=(h > 0))
            yield
            S.op("dve", lambda e: e.tensor_tensor(out=u_bf[:, :, 0:128], in0=ub[:, us, :],
                                                  in1=bka[0:64, :].rearrange("p (h d) -> p h d", h=4), op=ALU.subtract),
                 reads=[r_ub, rba], writes=[r_u])
            yield
            bks, rbs = nb()
            for h in range(4):
                S.op("pe", lambda e, h=h: e.matmul(bks[:, h * 128:(h + 1) * 128], lhsT=kd_bf[:, c * 4 + h, :],
                                                   rhs=u_bf[:, h, 0:128], start=True, stop=True),
                     reads=[r_kd, r_u], writes=[rbs], acc=(h > 0))
            yield
            bk2, rb2 = nb("C2")
            for h in range(4):
                S.op("pe", lambda e, h=h: e.matmul(bk2[0:64, h * 128:(h + 1) * 128], lhsT=qkT[:, c * 4 + h, :],
                                                   rhs=u_bf[:, h, 0:128], start=True, stop=True),
                     reads=[r_qkT, r_u], writes=[rb2], acc=(h > 0))
            yield
            for h in range(4):
                S.op("dve", lambda e, h=h: e.scalar_tensor_tensor(
                    out=Sst[:, h, 0:128], in0=Sst[:, h, 0:128], scalar=ld128[:, c * 4 + h:c * 4 + h + 1],
                    in1=bks[:, h * 128:(h + 1) * 128], op0=ALU.mult, op1=ALU.add),
                    reads=[r_S[h], r_ld, rbs], writes=[r_S[h]])
                yield
            S.op("act", lambda e: e.copy(out=S_bf[:, :, 0:128], in_=Sst[:, :, 0:128]), reads=r_S, writes=[r_Sbf])
            ctx[("c", t, c)] = (bko, rbo, bk2, rb2)
            yield

        def outs2(t, c):
            xt, rx, BB, mixT, r_mixT = ctx[("2", t)]
            gs, r_gs = BB["gs"]
            zab, r_zab = BB["zab"]
            bko, rbo, bk2, rb2 = ctx.pop(("c", t, c))
            us = slice(c * 4, (c + 1) * 4)
            S.op("dve", lambda e: e.tensor_tensor(out=o1[:, :, :], in0=bko[0:64, :].rearrange("p (h d) -> p h d", h=4),
                                                  in1=bc_u(gs[:, 11, us], 128), op=ALU.mult),
                 reads=[rbo, r_gs], writes=[r_o1])
            yield
            S.op("dve", lambda e: e.tensor_tensor(out=o2[:, :, :], in0=bk2[0:64, :].rearrange("p (h d) -> p h d", h=4),
                                                  in1=bc_u(gs[:, 10, us], 128), op=ALU.mult),
                 reads=[rb2, r_gs], writes=[r_o2])
            yield
            S.op("act", lambda e: e.activation(out=sz[:, :, :].rearrange("p h d -> p (h d)"), in_=zab[:, c, 0:512],
                                               func=AF.Silu), reads=[r_zab], writes=[r_sz])
            yield
            S.op("dve", lambda e: e.tensor_tensor(out=o1[:, :, :], in0=o1[:, :, :], in1=o2[:, :, :], op=ALU.add),
                 reads=[r_o1, r_o2], writes=[r_o1])
            yield
            S.op("dve", lambda e: e.tensor_tensor(out=osq[:, :, :], in0=o1[:, :, :], in1=o1[:, :, :], op=ALU.mult),
                 reads=[r_o1], writes=[r_osq])
            yield
            st, rs = stat.next()
            S.op("dve", lambda e: e.tensor_reduce(out=st[0:64, 0:4], in_=osq[:, :, :], axis=AX.X, op=ALU.add),
                 reads=[r_osq], writes=[rs])
            yield
            S.op("act", lambda e: e.activation(out=st[0:64, 4:8], in_=st[0:64, 0:4], func=AF.Ln, bias=epsT[0:64, :],
                                               scale=1.0 / 128), reads=[rs, r_eps], writes=[rs])
            yield
            S.op("act", lambda e: e.activation(out=st[0:64, 8:12], in_=st[0:64, 4:8], func=AF.Exp, scale=-0.5), reads=[rs], writes=[rs])
            yield
            S.op("pool", lambda e: e.tensor_tensor(out=sz[:, :, :], in0=sz[:, :, :],
                                                  in1=g_gdn[:, :].unsqueeze(1).broadcast_to([64, 4, 128]), op=ALU.mult),
                 reads=[r_sz, r_ggdn], writes=[r_sz])
            yield
            S.op("dve", lambda e: e.tensor_tensor(out=o1[:, :, :], in0=o1[:, :, :], in1=bc_u(st[0:64, 8:12], 128),
                                                  op=ALU.mult), reads=[r_o1, rs], writes=[r_o1])
            yield
            S.op("dve", lambda e: e.tensor_tensor(out=y_bf[:, :, :], in0=o1[:, :, :], in1=sz[:, :, :], op=ALU.mult),
                 reads=[r_o1, r_sz], writes=[r_ybf])
            yield
            bk, rb = nb()
            bkb = bk[:, :].bitcast(BF16)
            for h in range(4):
                S.op("pe", lambda e, h=h: e.transpose(bkb[:, h * 64:(h + 1) * 64], y_bf[:, h, :], idb[0:64, 0:64]),
                     reads=[r_ybf, r_idb], writes=[rb], acc=(h > 0))
            yield
            S.op("act", lambda e: e.copy(out=mixT[:, 4:8, c * 64:(c + 1) * 64],
                                         in_=bkb[:, 0:256].rearrange("p (h t) -> p h t", h=4)),
                 reads=[rb], writes=[r_mixT], acc=True)
            yield
            if c == 1:
                ctx.pop(("2", t))
                yield from wout_epi(t, xt, rx, mixT, r_mixT)

        steps = [(t, c) for t in range(nt) for c in range(2)]
        drive(front2(0))
        for k in range(len(steps) + 1):
            gens = []
            pls = []
            if k < len(steps):
                t_, c_ = steps[k]
                gens.append(chain2(t_, c_))
                pls.append("C1")
                if c_ == 1 and t_ + 1 < nt:
                    gens.append(front2(t_ + 1))
                    pls.append("all")
            if k > 0:
                gens.append(outs2(*steps[k - 1]))
                pls.append("O")
            drive(*gens, pools=pls)
        for h in range(4):
            S.dma("sp", pS_d[h, :, :], Sst[:, h, 0:128], r_S[h], reads=[r_S[h]], final=True)
        S.barrier()
        A2.close()
    else:
        for t in range(-npre, 0):
            drive(pre_front(t, halo=(t == -1)))
            drive(pre_back(t, False))
        for t in range(nt + 1):
            drive(own_front(t) if t < nt else None, own_back(t - 1) if t > 0 else None)

    S.barrier()
    if not state["A_closed"]:
        A.close()
    AC.close()

    S.barrier()
    W.close()

    if do_ffn:
        Bs = contextlib.ExitStack()
        w_up = S.sbuf("w_up_bf", [128, KC, 4096], BF16, Bs)
        r_wup2 = [Res("w_up0"), Res("w_up1")]
        for hf in range(2):
            for kc in range(KC):
                S.dma("pool", w_up[:, kc, hf * 2048:(hf + 1) * 2048], w_up_d[:, kc, hf * 2048:(hf + 1) * 2048], r_wup2[hf],
                      writes=[r_wup2[hf]], acc=True)
        w_dn = S.sbuf("w_dn_bf", [128, 32, D], BF16, Bs)
        r_wdn4 = [Res("w_dn%d" % i) for i in range(4)]
        for fc in range(32):
            S.dma("pool", w_dn[:, fc, :], w_dn_d[:, fc, :], r_wdn4[fc // 8], writes=[r_wdn4[fc // 8]], acc=True)
        g_fpre, r_gfpre = const_tile("g_fpre", [128, D], F32, g_fpre_d[:].partition_broadcast(128), stack=Bs)
        g_fpost, r_gfpost = const_tile("g_fpost", [128, D], F32, g_fpost_d[:].partition_broadcast(128), stack=Bs)
        G = ffn_group
        NTOK = G * 128
        x1g_ring = Ring(S, "x1g", 2, [128, G, D], F32, Bs)
        h2T_ring = Ring(S, "h2T", 2, [128, KC, NTOK], BF16, Bs)
        u2T = S.sbuf("u2T", [128, 32, NTOK], BF16, Bs)
        r_u2T = [Res("u2T%d" % i) for i in range(32)]
        rl_ring = Ring(S, "rl", 3, [128, NTOK], F32, Bs)
        yo_ring = Ring(S, "yo", 2, [128, D], F32, Bs)
        for g in range(nt // G):
            x1g, r_x1g = x1g_ring.next()
            h2T, r_h2T = h2T_ring.next()
            for ti in range(G):
                t = g * G + ti
                S.dma("sp", x1g[:, ti, :], y_d[t * 128:(t + 1) * 128, :], r_x1g, reads=[state["ry"][t]], writes=[r_x1g],
                      acc=(ti > 0))
            for ti in range(G):
                state["ry"][g * G + ti].r[id(r_x1g.dsem)] = (r_x1g.dsem, r_x1g.dcnt, "dma")
            for ti in range(G):
                norm_transpose(x1g[:, ti, :], r_x1g, g_fpre, r_gfpre, h2T[:, :, ti * 128:(ti + 1) * 128], r_h2T)
            for fc in range(32):
                bk, rb = nb()
                for kc in range(KC):
                    S.op("pe", lambda e, kc=kc, fc=fc, bk=bk: e.matmul(bk[:, 0:NTOK], lhsT=w_up[:, kc, fc * 128:(fc + 1) * 128],
                                                                       rhs=h2T[:, kc, :], start=(kc == 0), stop=(kc == KC - 1)),
                         reads=[r_h2T, r_wup2[fc // 16]], writes=[rb], acc=(kc > 0))
                rl, r_rl = rl_ring.next()
                S.op("act", lambda e, bk=bk, rl=rl: e.activation(out=rl[:, :], in_=bk[:, 0:NTOK], func=AF.Relu),
                     reads=[rb], writes=[r_rl])
                S.op("dve", lambda e, fc=fc, rl=rl: e.tensor_tensor(out=u2T[:, fc, :], in0=rl[:, :], in1=rl[:, :], op=ALU.mult),
                     reads=[r_rl], writes=[r_u2T[fc]])
            for ti in range(G):
                t = g * G + ti
                bk0, rb0 = nb()
                bk1, rb1 = nb()
                for n, (bk, rb) in enumerate(((bk0, rb0), (bk1, rb1))):
                    for fc in range(32):
                        S.op("pe", lambda e, fc=fc, n=n, bk=bk, ti=ti: e.matmul(
                            bk[:, :], lhsT=u2T[:, fc, ti * 128:(ti + 1) * 128], rhs=w_dn[:, fc, n * 512:(n + 1) * 512],
                            start=(fc == 0), stop=(fc == 31)), reads=[r_u2T[fc], r_wdn4[fc // 8]], writes=[rb], acc=(fc > 0))
                yo, r_yo = yo_ring.next()
                epilogue(bk0, rb0, bk1, rb1, g_fpost, r_gfpost, x1g[:, ti, :], r_x1g, yo[:, :], r_yo)
                S.dma("pool", y_d[t * 128:(t + 1) * 128, :], yo[:, :], r_yo, reads=[r_yo], writes=[state["ry"][t]], final=True)
        if do_sample and state.get("r_ys") is not None:
            x1s2, r_x1s2 = S.sbuf("x1s2", [NS, D], F32, Bs), Res("x1s2")
            S.dma("sp", x1s2[:, :], ys_d[:, :], r_x1s2, reads=[state["r_ys"]], writes=[r_x1s2])
            h2Ts, r_h2Ts = S.sbuf("h2Ts", [128, KC, NS], BF16, Bs), Res("h2Ts")
            norm_transpose(x1s2[:, :], r_x1s2, g_fpre, r_gfpre, h2Ts[:, :, :], r_h2Ts, n=NS)
            for fc in range(32):
                bk, rb = nb()
                for kc in range(KC):
                    S.op("pe", lambda e, kc=kc, fc=fc, bk=bk: e.matmul(bk[:, 0:NS], lhsT=w_up[:, kc, fc * 128:(fc + 1) * 128],
                                                                       rhs=h2Ts[:, kc, :], start=(kc == 0), stop=(kc == KC - 1)),
                         reads=[r_h2Ts, r_wup2[fc // 16]], writes=[rb], acc=(kc > 0))
                rl, r_rl = rl_ring.next()
                S.op("act", lambda e, bk=bk, rl=rl: e.activation(out=rl[:, 0:NS], in_=bk[:, 0:NS], func=AF.Relu),
                     reads=[rb], writes=[r_rl])
                S.op("dve", lambda e, fc=fc, rl=rl: e.tensor_tensor(out=u2T[:, fc, 0:NS], in0=rl[:, 0:NS], in1=rl[:, 0:NS],
                                                                    op=ALU.mult), reads=[r_rl], writes=[r_u2T[fc]])
            bk0, rb0 = nb()
            bk1, rb1 = nb()
            for n, (bk, rb) in enumerate(((bk0, rb0), (bk1, rb1))):
                for fc in range(32):
                    S.op("pe", lambda e, fc=fc, n=n, bk=bk: e.matmul(
                        bk[0:NS, :], lhsT=u2T[:, fc, 0:NS], rhs=w_dn[:, fc, n * 512:(n + 1) * 512],
                        start=(fc == 0), stop=(fc == 31)), reads=[r_u2T[fc], r_wdn4[fc // 8]], writes=[rb], acc=(fc > 0))
            yso, r_yso = S.sbuf("yso", [NS, D], F32, Bs), Res("yso")
            epilogue(bk0, rb0, bk1, rb1, g_fpost, r_gfpost, x1s2[:, :], r_x1s2, yso[:, :], r_yso, n=NS)
            S.dma("sp", ys_d[:, :], yso[:, :], r_yso, reads=[r_yso], writes=[state["r_ys"]], final=True)
        S.barrier()
        Bs.close()
    S.finish()
    S.close()
    info = dict(nops=S.nops, nwaits=S.nwaits, ndma=S.ndma)
    return nc, dbg_outs, info


def _core_inputs(inp, consts, b, j, nt=NT, npre=NPRE):
    seg = nt * 128
    x = inp["x_prompt"][b]
    lo = j * seg
    xin = np.zeros(((npre + nt) * 128, D), np.float32)
    npz = min(lo, npre * 128)
    if npz > 0:
        xin[npre * 128 - npz:npre * 128] = x[lo - npz:lo]
    xin[npre * 128:] = x[lo:lo + seg]
    m = {"xin": xin}
    m.update(consts)
    if j != 0:
        m["bias0"] = consts["biasN"]
    core = b * 4 + j
    sel = np.zeros((1, 4), np.float32)
    sel[0, j] = 1.0
    m["sel"] = sel
    s0, s1 = core * 16, (core + 1) * 16
    m["xs"] = np.ascontiguousarray(inp["x_sample"][s0:s1, 0, :])
    m["cst"] = np.ascontiguousarray(inp["state_conv"][0, s0:s1])
    m["ck"] = np.ascontiguousarray(inp["cache_win_k"][0, s0:s1]).reshape(16, 128, 128)
    m["cv"] = np.ascontiguousarray(inp["cache_win_v"][0, s0:s1]).reshape(16, 128, 128)
    m["sst"] = np.ascontiguousarray(inp["state_gdn"][0, s0:s1])
    m["convwt"] = np.ascontiguousarray(inp["conv_w"][0]).reshape(1, 4 * 1536)
    m["sinksh"] = np.ascontiguousarray(np.tile(inp["attn_sinks"][0], 16).reshape(128, 1))
    return m


def _shared_inputs(inp):
    w_in = np.ascontiguousarray(inp["w_in"][0].reshape(KC, 128, INW).transpose(1, 0, 2))
    w_out = np.ascontiguousarray(inp["w_out"][0].reshape(KC, 128, D).transpose(1, 0, 2))
    w_up = np.ascontiguousarray(inp["w_up"][0].reshape(KC, 128, 4096).transpose(1, 0, 2))
    w_dn = np.ascontiguousarray(inp["w_down"][0].reshape(32, 128, D).transpose(1, 0, 2))
    convw = np.ascontiguousarray(inp["conv_w"][0].T.reshape(12, 128, 4).transpose(1, 0, 2))
    sh = {"w_in": w_in, "w_out": w_out, "w_up": w_up, "w_down": w_dn, "convw": convw,
          "g_pre": inp["norm_mix_pre"], "g_post": inp["norm_mix_post"], "g_fpre": inp["norm_ffn_pre"],
          "g_fpost": inp["norm_ffn_post"], "g_gdn": inp["gdn_norm"], "sinks": inp["attn_sinks"],
          "alog": inp["gdn_a_log"], "dtb": inp["gdn_dt_bias"]}
    return {k: np.ascontiguousarray(v, dtype=np.float32) for k, v in sh.items()}


def kernel(**inputs):
    inp = {k: np.asarray(v) for k, v in inputs.items()}
    consts = _consts()
    shared = _shared_inputs(inp)
    nc, _, _ = build_program()
    in_maps = []
    for core in range(8):
        b, j = core // 4, core % 4
        m = _core_inputs(inp, consts, b, j)
        m.update(shared)
        in_maps.append(m)
    res = run_bass_kernel_spmd(nc, in_maps, core_ids=list(range(8)))
    R = res.results
    y_prompt = np.zeros((2, 8192, D), np.float32)
    for core in range(8):
        b, j = core // 4, core % 4
        y_prompt[b, j * SEG:(j + 1) * SEG] = R[core]["y"]
    p_conv = np.stack([R[3]["p_conv"], R[7]["p_conv"]])[None]
    p_k = np.stack([R[3]["p_k"], R[7]["p_k"]]).reshape(1, 2, 128, 2, 64)
    p_v = np.stack([R[3]["p_v"], R[7]["p_v"]]).reshape(1, 2, 128, 2, 64)
    p_S = np.stack([R[3]["p_S"], R[7]["p_S"]])[None]
    y_sample = np.concatenate([R[c]["ys"] for c in range(8)], 0).reshape(128, 1, D)
    s_conv = np.concatenate([R[c]["s_conv"] for c in range(8)], 0)[None]
    s_k = np.concatenate([R[c]["s_k"] for c in range(8)], 0).reshape(1, 128, 128, 2, 64)
    s_v = np.concatenate([R[c]["s_v"] for c in range(8)], 0).reshape(1, 128, 128, 2, 64)
    s_S = np.concatenate([R[c]["s_S"] for c in range(8)], 0)[None]
    return (y_prompt, y_sample, p_conv, p_k, p_v, p_S, s_conv, s_k, s_v, s_S)
```
